# Optimizing a Trainium2 kernel written in Bass

```python
import math
import jax, jax.numpy as jnp
from jax import lax
import numpy as np

D_MODEL = 1024
BATCH = 8
SEQ = 4096
DEPTH = 4

SSD_HEADS = 16
SSD_HEAD_DIM = D_MODEL // SSD_HEADS
SSD_INNER = SSD_HEADS * SSD_HEAD_DIM
SSD_GROUPS = 2
SSD_STATE = 128
SSD_CONV = 4
SSD_CHUNK = 128
SSD_XBC = SSD_INNER + 2 * SSD_GROUPS * SSD_STATE
CONF_WIDTH = D_MODEL
CONF_CONV_WIDTH = 31
NSA_HEADS = 16
NSA_KV_GROUPS = 4
NSA_HPG = NSA_HEADS // NSA_KV_GROUPS
NSA_HEAD_DIM = 64
NSA_WIDTH = NSA_HEADS * NSA_HEAD_DIM
KV_WIDTH = NSA_KV_GROUPS * NSA_HEAD_DIM
CMP_BLOCK = 32
CMP_STRIDE = 16
CMP_HIDDEN = 4 * NSA_HEAD_DIM
SEL_BLOCK = 64
SEL_TOPK = 16
WINDOW = 512
NSA_QBLOCK = 32
FORCE_BONUS = 1e6
NORM_EPS = 1e-6

N_EVEN = (DEPTH + 1) // 2
N_ODD = DEPTH // 2
E_IN = SSD_INNER + SSD_XBC + SSD_HEADS + 3 * CONF_WIDTH
O_IN = 2 * NSA_WIDTH + 6 * KV_WIDTH + 3 * NSA_HEADS

kernel_name = "hybrid_ssd_conformer_nsa_trunk"


def rmsnorm(x, g):
    xf = x.astype(jnp.float32)
    y = xf * lax.rsqrt(jnp.mean(xf * xf, axis=-1, keepdims=True) + NORM_EPS)
    return (y * g.astype(jnp.float32)).astype(x.dtype)


def causal_dwconv(x, w, b):
    k, c = w.shape
    y = lax.conv_general_dilated(x, w[:, None, :].astype(x.dtype), window_strides=(1,),
                                 padding=[(k - 1, 0)], dimension_numbers=('NWC', 'WIO', 'NWC'),
                                 feature_group_count=c)
    return y + b.astype(x.dtype)


def masked_softmax(s, mask):
    s = jnp.where(mask, s.astype(jnp.float32), -jnp.inf)
    m = jnp.max(s, axis=-1, keepdims=True)
    m = jnp.where(jnp.isfinite(m), m, 0.0)
    e = jnp.exp(s - m)
    return e / jnp.maximum(jnp.sum(e, axis=-1, keepdims=True), 1e-30)


def ssd_scan(x, dt, bm, cm, a_log, d_skip):
    f32 = jnp.float32
    bsz, s = x.shape[:2]
    nc = s // SSD_CHUNK
    r = SSD_HEADS // SSD_GROUPS
    xh = x.astype(f32).reshape(bsz, s, SSD_GROUPS, r, SSD_HEAD_DIM)
    xs = jnp.moveaxis(xh.reshape(bsz, nc, SSD_CHUNK, SSD_GROUPS, r, SSD_HEAD_DIM), 1, 0)
    dts = jnp.moveaxis(dt.reshape(bsz, nc, SSD_CHUNK, SSD_GROUPS, r), 1, 0)
    bs = jnp.moveaxis(bm.astype(f32).reshape(bsz, nc, SSD_CHUNK, SSD_GROUPS, SSD_STATE), 1, 0)
    cs = jnp.moveaxis(cm.astype(f32).reshape(bsz, nc, SSD_CHUNK, SSD_GROUPS, SSD_STATE), 1, 0)
    a = -jnp.exp(a_log.astype(f32)).reshape(SSD_GROUPS, r)
    causal = np.tril(np.ones((SSD_CHUNK, SSD_CHUNK), dtype=bool))[None, :, :, None, None]

    def step(state, inp):
        xc, dtc, bc, cc = inp
        cum = jnp.cumsum(dtc * a, axis=1)
        seg = cum[:, :, None] - cum[:, None, :]
        decay = jnp.exp(jnp.where(causal, seg, -jnp.inf))
        cb = jnp.einsum('btgn,bsgn->btsg', cc, bc)
        wts = cb[..., None] * decay * dtc[:, None]
        y_diag = jnp.einsum('btsgr,bsgrp->btgrp', wts, xc)
        y_off = jnp.einsum('btgn,bgrpn->btgrp', cc, state) * jnp.exp(cum)[..., None]
        to_end = jnp.exp(cum[:, -1:] - cum) * dtc
        new_state = state * jnp.exp(cum[:, -1])[..., None, None] + \
            jnp.einsum('bsgn,bsgr,bsgrp->bgrpn', bc, to_end, xc)
        return new_state, y_diag + y_off

    state0 = jnp.zeros((bsz, SSD_GROUPS, r, SSD_HEAD_DIM, SSD_STATE), f32)
    _, ys = lax.scan(step, state0, (xs, dts, bs, cs))
    ys = jnp.moveaxis(ys, 0, 1).reshape(bsz, s, SSD_GROUPS, r, SSD_HEAD_DIM)
    ys = ys + d_skip.astype(f32).reshape(SSD_GROUPS, r, 1) * xh
    return ys.reshape(bsz, s, SSD_INNER)


def even_layer(h, w_in, ssd_conv_w, ssd_conv_b, dt_bias, a_log, d_skip, ssd_norm,
               conf_conv_w, conf_conv_b, conf_ln_g, conf_ln_b, w_out):
    f32 = jnp.float32
    bsz, s, _ = h.shape
    proj = h @ w_in
    o1 = SSD_INNER
    o2 = o1 + SSD_XBC
    o3 = o2 + SSD_HEADS
    o4 = o3 + 2 * CONF_WIDTH
    z, xbc, dt_raw, glu_in, zc = jnp.split(proj, [o1, o2, o3, o4], axis=-1)
    xbc = jax.nn.silu(causal_dwconv(xbc, ssd_conv_w, ssd_conv_b))
    gn = SSD_GROUPS * SSD_STATE
    xs, bm, cm = jnp.split(xbc, [SSD_INNER, SSD_INNER + gn], axis=-1)
    dt = jax.nn.softplus(dt_raw.astype(f32) + dt_bias.astype(f32))
    y = ssd_scan(xs, dt, bm, cm, a_log, d_skip)
    yg = (y * jax.nn.silu(z.astype(f32))).reshape(bsz, s, SSD_GROUPS, SSD_INNER // SSD_GROUPS)
    yg = yg * lax.rsqrt(jnp.mean(yg * yg, axis=-1, keepdims=True) + NORM_EPS)
    y_a = (yg.reshape(bsz, s, SSD_INNER) * ssd_norm.astype(f32)).astype(h.dtype)
    ua, ub = jnp.split(glu_in, 2, axis=-1)
    u = causal_dwconv(ua * jax.nn.sigmoid(ub), conf_conv_w, conf_conv_b).astype(f32)
    mu = jnp.mean(u, axis=-1, keepdims=True)
    var = jnp.mean(jnp.square(u - mu), axis=-1, keepdims=True)
    un = (u - mu) * lax.rsqrt(var + NORM_EPS) * conf_ln_g.astype(f32) + conf_ln_b.astype(f32)
    y_b = (jax.nn.silu(un) * jax.nn.silu(zc.astype(f32))).astype(h.dtype)
    return jnp.concatenate([y_a, y_b], axis=-1) @ w_out


def compress_kv(t, pe, w1, w2):
    bsz, s = t.shape[:2]
    n_cmp = (s - CMP_BLOCK) // CMP_STRIDE + 1
    idx = np.arange(n_cmp)[:, None] * CMP_STRIDE + np.arange(CMP_BLOCK)[None, :]
    blk = t[:, idx] + pe[None, None, :, None, :].astype(t.dtype)
    blk = blk.transpose(0, 1, 3, 2, 4).reshape(bsz, n_cmp, NSA_KV_GROUPS, CMP_BLOCK * NSA_HEAD_DIM)
    return jax.nn.silu(blk @ w1) @ w2


def odd_layer(h, w_in, gate_bias, pe_k, w1_k, w2_k, pe_v, w1_v, w2_v, w_out):
    f32 = jnp.float32
    bsz, s, _ = h.shape
    G, R, Dh, Q = NSA_KV_GROUPS, NSA_HPG, NSA_HEAD_DIM, NSA_QBLOCK
    scale = NSA_HEAD_DIM ** -0.5
    proj = h @ w_in
    sizes = [NSA_WIDTH] + [KV_WIDTH] * 6 + [3 * NSA_HEADS]
    q, kc, vc, ks, vs, kw, vw, gl, z = jnp.split(proj, np.cumsum(sizes).tolist(), axis=-1)
    q = q.reshape(bsz, s, G, R, Dh)
    kc, vc, ks, vs, kw, vw = [t.reshape(bsz, s, G, Dh) for t in (kc, vc, ks, vs, kw, vw)]
    gates = jax.nn.sigmoid(gl.astype(f32) + gate_bias.astype(f32)).reshape(bsz, s, G, R, 3)
    k_cmp = compress_kv(kc, pe_k, w1_k, w2_k)
    v_cmp = compress_kv(vc, pe_v, w1_v, w2_v).astype(f32)
    n_cmp = k_cmp.shape[1]
    cmp_start = np.arange(n_cmp) * CMP_STRIDE
    cmp_end = cmp_start + CMP_BLOCK - 1
    n_sel = s // SEL_BLOCK
    topk = min(SEL_TOPK, n_sel)
    sel_start = np.arange(n_sel) * SEL_BLOCK
    overlap = jnp.asarray(((cmp_start[:, None] < sel_start[None, :] + SEL_BLOCK) &
                           (cmp_start[:, None] + CMP_BLOCK > sel_start[None, :])).astype(np.float32))
    ksb = ks.reshape(bsz, n_sel, SEL_BLOCK, G, Dh).transpose(0, 3, 1, 2, 4)
    vsb = vs.reshape(bsz, n_sel, SEL_BLOCK, G, Dh).transpose(0, 3, 1, 2, 4)
    b_idx = jnp.arange(bsz)[:, None, None, None]
    g_idx = jnp.arange(G)[None, :, None, None]
    pad = ((0, 0), (WINDOW, 0), (0, 0), (0, 0))
    kw_pad = jnp.pad(kw, pad)
    vw_pad = jnp.pad(vw, pad)

    def block_fn(blk):
        s0 = blk * Q
        qb = lax.dynamic_slice_in_dim(q, s0, Q, axis=1)
        gb = lax.dynamic_slice_in_dim(gates, s0, Q, axis=1)
        tpos = s0 + jnp.arange(Q)
        sc = jnp.einsum('bqgrd,bcgd->bgrqc', qb, k_cmp) * scale
        p_cmp = masked_softmax(sc, cmp_end[None, :] <= tpos[:, None])
        o_cmp = jnp.einsum('bgrqc,bcgd->bqgrd', p_cmp, v_cmp)
        imp = jnp.einsum('bgrqc,cj->bgqj', p_cmp, overlap)
        jb = jnp.arange(n_sel)[None, :]
        cur = (tpos // SEL_BLOCK)[:, None]
        valid = jb * SEL_BLOCK <= tpos[:, None]
        forced = (jb == 0) | (jb == cur) | (jb == cur - 1)
        imp = jnp.where(valid, imp + jnp.where(forced, FORCE_BONUS, 0.0), -jnp.inf)
        _, sel = lax.top_k(imp, topk)
        ksg = ksb[b_idx, g_idx, sel]
        vsg = vsb[b_idx, g_idx, sel].astype(f32)
        ss = jnp.einsum('bqgrd,bgqkld->bgrqkl', qb, ksg) * scale
        kpos = sel[..., None] * SEL_BLOCK + jnp.arange(SEL_BLOCK)
        smask = (kpos <= tpos[None, None, :, None, None])[:, :, None]
        p_sel = masked_softmax(ss.reshape(bsz, G, R, Q, topk * SEL_BLOCK),
                               smask.reshape(bsz, G, 1, Q, topk * SEL_BLOCK))
        o_sel = jnp.einsum('bgrqkl,bgqkld->bqgrd',
                           p_sel.reshape(bsz, G, R, Q, topk, SEL_BLOCK), vsg)
        kwb = lax.dynamic_slice_in_dim(kw_pad, s0, WINDOW + Q, axis=1)
        vwb = lax.dynamic_slice_in_dim(vw_pad, s0, WINDOW + Q, axis=1).astype(f32)
        kp = s0 - WINDOW + jnp.arange(WINDOW + Q)
        rel = tpos[:, None] - kp[None, :]
        wmask = (rel >= 0) & (rel < WINDOW) & (kp[None, :] >= 0)
        sw = jnp.einsum('bqgrd,bkgd->bgrqk', qb, kwb) * scale
        p_win = masked_softmax(sw, wmask)
        o_win = jnp.einsum('bgrqk,bkgd->bqgrd', p_win, vwb)
        return gb[..., 0:1] * o_cmp + gb[..., 1:2] * o_sel + gb[..., 2:3] * o_win

    outs = lax.map(block_fn, jnp.arange(s // Q))
    o = jnp.moveaxis(outs, 0, 1).reshape(bsz, s, NSA_WIDTH)
    y = (o * jax.nn.silu(z.astype(f32))).astype(h.dtype)
    return y @ w_out


def setup_inputs(seed: int = 0) -> dict:
    key = jax.random.key(seed)
    ks = jax.random.split(key, 25)
    f32 = jnp.float32

    def nrm(k, shape, sc):
        return jax.random.normal(k, shape, f32) * sc

    x = nrm(ks[0], (BATCH, SEQ, D_MODEL), 1.0)
    e_norm = 1.0 + nrm(ks[1], (N_EVEN, D_MODEL), 0.05)
    e_w_in = nrm(ks[2], (N_EVEN, D_MODEL, E_IN), D_MODEL ** -0.5)
    e_ssd_conv_w = nrm(ks[3], (N_EVEN, SSD_CONV, SSD_XBC), SSD_CONV ** -0.5)
    e_ssd_conv_b = nrm(ks[4], (N_EVEN, SSD_XBC), 0.02)
    dt0 = jnp.exp(jax.random.uniform(ks[5], (N_EVEN, SSD_HEADS), f32, math.log(1e-3), math.log(1e-1)))
    e_dt_bias = dt0 + jnp.log(-jnp.expm1(-dt0))
    e_a_log = jnp.log(jax.random.uniform(ks[6], (N_EVEN, SSD_HEADS), f32, 1.0, 16.0))
    e_d_skip = 1.0 + nrm(ks[7], (N_EVEN, SSD_HEADS), 0.1)
    e_ssd_norm = 1.0 + nrm(ks[8], (N_EVEN, SSD_INNER), 0.05)
    e_conf_conv_w = nrm(ks[9], (N_EVEN, CONF_CONV_WIDTH, CONF_WIDTH), CONF_CONV_WIDTH ** -0.5)
    e_conf_conv_b = nrm(ks[10], (N_EVEN, CONF_WIDTH), 0.02)
    e_conf_ln_g = 1.0 + nrm(ks[11], (N_EVEN, CONF_WIDTH), 0.05)
    e_conf_ln_b = nrm(ks[12], (N_EVEN, CONF_WIDTH), 0.02)
    e_w_out = nrm(ks[13], (N_EVEN, SSD_INNER + CONF_WIDTH, D_MODEL), (SSD_INNER + CONF_WIDTH) ** -0.5)
    o_norm = 1.0 + nrm(ks[14], (N_ODD, D_MODEL), 0.05)
    o_w_in = nrm(ks[15], (N_ODD, D_MODEL, O_IN), D_MODEL ** -0.5)
    o_gate_bias = nrm(ks[16], (N_ODD, 3 * NSA_HEADS), 0.1)
    o_cmp_pe_k = nrm(ks[17], (N_ODD, CMP_BLOCK, NSA_HEAD_DIM), 0.1)
    o_cmp_w1_k = nrm(ks[18], (N_ODD, CMP_BLOCK * NSA_HEAD_DIM, CMP_HIDDEN), (CMP_BLOCK * NSA_HEAD_DIM) ** -0.5)
    o_cmp_w2_k = nrm(ks[19], (N_ODD, CMP_HIDDEN, NSA_HEAD_DIM), CMP_HIDDEN ** -0.5)
    o_cmp_pe_v = nrm(ks[20], (N_ODD, CMP_BLOCK, NSA_HEAD_DIM), 0.1)
    o_cmp_w1_v = nrm(ks[21], (N_ODD, CMP_BLOCK * NSA_HEAD_DIM, CMP_HIDDEN), (CMP_BLOCK * NSA_HEAD_DIM) ** -0.5)
    o_cmp_w2_v = nrm(ks[22], (N_ODD, CMP_HIDDEN, NSA_HEAD_DIM), CMP_HIDDEN ** -0.5)
    o_w_out = nrm(ks[23], (N_ODD, NSA_WIDTH, D_MODEL), NSA_WIDTH ** -0.5)
    final_norm = 1.0 + nrm(ks[24], (D_MODEL,), 0.05)
    return {"x": x, "e_norm": e_norm, "e_w_in": e_w_in, "e_ssd_conv_w": e_ssd_conv_w,
            "e_ssd_conv_b": e_ssd_conv_b, "e_dt_bias": e_dt_bias, "e_a_log": e_a_log,
            "e_d_skip": e_d_skip, "e_ssd_norm": e_ssd_norm, "e_conf_conv_w": e_conf_conv_w,
            "e_conf_conv_b": e_conf_conv_b, "e_conf_ln_g": e_conf_ln_g, "e_conf_ln_b": e_conf_ln_b,
            "e_w_out": e_w_out, "o_norm": o_norm, "o_w_in": o_w_in, "o_gate_bias": o_gate_bias,
            "o_cmp_pe_k": o_cmp_pe_k, "o_cmp_w1_k": o_cmp_w1_k, "o_cmp_w2_k": o_cmp_w2_k,
            "o_cmp_pe_v": o_cmp_pe_v, "o_cmp_w1_v": o_cmp_w1_v, "o_cmp_w2_v": o_cmp_w2_v,
            "o_w_out": o_w_out, "final_norm": final_norm}


def reference(x, e_norm, e_w_in, e_ssd_conv_w, e_ssd_conv_b, e_dt_bias, e_a_log, e_d_skip,
              e_ssd_norm, e_conf_conv_w, e_conf_conv_b, e_conf_ln_g, e_conf_ln_b, e_w_out,
              o_norm, o_w_in, o_gate_bias, o_cmp_pe_k, o_cmp_w1_k, o_cmp_w2_k, o_cmp_pe_v,
              o_cmp_w1_v, o_cmp_w2_v, o_w_out, final_norm):
    h = x
    for layer in range(DEPTH):
        i = layer // 2
        if layer % 2 == 0:
            h = h + even_layer(rmsnorm(h, e_norm[i]), e_w_in[i], e_ssd_conv_w[i], e_ssd_conv_b[i],
                               e_dt_bias[i], e_a_log[i], e_d_skip[i], e_ssd_norm[i],
                               e_conf_conv_w[i], e_conf_conv_b[i], e_conf_ln_g[i], e_conf_ln_b[i],
                               e_w_out[i])
        else:
            h = h + odd_layer(rmsnorm(h, o_norm[i]), o_w_in[i], o_gate_bias[i], o_cmp_pe_k[i],
                              o_cmp_w1_k[i], o_cmp_w2_k[i], o_cmp_pe_v[i], o_cmp_w1_v[i],
                              o_cmp_w2_v[i], o_w_out[i])
    return rmsnorm(h, final_norm)
```

```python
import numpy as np
from contextlib import ExitStack
import concourse.bass as bass
import concourse.mybir as mybir
from concourse.bass_utils import run_bass_kernel_spmd

F32 = mybir.dt.float32
BF16 = mybir.dt.bfloat16
AF = mybir.ActivationFunctionType
ALU = mybir.AluOpType
AX = mybir.AxisListType

D = 1024
E_IN = 5648
O_IN = 3632
EPS = 1e-6
NEG = -30000.0
ODD_STOP = None
ODD_DBG = 3


class Buf:
    __slots__ = ("name", "w", "r", "psum")

    def __init__(self, name, psum=False):
        self.name = name
        self.psum = psum
        self.w = {}
        self.r = {}


class KB:
    def __init__(self, nc, es, n_dsem=90):
        self.nc = nc
        self.es = es
        self.engs = {"pe": nc.tensor, "act": nc.scalar, "dve": nc.vector, "pool": nc.gpsimd, "sp": nc.sync}
        self.esem = {e: es.enter_context(nc.semaphore("s_" + e)) for e in self.engs}
        self.ecnt = {e: 0 for e in self.engs}
        self.dsem = [es.enter_context(nc.semaphore("d%d" % i)) for i in range(n_dsem)]
        self.dcnt = [0] * n_dsem
        self.dnext = 0
        self.seen = {e: {} for e in self.engs}
        self.nbuf = 0

    def buf(self, name=None):
        self.nbuf += 1
        return Buf(name or ("b%d" % self.nbuf))

    def sb(self, name, shape, dt):
        self.nbuf += 1
        name = "%s_u%d" % (name, self.nbuf)
        t = self.es.enter_context(self.nc.sbuf_tensor(name, list(shape), dt))
        return t, Buf(name)

    def _wait(self, e, key, val):
        if self.seen[e].get(key, 0) >= val:
            return
        sem = self.esem[key] if isinstance(key, str) else self.dsem[key]
        self.engs[e].wait_ge(sem, val)
        self.seen[e][key] = val

    def _deps(self, e, reads, writes, waw):
        for b in reads:
            for k, v in b.w.items():
                self._wait(e, k, v)
            if b.psum:
                for k, v in b.r.items():
                    if k != e:
                        self._wait(e, k, v)
        for b in writes:
            for k, v in b.w.items():
                if k == e or waw:
                    continue
                self._wait(e, k, v)
            for k, v in b.r.items():
                if k == e:
                    continue
                self._wait(e, k, v)

    def op(self, e, reads, writes, fn, waw=False):
        self._deps(e, reads, writes, waw)
        ins = fn(self.engs[e])
        self.ecnt[e] += 1
        c = self.ecnt[e]
        ins.then_inc(self.esem[e], 1)
        for b in reads:
            b.r[e] = c
        for b in writes:
            if waw:
                b.w[e] = c
            else:
                b.w = {e: c}
                b.r = {}
        return ins

    def dma(self, e, reads, writes, out, in_, waw=False, **kw):
        if e == "pool":
            self.pool_q = getattr(self, "pool_q", [])
            if len(self.pool_q) >= 6:
                k_, v_ = self.pool_q.pop(0)
                self._wait(e, k_, v_)
        i = self.dnext
        self.dnext = (i + 1) % len(self.dsem)
        if self.dcnt[i] > 0:
            self._wait(e, i, self.dcnt[i])
        self._deps(e, reads, writes, waw)
        self.dcnt[i] += 16
        v = self.dcnt[i]
        self.engs[e].dma_start(out=out, in_=in_, **kw).then_inc(self.dsem[i], 16)
        if e == "pool":
            self.pool_q.append((i, v))
        for b in reads:
            b.r[i] = v
        for b in writes:
            if waw:
                b.w[i] = v
            else:
                b.w = {i: v}
                b.r = {}

    def barrier(self):
        for e in self.engs:
            for e2 in self.engs:
                if e2 != e and self.ecnt[e2] > 0:
                    self._wait(e, e2, self.ecnt[e2])
            for i, v in enumerate(self.dcnt):
                if v > 0:
                    self._wait(e, i, v)

    def finish(self, bufs):
        for b in bufs:
            for k, v in b.w.items():
                self._wait("sp", k, v)


def build_program(S, layers, debug_h=False):
    nc = bass.Bass("TRN2", target_bir_lowering=False)
    NT = S // 128
    NB = S // 512

    def din(name, shape, dt=F32):
        return nc.dram_tensor(name, list(shape), dt, kind="ExternalInput").ap()

    x_in = din("x", [S, D])
    out_d = nc.dram_tensor("out", [S, D], F32, kind="ExternalOutput").ap()
    hD = nc.dram_tensor("h_scr", [S, D], F32, kind="Internal").ap()
    if debug_h:
        dbg_d = nc.dram_tensor("dbg", [128, 256], F32, kind="ExternalOutput").ap()
    cst = din("consts_f32", [128, 5 * 128])
    cstb = din("consts_bf16", [128, 2 * 128], BF16)
    final_norm = din("final_norm", [128, D])

    n_even = sum(1 for l in layers if l[0] == "e")
    n_odd = sum(1 for l in layers if l[0] == "o")
    ew = {}
    if n_even:
        ew = dict(
            norm=din("e_norm", [2, 128, D]), w_in=din("e_w_in", [2, D, E_IN]),
            conv_w=din("e_ssd_conv_w", [2, 128, 12, 4]), conv_b=din("e_ssd_conv_b", [2, 128, 12]),
            dt_bias=din("e_dt_bias", [2, 128, 16]), a_log=din("e_a_log", [2, 128, 16]),
            d_skip=din("e_d_skip", [2, 128, 16]), ssd_norm=din("e_ssd_norm", [2, 128, D]),
            cconv_w=din("e_conf_conv_w", [2, 128, 8, 31]), cconv_b=din("e_conf_conv_b", [2, 128, 8]),
            ln_g=din("e_conf_ln_g", [2, 128, 8]), ln_b=din("e_conf_ln_b", [2, 128, 8]),
            w_out=din("e_w_out", [2, 2048, D]),
        )
        ew["w_in_bf"] = nc.dram_tensor("e_w_in_bf", [2, D, E_IN], BF16, kind="Internal").ap()
        ew["w_out_bf"] = nc.dram_tensor("e_w_out_bf", [2, 2048, D], BF16, kind="Internal").ap()

    ow = {}
    if n_odd:
        ow = dict(
            norm=din("o_norm", [2, 128, D]), w_in=din("o_w_in", [2, D, O_IN]), gate_bias=din("o_gate_bias", [2, 128, 48]),
            peT_k=din("o_peT_k", [2, 64, 32]), w1_k=din("o_cmp_w1_k", [2, 2048, 256]), w2_k=din("o_cmp_w2_k", [2, 256, 64]),
            peT_v=din("o_peT_v", [2, 64, 32]), w1_v=din("o_cmp_w1_v", [2, 2048, 256]), w2_v=din("o_cmp_w2_v", [2, 256, 64]),
            w_out=din("o_w_out", [2, D, D]), amask=din("o_amask", [128, 128]), emat=din("o_emat", [64, S], BF16), cmsk=din("o_cmsk", [10, 128 + 288], BF16),
        )
        ow["w_in_bf"] = nc.dram_tensor("o_w_in_bf", [2, D, O_IN], BF16, kind="Internal").ap()
        ow["w_out_bf"] = nc.dram_tensor("o_w_out_bf", [2, D, D], BF16, kind="Internal").ap()
        ow["qT_D"] = nc.dram_tensor("o_qT_D", [1024, S], BF16, kind="Internal").ap()
        ow["kT_D"] = nc.dram_tensor("o_kT_D", [4, 256, S], BF16, kind="Internal").ap()
        ow["sz_D"] = nc.dram_tensor("o_sz_D", [S, D], BF16, kind="Internal").ap()

    with ExitStack() as es:
        kb = KB(nc, es)
        cf, cf_b = kb.sb("cf", [128, 5 * 128], F32)
        cb, cb_b = kb.sb("cb", [128, 2 * 128], BF16)
        kb.dma("sp", [], [cf_b], cf[:], cst[:, :])
        kb.dma("sp", [], [cb_b], cb[:], cstb[:, :])
        ident_f = cf[:, 0:128]
        L_f = cf[:, 128:256]
        U_f = cf[:, 256:384]
        tri_f = cf[:, 384:512]
        ones_f = cf[:, 512:640]
        ident_b = cb[:, 0:128]

        PS = []
        for i in range(8):
            t = es.enter_context(nc.psum_tensor("ps%d" % i, [128, 512], F32))
            PS.append((t, Buf("ps%d" % i, psum=True)))

        h_b = [kb.buf("hD%d" % i) for i in range(NT)]
        outs = []

        state = {"src": x_in}

        WB = [kb.sb("wb%d" % i, [128, 8, 512], BF16) for i in range(3)]
        wb_i = [0]

        wdram_b = {}

        def load_w(w_bf_ap, r0, c0, ncols, wkey=None):
            t, b = WB[wb_i[0] % len(WB)]
            wb_i[0] += 1
            src = w_bf_ap[r0:r0 + 1024, c0:c0 + ncols].rearrange("(k p) c -> p k c", p=128)
            kb.dma("sp", [wdram_b[wkey]] if wkey else [], [b], t[:, :, 0:ncols], src)
            return t, b

        def cast_weight(src_ap, dst_ap, rows, cols, nm):
            wdram_b[nm] = kb.buf(nm)
            stg = [kb.sb("%s_s%d" % (nm, i), [128, 2048], F32) for i in range(2)]
            stb = [kb.sb("%s_b%d" % (nm, i), [128, 2048], BF16) for i in range(2)]
            i = 0
            for r in range(0, rows, 128):
                for c in range(0, cols, 2048):
                    w = min(2048, cols - c)
                    s, sb_ = stg[i % 2]
                    d, db_ = stb[i % 2]
                    kb.dma("sp", [], [sb_], s[:, 0:w], src_ap[r:r + 128, c:c + w])
                    kb.op("pool", [sb_], [db_], lambda e: e.tensor_copy(out=d[:, 0:w], in_=s[:, 0:w]))
                    kb.dma("pool", [db_], [wdram_b[nm]], dst_ap[r:r + 128, c:c + w], d[:, 0:w], waw=True)
                    i += 1

        def even_layer(li):
            src = state["src"]
            norm_t, norm_b = kb.sb("e_norm_t", [128, D], F32)
            ssdn_t, ssdn_b = kb.sb("e_ssdn_t", [128, D], F32)
            small, small_b = kb.sb("e_small", [128, 64], F32)
            cw, cw_b = kb.sb("e_cw", [128, 12, 4], F32)
            cbias, cbias_b = kb.sb("e_cb", [128, 12], F32)
            ccw, ccw_b = kb.sb("e_ccw", [128, 8, 31], F32)
            cc4, cc4_b = kb.sb("e_cc4", [128, 3, 8], F32)
            kb.dma("sp", [], [norm_b], norm_t[:], ew["norm"][li])
            kb.dma("sp", [], [ssdn_b], ssdn_t[:], ew["ssd_norm"][li])
            kb.dma("sp", [], [small_b], small[:, 0:16], ew["dt_bias"][li])
            kb.dma("sp", [], [small_b], small[:, 16:32], ew["a_log"][li], waw=True)
            kb.dma("sp", [], [small_b], small[:, 32:48], ew["d_skip"][li], waw=True)
            kb.dma("sp", [], [cw_b], cw[:], ew["conv_w"][li])
            kb.dma("sp", [], [cbias_b], cbias[:], ew["conv_b"][li])
            kb.dma("sp", [], [ccw_b], ccw[:], ew["cconv_w"][li])
            kb.dma("sp", [], [cc4_b], cc4[:, 0, :], ew["cconv_b"][li])
            kb.dma("sp", [], [cc4_b], cc4[:, 1, :], ew["ln_g"][li], waw=True)
            kb.dma("sp", [], [cc4_b], cc4[:, 2, :], ew["ln_b"][li], waw=True)
            kb.op("act", [small_b], [small_b], lambda e: e.activation(out=small[:, 16:32], in_=small[:, 16:32], func=AF.Exp))
            kb.op("dve", [small_b], [small_b], lambda e: e.tensor_scalar(out=small[:, 16:32], in0=small[:, 16:32], scalar1=-1.0, scalar2=None, op0=ALU.mult))
            dtb = small[:, 0:16]
            a_t = small[:, 16:32]
            dsk = small[:, 32:48]
            w_in_bf = ew["w_in_bf"][li]
            w_out_bf = ew["w_out_bf"][li]

            wdt, wdt_b = kb.sb("e_wdt", [128, 8, 16], BF16)
            kb.dma("sp", [wdram_b["cwi%d" % li]], [wdt_b], wdt[:], w_in_bf[0:1024, 2560:2576].rearrange("(k p) c -> p k c", p=128))
            hblk, hblk_b = kb.sb("e_hblk", [128, 4, D], F32)
            hn, hn_b = kb.sb("e_hn", [128, D], BF16)
            st4, st4_b = kb.sb("e_st4", [128, 8], F32)
            hnT, hnT_b = kb.sb("e_hnT", [128, 8, 512], BF16)
            sz, sz_b = kb.sb("e_sz", [128, 4, D], BF16)
            xh, xh_b = kb.sb("e_xh", [128, 12, 516], BF16)
            xh_bs = [kb.buf("xh%d" % j) for j in range(12)]
            cacc, cacc_b = kb.sb("e_cacc", [128, 512], F32)
            caccp, caccp_b = kb.sb("e_caccp", [128, 512], F32)
            uats = [kb.sb("e_uat%d" % i, [128, 512], F32) for i in range(2)]
            xa, xa_b = kb.sb("e_xa", [128, 12, 512], BF16)
            xa_bs = [kb.buf("xa%d" % j) for j in range(12)]
            dt_t, dt_b = kb.sb("e_dt", [128, 4, 16], F32)
            dta_t, dta_b = kb.sb("e_dta", [128, 4, 16], F32)
            sp1, sp1_b = kb.sb("e_sp1", [128, 64], F32)
            sp2, sp2_b = kb.sb("e_sp2", [128, 64], F32)
            xt, xt_b = kb.sb("e_xt", [128, D], BF16)
            Bt, Bt_b = kb.sb("e_Bt", [128, 2, 128], BF16)
            cums, cums_b = kb.sb("e_cums", [128, 64], F32)
            CBm, CBm_b = kb.sb("e_CBm", [128, 2, 128], F32)
            lh = [kb.sb("e_lh%d" % i, [128, 128], F32) for i in range(8)]
            dec = [kb.sb("e_dec%d" % i, [128, 512], F32) for i in range(2)]
            wT = [kb.sb("e_wT%d" % i, [128, 128], BF16) for i in range(4)]
            y1, y1_b = kb.sb("e_y1", [128, D], F32)
            y2, y2_b = kb.sb("e_y2", [128, D], F32)
            junk, junk_b = y2, y2_b
            xw, xw_b = kb.sb("e_xw", [128, D], BF16)
            st32, st32_b = kb.sb("e_st32", [128, 2, 512], F32)
            stbf, stbf_b = kb.sb("e_stbf", [128, 2, 512], BF16)
            ya, ya_b = kb.sb("e_ya", [128, D], BF16)
            yaT, yaT_b = kb.sb("e_yaT", [128, 8, 512], BF16)
            ybT, ybT_b = kb.sb("e_ybT", [128, 8, 512], BF16)
            uh, uh_b = kb.sb("e_uh", [128, 8, 542], BF16)
            dgs = [kb.sb("e_dg%d" % i, [128, 31, 128], BF16) for i in range(2)]
            uh_bs = [kb.buf("uh%d" % j) for j in range(8)]
            uc, uc_b = kb.sb("e_uc", [128, 8, 512], F32)
            uc_bs = [kb.buf("uc%d" % j) for j in range(8)]
            sg, sg_b = kb.sb("e_sg", [128, 512], F32)
            sq, sq_b = kb.sb("e_sq", [128, 512], F32)
            lnm, lnm_b = kb.sb("e_lnm", [128, 512], F32)
            lnr, lnr_b = kb.sb("e_lnr", [128, 512], F32)
            un, un_b = kb.sb("e_un", [128, 512], F32)
            s1, s1_b = kb.sb("e_s1", [128, 512], F32)
            szc, szc_b = kb.sb("e_szc", [128, 512], F32)

            kb.op("pool", [], [st32_b], lambda e: e.memset(st32[:], 0.0))
            kb.op("pool", [], [stbf_b], lambda e: e.memset(stbf[:], 0.0))
            for j in range(12):
                kb.op("pool", [], [xh_bs[j]], lambda e, j=j: e.memset(xh[:, j, 0:3], 0.0))
            for j in range(8):
                kb.op("pool", [], [uh_bs[j]], lambda e, j=j: e.memset(uh[:, j, 0:30], 0.0))

            def rstd_from_ss(ss_ap, n, out_ap, bufs):
                kb.op("dve", bufs, bufs, lambda e: e.tensor_scalar(out=out_ap, in0=ss_ap, scalar1=1.0 / n, scalar2=EPS, op0=ALU.mult, op1=ALU.add))
                kb.op("act", bufs, bufs, lambda e: e.activation(out=out_ap, in_=out_ap, func=AF.Sqrt))
                kb.op("dve", bufs, bufs, lambda e: e.reciprocal(out=out_ap, in_=out_ap))

            def proj_fm(wt, wb_, col0, pt, pb_):
                for k in range(8):
                    kb.op("pe", [wb_, hnT_b], [pb_], lambda e, k=k: e.matmul(pt[:], lhsT=wt[:, k, col0:col0 + 128], rhs=hnT[:, k, :], start=(k == 0), stop=(k == 7)))

            for blk in range(NB):
                t0 = blk * 512
                def emit_diag(j, kks=range(31)):
                    dgt, dgb = dgs[j % 2]
                    for kk in kks:
                        kb.op("pool", [cf_b, ccw_b], [dgb], lambda e, kk=kk: e.tensor_scalar(out=dgt[:, kk, :], in0=ident_f, scalar1=ccw[:, j, kk:kk + 1], scalar2=0.0, op0=ALU.mult, op1=ALU.add), waw=True)

                emit_diag(0)
                for tt in range(4):
                    ti = blk * 4 + tt
                    kb.dma("sp", [h_b[ti]], [hblk_b], hblk[:, tt, :], src[t0 + tt * 128:t0 + (tt + 1) * 128, :], waw=(tt > 0))
                for tt in range(4):
                    kb.op("act", [hblk_b], [junk_b, st4_b], lambda e, tt=tt: e.activation(out=junk[:], in_=hblk[:, tt, :], func=AF.Square, accum_out=st4[:, tt:tt + 1]))
                rstd_from_ss(st4[:, 0:4], float(D), st4[:, 4:8], [st4_b])
                for tt in range(4):
                    kb.op("dve", [hblk_b, st4_b, norm_b], [hn_b], lambda e, tt=tt: e.scalar_tensor_tensor(out=hn[:], in0=hblk[:, tt, :], scalar=st4[:, 4 + tt:5 + tt], in1=norm_t[:], op0=ALU.mult, op1=ALU.mult))
                    pt, pb_ = PS[2]
                    ptb = pt[:].bitcast(BF16)
                    for k in range(8):
                        kb.op("pe", [hn_b, cb_b], [pb_], lambda e, k=k: e.transpose(out=ptb[:, k * 128:(k + 1) * 128], in_=hn[:, k * 128:(k + 1) * 128], identity=ident_b))
                    kb.op("act", [pb_], [hnT_b], lambda e, tt=tt: e.activation(out=hnT[:, :, tt * 128:(tt + 1) * 128], in_=ptb.rearrange("p (k t) -> p k t", k=8), func=AF.Copy), waw=(tt > 0))

                pi = 0
                for half in range(2):
                    wt, wb_ = load_w(w_in_bf, 0, half * 512, 512, wkey="cwi%d" % li)
                    for tt in range(4):
                        pt, pb_ = PS[(0, 1, 4, 5)[pi % 4]]
                        pi += 1
                        for k in range(8):
                            kb.op("pe", [wb_, hnT_b], [pb_], lambda e, k=k, tt=tt: e.matmul(pt[:], lhsT=hnT[:, k, tt * 128:(tt + 1) * 128], rhs=wt[:, k, :], start=(k == 0), stop=(k == 7)))
                        kb.op("act", [pb_], [sz_b], lambda e, tt=tt, half=half: e.activation(out=sz[:, tt, half * 512:(half + 1) * 512], in_=pt[:], func=AF.Silu), waw=True)

                pend = []
                dacc = [(cacc, cacc_b), (lnm, lnm_b), (lnr, lnr_b)]
                dai = 0
                for sl in range(3):
                    wt, wb_ = load_w(w_in_bf, 0, 1024 + sl * 512, 512, wkey="cwi%d" % li)
                    for jj in range(4):
                        j = sl * 4 + jj
                        pt, pb_ = PS[(0, 1, 4, 5)[pi % 4]]
                        pi += 1
                        proj_fm(wt, wb_, jj * 128, pt, pb_)
                        kb.op("act", [pb_], [xh_bs[j]], lambda e, j=j: e.activation(out=xh[:, j, 3:515], in_=pt[:], func=AF.Copy))
                        if j % 3 == 0:
                            ca, ca_b = caccp, caccp_b
                            t2, t2_b = un, un_b
                            kb.op("pool", [xh_bs[j], cw_b], [ca_b], lambda e, j=j: e.tensor_scalar(out=ca[:], in0=xh[:, j, 0:512], scalar1=cw[:, j, 0:1], scalar2=0.0, op0=ALU.mult, op1=ALU.add))
                            for kk in range(1, 4):
                                kb.op("pool", [xh_bs[j], cw_b], [t2_b], lambda e, j=j, kk=kk: e.tensor_scalar(out=t2[:], in0=xh[:, j, kk:kk + 512], scalar1=cw[:, j, kk:kk + 1], scalar2=0.0, op0=ALU.mult, op1=ALU.add))
                                kb.op("pool", [t2_b, ca_b], [ca_b], lambda e: e.tensor_tensor(out=ca[:], in0=ca[:], in1=t2[:], op=ALU.add))
                            delay = 2
                        else:
                            ca, ca_b = dacc[dai % 3]
                            dai += 1
                            kb.op("dve", [xh_bs[j], cw_b], [ca_b], lambda e, j=j, ca=ca: e.tensor_scalar(out=ca[:], in0=xh[:, j, 0:512], scalar1=cw[:, j, 0:1], scalar2=None, op0=ALU.mult))
                            for kk in range(1, 4):
                                kb.op("dve", [xh_bs[j], cw_b, ca_b], [ca_b], lambda e, j=j, kk=kk, ca=ca: e.scalar_tensor_tensor(out=ca[:], in0=xh[:, j, kk:kk + 512], scalar=cw[:, j, kk:kk + 1], in1=ca[:], op0=ALU.mult, op1=ALU.add))
                            delay = 1

                        def fin(j=j, ca=ca, ca_b=ca_b):
                            kb.op("act", [ca_b, cbias_b], [xa_bs[j]], lambda e: e.activation(out=xa[:, j, :], in_=ca[:], func=AF.Silu, bias=cbias[:, j:j + 1]))
                            kb.op("pool", [xh_bs[j]], [xh_bs[j]], lambda e: e.tensor_copy(out=xh[:, j, 0:3], in_=xh[:, j, 512:515]))
                        pend.append((j + delay, fin))
                        for it in list(pend):
                            if it[0] <= j:
                                it[1]()
                                pend.remove(it)
                for it in pend:
                    it[1]()

                wt, wb_ = wdt, wdt_b
                pt, pb_ = PS[pi % 2]
                pi += 1
                for tt in range(4):
                    for k in range(8):
                        kb.op("pe", [wb_, hnT_b], [pb_], lambda e, k=k, tt=tt: e.matmul(pt[:, tt * 16:(tt + 1) * 16], lhsT=hnT[:, k, tt * 128:(tt + 1) * 128], rhs=wt[:, k, 0:16], start=(k == 0), stop=(k == 7)), waw=(tt > 0 or k > 0))
                kb.op("act", [pb_], [sp1_b], lambda e: e.activation(out=sp1[:], in_=pt[:, 0:64], func=AF.Copy))
                kb.op("dve", [sp1_b, small_b], [sp1_b], lambda e: e.tensor_tensor(out=sp1[:].rearrange("p (t h) -> p t h", t=4), in0=sp1[:].rearrange("p (t h) -> p t h", t=4), in1=dtb.unsqueeze(1).broadcast_to([128, 4, 16]), op=ALU.add))
                kb.op("act", [sp1_b], [sp2_b], lambda e: e.activation(out=sp2[:], in_=sp1[:], func=AF.Abs))
                kb.op("act", [sp2_b], [sp2_b], lambda e: e.activation(out=sp2[:], in_=sp2[:], func=AF.Exp, scale=-1.0))
                kb.op("act", [sp2_b], [sp2_b], lambda e: e.activation(out=sp2[:], in_=sp2[:], func=AF.Ln, bias=1.0))
                kb.op("dve", [sp1_b, sp2_b], [dt_b], lambda e: e.scalar_tensor_tensor(out=dt_t[:].rearrange("p t h -> p (t h)"), in0=sp1[:], scalar=0.0, in1=sp2[:], op0=ALU.max, op1=ALU.add))
                kb.op("dve", [dt_b, small_b], [dta_b], lambda e: e.tensor_tensor(out=dta_t[:], in0=dt_t[:], in1=a_t.unsqueeze(1).broadcast_to([128, 4, 16]), op=ALU.mult))

                for sl in range(2):
                    wa, wab = load_w(w_in_bf, 0, 2576 + sl * 512, 512, wkey="cwi%d" % li)
                    wu, wub = load_w(w_in_bf, 0, 3600 + sl * 512, 512, wkey="cwi%d" % li)
                    for jj in range(4):
                        j = sl * 4 + jj
                        pa, pab = PS[(0, 4)[j % 2]]
                        pu, pub = PS[(1, 5)[j % 2]]
                        proj_fm(wa, wab, jj * 128, pa, pab)
                        proj_fm(wu, wub, jj * 128, pu, pub)
                        sgt, sgtb = (sg, sg_b) if j % 2 == 0 else (szc, szc_b)
                        kb.op("act", [pub], [sgtb], lambda e: e.activation(out=sgt[:], in_=pu[:], func=AF.Sigmoid))
                        uat, uatb = uats[j % 2]
                        kb.op("act", [pab], [uatb], lambda e: e.activation(out=uat[:], in_=pa[:], func=AF.Copy))
                        kb.op("dve", [uatb, sgtb], [uh_bs[j]], lambda e, j=j: e.tensor_tensor(out=uh[:, j, 30:542], in0=uat[:], in1=sgt[:], op=ALU.mult))

                def emit_conv_mm(j, kk):
                    dgt, dgb = dgs[j % 2]
                    cp, cpb = PS[j % 2]
                    if j + 1 < 8:
                        emit_diag(j + 1, [kk])
                    kb.op("pe", [dgb, uh_bs[j]], [cpb], lambda e: e.matmul(cp[:], lhsT=dgt[:, kk, :], rhs=uh[:, j, kk:kk + 512], start=(kk == 0), stop=(kk == 30)))
                    if kk == 30:
                        kb.op("act", [cpb, cc4_b], [uc_bs[j]], lambda e: e.activation(out=uc[:, j, :], in_=cp[:], func=AF.Identity, bias=cc4[:, 0, j:j + 1]))
                        kb.op("dve", [uh_bs[j]], [uh_bs[j]], lambda e: e.tensor_copy(out=uh[:, j, 0:30], in_=uh[:, j, 512:542]))

                fillers = [(j, kk) for j in range(8) for kk in range(31)]

                def fill(n):
                    for _ in range(min(n, len(fillers))):
                        j_, kk_ = fillers.pop(0)
                        emit_conv_mm(j_, kk_)

                for tt in range(4):
                    cs = slice(tt * 128, (tt + 1) * 128)
                    pt, pb_ = PS[2]
                    ptb = pt[:].bitcast(BF16)
                    for j in range(8):
                        kb.op("pe", [xa_bs[j], cb_b], [pb_], lambda e, j=j: e.transpose(out=ptb[:, j * 128:(j + 1) * 128], in_=xa[:, j, cs], identity=ident_b))
                    kb.op("act", [pb_], [xt_b], lambda e: e.activation(out=xt[:], in_=ptb, func=AF.Copy))
                    pt3, pb3 = PS[3]
                    pt3b = pt3[:].bitcast(BF16)
                    for g in range(2):
                        kb.op("pe", [xa_bs[8 + g], cb_b], [pb3], lambda e, g=g: e.transpose(out=pt3b[:, g * 128:(g + 1) * 128], in_=xa[:, 8 + g, cs], identity=ident_b))
                    kb.op("act", [pb3], [Bt_b], lambda e: e.activation(out=Bt[:], in_=pt3b[:, 0:256].rearrange("p (g n) -> p g n", g=2), func=AF.Copy))
                    fill(8)
                    pt, pb_ = PS[2]
                    kb.op("pe", [dta_b, cf_b], [pb_], lambda e: e.matmul(pt[:, 0:16], lhsT=L_f, rhs=dta_t[:, tt, :], start=True, stop=True))
                    kb.op("pe", [dta_b, cf_b], [pb_], lambda e: e.matmul(pt[:, 16:32], lhsT=ones_f, rhs=dta_t[:, tt, :], start=True, stop=True))
                    fill(6)
                    kb.op("act", [pb_], [cums_b], lambda e: e.activation(out=cums[:, 0:32], in_=pt[:, 0:32], func=AF.Exp))
                    kb.op("dve", [pb_], [cums_b], lambda e: e.tensor_copy(out=cums[:, 48:64], in_=pt[:, 0:16]))
                    kb.op("dve", [pb_, cums_b], [cums_b], lambda e: e.tensor_tensor(out=cums[:, 48:64], in0=pt[:, 16:32], in1=cums[:, 48:64], op=ALU.subtract))
                    kb.op("act", [cums_b], [cums_b], lambda e: e.activation(out=cums[:, 32:48], in_=cums[:, 48:64], func=AF.Exp))
                    kb.op("dve", [cums_b, dt_b], [cums_b], lambda e: e.tensor_tensor(out=cums[:, 32:48], in0=cums[:, 32:48], in1=dt_t[:, tt, :], op=ALU.mult))
                    ecum = cums[:, 0:16]
                    etot = cums[:, 16:32]
                    toend = cums[:, 32:48]
                    for g in range(2):
                        pt3, pb3 = PS[3]
                        kb.op("pe", [xa_bs[8 + g], xa_bs[10 + g]], [pb3], lambda e, g=g: e.matmul(pt3[:, 0:128], lhsT=xa[:, 8 + g, cs], rhs=xa[:, 10 + g, cs], start=True, stop=True))
                        kb.op("dve", [pb3, cf_b], [CBm_b], lambda e, g=g: e.tensor_tensor(out=CBm[:, g, :], in0=pt3[:, 0:128], in1=tri_f, op=ALU.mult), waw=(g > 0))
                    for g in range(2):
                        pt, pb_ = PS[6 + g]
                        kb.op("pe", [xa_bs[10 + g], stbf_b], [pb_], lambda e, g=g: e.matmul(pt[:], lhsT=xa[:, 10 + g, cs], rhs=stbf[:, g, :], start=True, stop=True))
                        kb.op("dve", [pb_, cums_b], [y1_b], lambda e, g=g: e.tensor_tensor(out=y1[:, g * 512:(g + 1) * 512].rearrange("p (h d) -> p h d", h=8), in0=pt[:].rearrange("p (h d) -> p h d", h=8), in1=ecum[:, g * 8:(g + 1) * 8].unsqueeze(2).broadcast_to([128, 8, 64]), op=ALU.mult), waw=(g > 0))
                    def stage_a(hq):
                        sgp, sgb = PS[4 + hq % 2]
                        dct, dcb = dec[hq % 2]
                        for hh in range(4):
                            h = hq * 4 + hh
                            lt, lb = lh[h % 8]
                            kb.op("pool", [cf_b, dta_b], [lb], lambda e, h=h: e.tensor_scalar(out=lt[:], in0=U_f, scalar1=dta_t[:, tt, h:h + 1], scalar2=0.0, op0=ALU.mult, op1=ALU.add))
                            kb.op("pe", [lb, cf_b], [sgb], lambda e, hh=hh: e.matmul(sgp[:, hh * 128:(hh + 1) * 128], lhsT=lt[:], rhs=L_f, start=True, stop=True), waw=(hh > 0))
                        kb.op("act", [sgb], [dcb], lambda e: e.activation(out=dct[:], in_=sgp[:], func=AF.Exp))

                    def stage_b(hq):
                        dct, dcb = dec[hq % 2]
                        for hh in range(4):
                            h = hq * 4 + hh
                            g = h // 8
                            wt_, wtb = wT[h % 4]
                            kb.op("dve", [dcb, dt_b, CBm_b], [wtb], lambda e, h=h, hh=hh, g=g: e.scalar_tensor_tensor(out=wt_[:], in0=dct[:, hh * 128:(hh + 1) * 128], scalar=dt_t[:, tt, h:h + 1], in1=CBm[:, g, :], op0=ALU.mult, op1=ALU.mult))
                            pt, pb_ = PS[6 + g]
                            hl = h % 8
                            kb.op("pe", [wtb, xt_b], [pb_], lambda e, h=h, hl=hl: e.matmul(pt[:, hl * 64:(hl + 1) * 64], lhsT=wt_[:], rhs=xt[:, h * 64:(h + 1) * 64], start=True, stop=True), waw=(hl > 0))

                    stage_a(0)
                    fill(4)
                    stage_a(1)
                    fill(4)
                    stage_b(0)
                    stage_a(2)
                    fill(6)
                    stage_b(1)
                    stage_a(3)
                    fill(6)
                    stage_b(2)
                    fill(6)
                    stage_b(3)
                    fill(6)
                    for g in range(2):
                        pt, pb_ = PS[6 + g]
                        kb.op("dve", [pb_, y1_b], [y1_b], lambda e, g=g: e.tensor_tensor(out=y1[:, g * 512:(g + 1) * 512], in0=pt[:], in1=y1[:, g * 512:(g + 1) * 512], op=ALU.add))
                    kb.op("dve", [xt_b, small_b], [y2_b], lambda e: e.tensor_tensor(out=y2[:].rearrange("p (h d) -> p h d", h=16), in0=xt[:].rearrange("p (h d) -> p h d", h=16), in1=dsk.unsqueeze(2).broadcast_to([128, 16, 64]), op=ALU.mult))
                    kb.op("dve", [y1_b, y2_b], [y1_b], lambda e: e.tensor_tensor(out=y1[:], in0=y1[:], in1=y2[:], op=ALU.add))
                    kb.op("dve", [xt_b, cums_b], [xw_b], lambda e: e.tensor_tensor(out=xw[:].rearrange("p (h d) -> p h d", h=16), in0=xt[:].rearrange("p (h d) -> p h d", h=16), in1=toend.unsqueeze(2).broadcast_to([128, 16, 64]), op=ALU.mult))
                    for g in range(2):
                        pt3, pb3 = PS[3]
                        kb.op("pe", [Bt_b, xw_b], [pb3], lambda e, g=g: e.matmul(pt3[:], lhsT=Bt[:, g, :], rhs=xw[:, g * 512:(g + 1) * 512], start=True, stop=True))
                        kb.op("dve", [st32_b, cums_b], [st32_b], lambda e, g=g: e.tensor_tensor(out=st32[:, g, :].rearrange("p (h d) -> p h d", h=8), in0=st32[:, g, :].rearrange("p (h d) -> p h d", h=8), in1=etot[:, g * 8:(g + 1) * 8].unsqueeze(2).broadcast_to([128, 8, 64]), op=ALU.mult))
                        kb.op("dve", [pb3, st32_b], [st32_b], lambda e, g=g: e.tensor_tensor(out=st32[:, g, :], in0=pt3[:], in1=st32[:, g, :], op=ALU.add))
                    kb.op("act", [st32_b], [stbf_b], lambda e: e.activation(out=stbf[:], in_=st32[:], func=AF.Copy))
                    fill(8)
                    kb.op("dve", [y1_b, sz_b], [y1_b], lambda e: e.tensor_tensor(out=y1[:], in0=y1[:], in1=sz[:, tt, :], op=ALU.mult))
                    for g in range(2):
                        kb.op("act", [y1_b], [junk_b, st4_b], lambda e, g=g: e.activation(out=junk[:, 0:512], in_=y1[:, g * 512:(g + 1) * 512], func=AF.Square, accum_out=st4[:, g:g + 1]))
                    rstd_from_ss(st4[:, 0:2], 512.0, st4[:, 2:4], [st4_b])
                    for g in range(2):
                        kb.op("dve", [y1_b, st4_b, ssdn_b], [ya_b], lambda e, g=g: e.scalar_tensor_tensor(out=ya[:, g * 512:(g + 1) * 512], in0=y1[:, g * 512:(g + 1) * 512], scalar=st4[:, 2 + g:3 + g], in1=ssdn_t[:, g * 512:(g + 1) * 512], op0=ALU.mult, op1=ALU.mult), waw=(g > 0))
                    pt, pb_ = PS[2]
                    ptb = pt[:].bitcast(BF16)
                    for k in range(8):
                        kb.op("pe", [ya_b, cb_b], [pb_], lambda e, k=k: e.transpose(out=ptb[:, k * 128:(k + 1) * 128], in_=ya[:, k * 128:(k + 1) * 128], identity=ident_b))
                    kb.op("act", [pb_], [yaT_b], lambda e, tt=tt: e.activation(out=yaT[:, :, tt * 128:(tt + 1) * 128], in_=ptb.rearrange("p (k t) -> p k t", k=8), func=AF.Copy), waw=(tt > 0))
                    fill(8)

                fill(len(fillers))
                mean_p, mean_pb = PS[4]
                msq_p, msq_pb = PS[5]
                for j in range(8):
                    sqt, sqtb = (sq, sq_b) if j % 2 == 0 else (s1, s1_b)
                    kb.op("act", [uc_bs[j]], [sqtb], lambda e, j=j: e.activation(out=sqt[:], in_=uc[:, j, :], func=AF.Square))
                    kb.op("pe", [uc_bs[j], cf_b], [mean_pb], lambda e, j=j: e.matmul(mean_p[:], lhsT=ones_f, rhs=uc[:, j, :], start=(j == 0), stop=(j == 7)))
                    kb.op("pe", [sqtb, cf_b], [msq_pb], lambda e, j=j: e.matmul(msq_p[:], lhsT=ones_f, rhs=sqt[:], start=(j == 0), stop=(j == 7)))
                kb.op("act", [mean_pb], [lnm_b], lambda e: e.activation(out=lnm[:], in_=mean_p[:], func=AF.Copy, scale=1.0 / 1024))
                kb.op("act", [lnm_b], [sq_b], lambda e: e.activation(out=sq[:], in_=lnm[:], func=AF.Square))
                kb.op("dve", [msq_pb, sq_b], [lnr_b], lambda e: e.scalar_tensor_tensor(out=lnr[:], in0=msq_p[:], scalar=1.0 / 1024, in1=sq[:], op0=ALU.mult, op1=ALU.subtract))
                kb.op("dve", [lnr_b], [lnr_b], lambda e: e.tensor_scalar(out=lnr[:], in0=lnr[:], scalar1=EPS, scalar2=None, op0=ALU.add))
                kb.op("act", [lnr_b], [lnr_b], lambda e: e.activation(out=lnr[:], in_=lnr[:], func=AF.Sqrt))
                kb.op("dve", [lnr_b], [lnr_b], lambda e: e.reciprocal(out=lnr[:], in_=lnr[:]))
                kb.op("dve", [lnm_b, lnr_b], [lnm_b], lambda e: e.scalar_tensor_tensor(out=lnm[:], in0=lnm[:], scalar=-1.0, in1=lnr[:], op0=ALU.mult, op1=ALU.mult))
                for sl in range(2):
                    wz, wzb = load_w(w_in_bf, 0, 4624 + sl * 512, 512, wkey="cwi%d" % li)
                    for jj in range(4):
                        j = sl * 4 + jj
                        pz, pzb = PS[j % 2]
                        proj_fm(wz, wzb, jj * 128, pz, pzb)
                        kb.op("act", [pzb], [szc_b], lambda e: e.activation(out=szc[:], in_=pz[:], func=AF.Silu))
                        kb.op("dve", [uc_bs[j], lnr_b], [un_b], lambda e, j=j: e.tensor_tensor(out=un[:], in0=uc[:, j, :], in1=lnr[:], op=ALU.mult))
                        kb.op("dve", [un_b, lnm_b], [un_b], lambda e: e.tensor_tensor(out=un[:], in0=un[:], in1=lnm[:], op=ALU.add))
                        kb.op("act", [un_b, cc4_b], [s1_b], lambda e, j=j: e.activation(out=s1[:], in_=un[:], func=AF.Silu, scale=cc4[:, 1, j:j + 1], bias=cc4[:, 2, j:j + 1]))
                        kb.op("dve", [s1_b, szc_b], [ybT_b], lambda e, j=j: e.tensor_tensor(out=ybT[:, j, :], in0=s1[:], in1=szc[:], op=ALU.mult), waw=(j > 0))

                for half in range(2):
                    for kh in range(2):
                        wo, wob = load_w(w_out_bf, kh * 1024, half * 512, 512, wkey="cwo%d" % li)
                        srcT, srcb = (yaT, yaT_b) if kh == 0 else (ybT, ybT_b)
                        for tt in range(4):
                            pt, pb_ = PS[4 + tt]
                            for k in range(8):
                                kb.op("pe", [wob, srcb], [pb_], lambda e, k=k, tt=tt, kh=kh: e.matmul(pt[:], lhsT=srcT[:, k, tt * 128:(tt + 1) * 128], rhs=wo[:, k, :], start=(kh == 0 and k == 0), stop=(kh == 1 and k == 7)))
                    for tt in range(4):
                        pt, pb_ = PS[4 + tt]
                        kb.op("dve", [pb_, hblk_b], [hblk_b], lambda e, tt=tt, half=half: e.tensor_tensor(out=hblk[:, tt, half * 512:(half + 1) * 512], in0=pt[:], in1=hblk[:, tt, half * 512:(half + 1) * 512], op=ALU.add))
                for tt in range(4):
                    ti = blk * 4 + tt
                    kb.dma("pool", [hblk_b], [h_b[ti]], hD[t0 + tt * 128:t0 + (tt + 1) * 128, :], hblk[:, tt, :])
            state["src"] = hD


        def odd_layer(li):
            src = state["src"]
            NSEL = S // 64
            NCMP = S // 16 - 1
            NCT = (NCMP + 127) // 128
            NCP = NCT * 128
            w_in_bf = ow["w_in_bf"][li]
            wk = "owi%d" % li
            ksA, ksA_b = kb.sb("o_ksA", [128, 4, S], BF16)
            kwT, kwT_b = kb.sb("o_kwT", [64, 4, S], BF16)
            vsA, vsA_b = kb.sb("o_vsA", [128, NT, 4, 65], BF16)
            vwA, vwA_b = kb.sb("o_vwA", [128, NT, 4, 65], BF16)
            kcT, kcT_b = kb.sb("o_kcT", [64, 4, NCP], BF16)
            vcA, vcA_b = kb.sb("o_vcA", [128, NCT, 4, 65], BF16)
            gts, gts_b = kb.sb("o_gts", [128, NT, 48], F32)
            wo, wo_b = kb.sb("o_wo", [128, 8, D], BF16)
            norm_t, norm_b = kb.sb("o_norm_t", [128, D], F32)
            gbias, gbias_b = kb.sb("o_gbias", [128, 48], F32)
            Amask, Amask_b = kb.sb("o_Amask", [128, 128], F32)
            kb.dma("sp", [], [norm_b], norm_t[:], ow["norm"][li])
            kb.dma("sp", [], [gbias_b], gbias[:], ow["gate_bias"][li])
            kb.dma("sp", [], [Amask_b], Amask[:], ow["amask"][:, :])
            kb.dma("sp", [wdram_b["owo%d" % li]], [wo_b], wo[:], ow["w_out_bf"][li].rearrange("(k p) c -> p k c", p=128))
            for g in range(4):
                kb.dma("sp", [], [ksA_b], ksA[64:128, g, :], ow["emat"][:, :], waw=True)
            kb.op("pool", [], [vsA_b], lambda e: e.memset(vsA[:], 1.0))
            kb.op("pool", [], [vwA_b], lambda e: e.memset(vwA[:], 1.0))
            kb.op("pool", [], [vcA_b], lambda e: e.memset(vcA[:], 0.0))
            kb.op("pool", [vcA_b], [vcA_b], lambda e: e.memset(vcA[:, :, :, 64:65], 1.0))
            kb.op("pool", [], [kcT_b], lambda e: e.memset(kcT[:], 0.0))

            def rstd_from_ss(ss_ap, n, out_ap, bufs):
                kb.op("dve", bufs, bufs, lambda e: e.tensor_scalar(out=out_ap, in0=ss_ap, scalar1=1.0 / n, scalar2=EPS, op0=ALU.mult, op1=ALU.add))
                kb.op("act", bufs, bufs, lambda e: e.activation(out=out_ap, in_=out_ap, func=AF.Sqrt))
                kb.op("dve", bufs, bufs, lambda e: e.reciprocal(out=out_ap, in_=out_ap))

            qT_D = ow["qT_D"]
            kT_D = ow["kT_D"]
            sz_D = ow["sz_D"]
            qD_b = kb.buf("qD")
            kD_b = kb.buf("kD")
            szD_b = kb.buf("szD")

            with ExitStack() as esA:
                old = kb.es
                kb.es = esA
                wgl, wgl_b = kb.sb("oA_wgl", [128, 8, 48], BF16)
                kb.dma("sp", [wdram_b[wk]], [wgl_b], wgl[:], w_in_bf[0:1024, 2560:2608].rearrange("(k p) c -> p k c", p=128))
                hts = [kb.sb("oA_h%d" % i, [128, D], F32) for i in range(2)]
                hn, hn_b = kb.sb("oA_hn", [128, D], BF16)
                junk, junk_b = kb.sb("oA_junk", [128, D], F32)
                st4, st4_b = kb.sb("oA_st4", [128, 2], F32)
                hnT, hnT_b = kb.sb("oA_hnT", [128, 8, 512], BF16)
                stg = [kb.sb("oA_stg%d" % i, [128, 512], BF16) for i in range(3)]
                gtmp, gtmp_b = kb.sb("oA_gtmp", [128, 48], F32)
                si = [0]
                pi = [0]

                def nps():
                    p = PS[(0, 1, 3, 4, 5, 6)[pi[0] % 6]]
                    pi[0] += 1
                    return p

                def fm_tile(wt, wb_, col0, dst_ap, scale=1.0):
                    pt, pb_ = nps()
                    for k in range(8):
                        kb.op("pe", [wb_, hnT_b], [pb_], lambda e, k=k: e.matmul(pt[:], lhsT=wt[:, k, col0:col0 + 128], rhs=hnT[:, k, :], start=(k == 0), stop=(k == 7)))
                    s_, sb_ = stg[si[0] % 3]
                    si[0] += 1
                    kb.op("act", [pb_], [sb_], lambda e: e.activation(out=s_[:], in_=pt[:], func=AF.Copy, scale=scale))
                    return s_, sb_

                for blk in range(NB):
                    t0 = blk * 512
                    for tt in range(4):
                        ti = blk * 4 + tt
                        ht, hb = hts[ti % 2]
                        kb.dma("sp", [h_b[ti]], [hb], ht[:], src[t0 + tt * 128:t0 + (tt + 1) * 128, :])
                        kb.op("act", [hb], [junk_b, st4_b], lambda e: e.activation(out=junk[:], in_=ht[:], func=AF.Square, accum_out=st4[:, 0:1]))
                        rstd_from_ss(st4[:, 0:1], float(D), st4[:, 1:2], [st4_b])
                        kb.op("dve", [hb, st4_b, norm_b], [hn_b], lambda e: e.scalar_tensor_tensor(out=hn[:], in0=ht[:], scalar=st4[:, 1:2], in1=norm_t[:], op0=ALU.mult, op1=ALU.mult))
                        pt, pb_ = PS[2]
                        ptb = pt[:].bitcast(BF16)
                        for k in range(8):
                            kb.op("pe", [hn_b, cb_b], [pb_], lambda e, k=k: e.transpose(out=ptb[:, k * 128:(k + 1) * 128], in_=hn[:, k * 128:(k + 1) * 128], identity=ident_b))
                        kb.op("act", [pb_], [hnT_b], lambda e, tt=tt: e.activation(out=hnT[:, :, tt * 128:(tt + 1) * 128], in_=ptb.rearrange("p (k t) -> p k t", k=8), func=AF.Copy), waw=(tt > 0))
                    for sl in range(2):
                        wt, wb_ = load_w(w_in_bf, 0, sl * 512, 512, wkey=wk)
                        for jj in range(4):
                            s_, sb_ = fm_tile(wt, wb_, jj * 128, None, scale=0.125)
                            r0 = (sl * 4 + jj) * 128
                            kb.dma("pool", [sb_], [qD_b], qT_D[r0:r0 + 128, t0:t0 + 512], s_[:], waw=True)
                    wt, wb_ = load_w(w_in_bf, 0, 1024, 512, wkey=wk)
                    for jj in range(4):
                        s_, sb_ = fm_tile(wt, wb_, jj * 128, None)
                        kind, half = jj // 2, jj % 2
                        kb.dma("pool", [sb_], [kD_b], kT_D[kind, half * 128:(half + 1) * 128, t0:t0 + 512], s_[:], waw=True)
                    for kind, c0, vA, vA_b in ((2, 1536, vsA, vsA_b), (3, 2048, vwA, vwA_b)):
                        wt, wb_ = load_w(w_in_bf, 0, c0, 512, wkey=wk)
                        for jj in range(2):
                            s_, sb_ = fm_tile(wt, wb_, jj * 128, None)
                            kb.dma("pool", [sb_], [kD_b], kT_D[kind, jj * 128:(jj + 1) * 128, t0:t0 + 512], s_[:], waw=True)
                        for tt in range(4):
                            ti = blk * 4 + tt
                            pt, pb_ = nps()
                            for k in range(8):
                                kb.op("pe", [wb_, hnT_b], [pb_], lambda e, k=k, tt=tt: e.matmul(pt[:, 0:256], lhsT=hnT[:, k, tt * 128:(tt + 1) * 128], rhs=wt[:, k, 256:512], start=(k == 0), stop=(k == 7)))
                            kb.op("act", [pb_], [vA_b], lambda e, ti=ti, vA=vA: e.activation(out=vA[:, ti, :, 0:64], in_=pt[:, 0:256].rearrange("p (g d) -> p g d", g=4), func=AF.Copy), waw=True)
                    wt, wb_ = wgl, wgl_b
                    for tt in range(4):
                        ti = blk * 4 + tt
                        pt, pb_ = nps()
                        for k in range(8):
                            kb.op("pe", [wb_, hnT_b], [pb_], lambda e, k=k, tt=tt: e.matmul(pt[:, 0:48], lhsT=hnT[:, k, tt * 128:(tt + 1) * 128], rhs=wt[:, k, 0:48], start=(k == 0), stop=(k == 7)))
                        kb.op("dve", [pb_, gbias_b], [gtmp_b], lambda e: e.tensor_tensor(out=gtmp[:], in0=pt[:, 0:48], in1=gbias[:], op=ALU.add))
                        kb.op("act", [gtmp_b], [gts_b], lambda e, ti=ti: e.activation(out=gts[:, ti, :], in_=gtmp[:], func=AF.Sigmoid), waw=True)
                    for half in range(2):
                        wt, wb_ = load_w(w_in_bf, 0, 2608 + half * 512, 512, wkey=wk)
                        for tt in range(4):
                            pt, pb_ = nps()
                            for k in range(8):
                                kb.op("pe", [wb_, hnT_b], [pb_], lambda e, k=k, tt=tt: e.matmul(pt[:], lhsT=hnT[:, k, tt * 128:(tt + 1) * 128], rhs=wt[:, k, :], start=(k == 0), stop=(k == 7)))
                            s_, sb_ = stg[si[0] % 3]
                            si[0] += 1
                            kb.op("act", [pb_], [sb_], lambda e: e.activation(out=s_[:], in_=pt[:], func=AF.Silu))
                            kb.dma("pool", [sb_], [szD_b], sz_D[t0 + tt * 128:t0 + (tt + 1) * 128, half * 512:(half + 1) * 512], s_[:], waw=True)
                kb.es = old
            kb.barrier()
            if ODD_STOP == "A":
                return
            kb.dma("sp", [kD_b], [ksA_b], ksA[0:64, :, :], kT_D[2].rearrange("(g d) s -> d g s", g=4), waw=True)
            kb.dma("sp", [kD_b], [kwT_b], kwT[:, :, :], kT_D[3].rearrange("(g d) s -> d g s", g=4))

            with ExitStack() as esB:
                old = kb.es
                kb.es = esB
                w1f, w1f_b = kb.sb("oB_w1f", [64, 16, 256], F32)
                w1b, w1b_b = kb.sb("oB_w1b", [64, 32, 256], BF16)
                w2f, w2f_b = kb.sb("oB_w2f", [128, 2, 64], F32)
                w2b, w2b_b = kb.sb("oB_w2b", [128, 2, 64], BF16)
                pef, pef_b = kb.sb("oB_pef", [64, 32], F32)
                peb, peb_b = kb.sb("oB_peb", [64, 32], BF16)
                b1, b1_b = kb.sb("oB_b1", [128, 2], F32)
                tcs = [kb.sb("oB_tc%d" % i, [64, S], BF16) for i in range(2)]
                h1T, h1T_b = kb.sb("oB_h1T", [128, 2, NCP], BF16)
                for kind in range(2):
                    nm = "kv"[kind]
                    for lh_ in range(2):
                        kb.dma("sp", [], [w1f_b], w1f[:], ow["w1_" + nm][li][lh_ * 1024:(lh_ + 1) * 1024, :].rearrange("(l d) j -> d l j", d=64))
                        kb.op("pool", [w1f_b], [w1b_b], lambda e, lh_=lh_: e.tensor_copy(out=w1b[:, lh_ * 16:(lh_ + 1) * 16, :], in_=w1f[:]), waw=(lh_ > 0))
                    kb.dma("sp", [], [w2f_b], w2f[:], ow["w2_" + nm][li].rearrange("(jt p) d -> p jt d", p=128))
                    kb.op("pool", [w2f_b], [w2b_b], lambda e: e.tensor_copy(out=w2b[:], in_=w2f[:]))
                    kb.dma("sp", [], [pef_b], pef[:], ow["peT_" + nm][li])
                    kb.op("pool", [pef_b], [peb_b], lambda e: e.tensor_copy(out=peb[:], in_=pef[:]))
                    pt, pb_ = PS[2]
                    for jt in range(2):
                        for l in range(32):
                            kb.op("pe", [w1b_b, peb_b], [pb_], lambda e, jt=jt, l=l: e.matmul(pt[:, jt:jt + 1], lhsT=w1b[:, l, jt * 128:(jt + 1) * 128], rhs=peb[:, l:l + 1], start=(jt == 0 and l == 0), stop=(l == 31), skip_group_check=True), waw=True)
                    kb.op("dve", [pb_], [b1_b], lambda e: e.tensor_copy(out=b1[:], in_=pt[:, 0:2]))
                    for g in range(4):
                        tc_, tcb = tcs[g % 2]
                        kb.dma("sp", [kD_b], [tcb], tc_[:], kT_D[kind, g * 64:(g + 1) * 64, :])
                        tcv = tc_[:].rearrange("p (n s) -> p n s", s=16)
                        for jt in range(2):
                            pj, pjb = PS[jt]
                            for l in range(32):
                                kb.op("pe", [w1b_b, tcb], [pjb], lambda e, jt=jt, l=l: e.matmul(pj[:, 0:NCMP], lhsT=w1b[:, l, jt * 128:(jt + 1) * 128], rhs=tcv[:, l // 16:l // 16 + NCMP, l % 16], start=(l == 0), stop=(l == 31)))
                            kb.op("act", [pjb, b1_b], [h1T_b], lambda e, jt=jt: e.activation(out=h1T[:, jt, 0:NCMP], in_=pj[:, 0:NCMP], func=AF.Silu, bias=b1[:, jt:jt + 1]), waw=(jt > 0))
                        if kind == 0:
                            po, pob = PS[3]
                            for jt in range(2):
                                kb.op("pe", [w2b_b, h1T_b], [pob], lambda e, jt=jt: e.matmul(po[0:64, 0:NCMP], lhsT=w2b[:, jt, :], rhs=h1T[:, jt, 0:NCMP], start=(jt == 0), stop=(jt == 1)))
                            kb.op("act", [pob], [kcT_b], lambda e, g=g: e.activation(out=kcT[:, g, 0:NCMP], in_=po[0:64, 0:NCMP], func=AF.Copy), waw=True)
                        else:
                            for nt in range(NCT):
                                rows = min(NCMP, (nt + 1) * 128) - nt * 128
                                po, pob = PS[3]
                                for jt in range(2):
                                    kb.op("pe", [w2b_b, h1T_b], [pob], lambda e, jt=jt, nt=nt, rows=rows: e.matmul(po[0:rows, 0:64], lhsT=h1T[:, jt, nt * 128:nt * 128 + rows], rhs=w2b[:, jt, :], start=(jt == 0), stop=(jt == 1)))
                                kb.op("act", [pob], [vcA_b], lambda e, g=g, nt=nt, rows=rows: e.activation(out=vcA[0:rows, nt, g, 0:64], in_=po[0:rows, 0:64], func=AF.Copy), waw=True)
                kb.es = old
            kb.barrier()
            if ODD_STOP == "B":
                return

            with ExitStack() as esC:
                old = kb.es
                kb.es = esC
                qas = [kb.sb("oC_qa%d" % i, [128, 4, 512], BF16) for i in range(2)]
                qab = [[kb.buf("qab%d_%d" % (i, g)) for g in range(4)] for i in range(2)]
                NPT = 4
                pts = [kb.sb("oC_pt%d" % i, [128, 512], BF16) for i in range(NPT)]
                NET = 6
                ets = [kb.sb("oC_et%d" % i, [128, NCP], F32) for i in range(NET)]
                eti = [0]
                rss = [kb.sb("oC_rs%d" % i, [128, 8], F32) for i in range(2)]
                cmsk, cmsk_b = kb.sb("oC_cmsk", [10, 128 + 288], BF16)
                kb.dma("sp", [], [cmsk_b], cmsk[:], ow["cmsk"][:, :])
                pacc, pacc_b = kb.sb("oC_pacc", [128, NCP], F32)
                imp, imp_b = kb.sb("oC_imp", [128, 64], F32)
                imp2, imp2_b = kb.sb("oC_imp2", [128, 64], F32)
                m8, m8_b = kb.sb("oC_m8", [128, 16], F32)
                bts = [kb.sb("oC_bt%d" % g, [128, 128], BF16) for g in range(4)]
                szt = [kb.sb("oC_sz%d" % i, [128, D], BF16) for i in range(2)]
                hts = [kb.sb("oC_h%d" % i, [128, D], F32) for i in range(3)]
                cfs, cfs_b = kb.sb("oC_cfs", [128, 3, 4], F32)
                acc, acc_b = kb.sb("oC_acc", [128, 256], F32)
                ys = [kb.sb("oC_y%d" % i, [128, D], BF16) for i in range(2)]
                yT, yT_b = kb.sb("oC_yT", [128, 8, 128], BF16)
                for g in range(4):
                    kb.op("pool", [], [bts[g][1]], lambda e, g=g: e.memset(bts[g][0][:], 0.0))
                pti = [0]
                sci = [0]

                def sc_bank():
                    p = PS[sci[0] % 4]
                    sci[0] += 1
                    return p

                def o_bank(g, X):
                    return PS[4 + X]

                ob3s = [kb.sb("oC_ob3_%d" % i, [128, 3, 260], F32) for i in range(2)]
                tmp3, tmp3_b = kb.sb("oC_tmp3", [128, 3, 256], F32)

                def load_tile(qi):
                    slot = qi % 2
                    t0 = qi * 128
                    qa, qa_b = qas[slot]
                    kb.dma("sp", [qD_b], [qa_b], qa[0:64, :, :].rearrange("d g (r t) -> d g r t", r=4), qT_D[:, t0:t0 + 128].rearrange("(g r d) t -> d g r t", g=4, r=4))
                    sz_, sz_b = szt[slot]
                    kb.dma("sp", [szD_b], [sz_b], sz_[:], sz_D[t0:t0 + 128, :])
                    ht, hb = hts[qi % 3]
                    kb.dma("sp", [h_b[qi]], [hb], ht[:], src[t0:t0 + 128, :])

                def need_sel(qi):
                    return (qi * 128 + 127) >= 1024 and NSEL > 16

                def imp_front(qi, gsel=None):
                    slot = qi % 2
                    t0 = qi * 128
                    qa, qa_b = qas[slot]
                    if not need_sel(qi):
                        for g in (range(4) if gsel is None else [gsel]):
                            kb.op("pool", [], [qab[slot][g]], lambda e, g=g: e.memset(qa[64:128, g, :], 0.0))
                        return
                    ncol = min(NCP, 8 * (qi + 1))
                    s0 = 128 + 258 - 8 * qi
                    for g in (range(4) if gsel is None else [gsel]):
                        rs4, rs4_b = rss[eti[0] % 2]
                        hets = []
                        for r in range(4):
                            ip, ipb = sc_bank()
                            kb.op("pe", [qa_b, kcT_b], [ipb], lambda e, r=r, g=g: e.matmul(ip[:, 0:ncol], lhsT=qa[0:64, g, r * 128:(r + 1) * 128], rhs=kcT[:, g, 0:ncol], start=True, stop=False))
                            kb.op("pe", [cmsk_b], [ipb], lambda e: e.matmul(ip[:, 0:ncol], lhsT=cmsk[0:10, 0:128], rhs=cmsk[0:10, s0:s0 + ncol], start=False, stop=True), waw=True)
                            et, et_b = ets[eti[0] % NET]
                            eti[0] += 1
                            kb.op("act", [ipb], [et_b, rs4_b], lambda e, r=r: e.activation(out=et[:, 0:ncol], in_=ip[:, 0:ncol], func=AF.Exp, accum_out=rs4[:, r:r + 1]), waw=True)
                            hets.append((et, et_b))
                        kb.op("dve", [rs4_b], [rs4_b], lambda e: e.tensor_scalar(out=rs4[:, 4:8], in0=rs4[:, 0:4], scalar1=1e-30, scalar2=None, op0=ALU.max))
                        kb.op("dve", [rs4_b], [rs4_b], lambda e: e.reciprocal(out=rs4[:, 4:8], in_=rs4[:, 4:8]))
                        if ncol < NCP:
                            kb.op("dve", [], [pacc_b], lambda e: e.memset(pacc[:, ncol:NCP], 0.0))
                        for r in range(4):
                            et, et_b = hets[r]
                            if r == 0:
                                kb.op("dve", [et_b, rs4_b], [pacc_b], lambda e, et=et: e.tensor_scalar(out=pacc[:, 0:ncol], in0=et[:, 0:ncol], scalar1=rs4[:, 4:5], scalar2=None, op0=ALU.mult), waw=True)
                            else:
                                kb.op("dve", [et_b, rs4_b, pacc_b], [pacc_b], lambda e, et=et, r=r: e.scalar_tensor_tensor(out=pacc[:, 0:ncol], in0=et[:, 0:ncol], scalar=rs4[:, 4 + r:5 + r], in1=pacc[:, 0:ncol], op0=ALU.mult, op1=ALU.add))
                        bt, bt_b = bts[g]
                        pv = pacc[:, 0:4 * NSEL].rearrange("p (j i) -> p j i", i=4)
                        kb.op("dve", [pacc_b], [imp_b], lambda e: e.tensor_reduce(out=imp[:, 0:NSEL], in_=pv, axis=AX.X, op=ALU.add))
                        kb.op("dve", [pacc_b, imp_b], [imp_b], lambda e: e.tensor_tensor(out=imp[:, 1:NSEL], in0=imp[:, 1:NSEL], in1=pv[:, 0:NSEL - 1, 3], op=ALU.add))
                        kb.op("dve", [imp_b, Amask_b], [imp_b], lambda e: e.tensor_tensor(out=imp[:, 0:NSEL], in0=imp[:, 0:NSEL], in1=Amask[:, 64 - 2 * qi:64 - 2 * qi + NSEL], op=ALU.add))
                        kb.op("dve", [imp_b], [imp_b], lambda e: e.memset(imp[:, 0:1], 1.0e6))
                        kb.op("dve", [imp_b], [m8_b], lambda e: e.max(out=m8[:, 0:8], in_=imp[:, 0:NSEL]))
                        kb.op("dve", [imp_b, m8_b], [imp2_b], lambda e: e.match_replace(out=imp2[:, 0:NSEL], in_to_replace=m8[:, 0:8], in_values=imp[:, 0:NSEL], imm_value=-2.0e9))
                        kb.op("dve", [imp2_b], [m8_b], lambda e: e.max(out=m8[:, 8:16], in_=imp2[:, 0:NSEL]))
                        kb.op("dve", [imp_b, m8_b], [bt_b], lambda e: e.tensor_scalar(out=bt[:, 64:64 + NSEL], in0=imp[:, 0:NSEL], scalar1=m8[:, 15:16], scalar2=NEG, op0=ALU.is_lt, op1=ALU.mult))

                def imp_back(qi):
                    if not need_sel(qi):
                        return
                    slot = qi % 2
                    qa, qa_b = qas[slot]
                    tp, tpb = PS[7]
                    tpv = tp[:].bitcast(BF16)
                    for g in range(4):
                        bt, bt_b = bts[g]
                        kb.op("pe", [bt_b, cb_b], [tpb], lambda e, g=g: e.transpose(out=tpv[:, g * 128:(g + 1) * 128], in_=bt[:], identity=ident_b), waw=(g > 0))
                    for g in range(4):
                        for r in range(4):
                            kb.op("dve", [tpb], [qab[slot][g]], lambda e, r=r, g=g: e.tensor_copy(out=qa[64:128, g, r * 128:(r + 1) * 128], in_=tpv[64:128, g * 128:(g + 1) * 128]), waw=(r > 0))

                def emit_qk(u):
                    sp_, spb = sc_bank()
                    kb.op("pe", u["rd"], [spb], lambda e: e.matmul(sp_[:], lhsT=u["lhsT"], rhs=u["rhs"], start=True, stop=True))
                    p_, p_b = pts[pti[0] % NPT]
                    pti[0] += 1
                    kb.op("act", [spb], [p_b], lambda e: e.activation(out=p_[:], in_=sp_[:], func=AF.Exp))
                    if u["mask"] is not None:
                        base, cm, step = u["mask"]
                        kb.op("pool", [p_b], [p_b], lambda e: e.affine_select(out=p_[:], in_=p_[:], pattern=[[0, 4], [step, 128]], compare_op=ALU.is_ge, fill=0.0, base=base, channel_multiplier=cm))
                    u["p"] = (p_, p_b)

                def emit_pv(u):
                    p_, p_b = u["p"]
                    o_ps, o_pb = u["o"]
                    for r in range(4):
                        st = u["first"] and r == 0
                        kb.op("pe", [p_b, u["vb"]], [o_pb], lambda e, r=r, st=st: e.matmul(o_ps[:, r * 65:(r + 1) * 65], lhsT=p_[:, r * 128:(r + 1) * 128], rhs=u["v"], start=st, stop=u["last"], skip_group_check=True), waw=not st)
                    if u["last"]:
                        ob, ob_b = ob3s[u["g"] % 2]
                        X_ = u["X"]
                        evq.append([2, lambda: kb.op("act", [o_pb], [ob_b], lambda e: e.activation(out=ob[:, X_, :], in_=o_ps[:, 0:260], func=AF.Copy), waw=True)])

                evq = []

                def evq_tick(force=False):
                    for it in list(evq):
                        it[0] -= 1
                        if it[0] <= 0 or force:
                            it[1]()
                            evq.remove(it)

                def combine(qi, g):
                    evq_tick(force=True)
                    slot = qi % 2
                    sz_, sz_b = szt[slot]
                    y, y_b = ys[qi % 2]
                    ob, ob_b = ob3s[g % 2]
                    gv = gts[:, qi, :].rearrange("p (h x) -> p h x", x=3)[:, 4 * g:4 * g + 4, :].rearrange("p r x -> p x r")
                    ov = ob[:].rearrange("p x (r c) -> p x r c", c=65)
                    kb.op("dve", [ob_b], [cfs_b], lambda e: e.tensor_scalar(out=cfs[:], in0=ov[:, :, :, 64], scalar1=1e-30, scalar2=None, op0=ALU.max))
                    kb.op("dve", [cfs_b], [cfs_b], lambda e: e.reciprocal(out=cfs[:], in_=cfs[:]))
                    kb.op("dve", [cfs_b, gts_b], [cfs_b], lambda e: e.tensor_tensor(out=cfs[:], in0=cfs[:], in1=gv, op=ALU.mult))
                    kb.op("dve", [ob_b, cfs_b], [tmp3_b], lambda e: e.tensor_tensor(out=tmp3[:].rearrange("p x (r d) -> p x r d", r=4), in0=ov[:, :, :, 0:64], in1=cfs[:].unsqueeze(3).broadcast_to([128, 3, 4, 64]), op=ALU.mult))
                    kb.op("dve", [tmp3_b], [acc_b], lambda e: e.tensor_tensor(out=acc[:], in0=tmp3[:, 0, :], in1=tmp3[:, 1, :], op=ALU.add))
                    kb.op("dve", [tmp3_b, acc_b], [acc_b], lambda e: e.tensor_tensor(out=acc[:], in0=acc[:], in1=tmp3[:, 2, :], op=ALU.add))
                    kb.op("dve", [acc_b, sz_b], [y_b], lambda e, g=g: e.tensor_tensor(out=y[:, g * 256:(g + 1) * 256], in0=acc[:], in1=sz_[:, g * 256:(g + 1) * 256], op=ALU.mult), waw=(g > 0))

                def units_for(qi, g):
                    slot = qi % 2
                    t0 = qi * 128
                    qa, qa_b = qas[slot]
                    us = []
                    cts = [ct for ct in range(NCT) if t0 + 127 - 2048 * ct - 31 >= 0]
                    for n_, ct in enumerate(cts):
                        base = t0 - 2048 * ct - 31
                        mk = None if base - 16 * 127 >= 0 else (base, -16, 1)
                        us.append(dict(rd=[qa_b, kcT_b], lhsT=kcT[:, g, ct * 128:(ct + 1) * 128], rhs=qa[0:64, g, :], mask=mk,
                                       v=vcA[:, ct, g, :], vb=vcA_b, o=o_bank(g, 0), X=0, first=(n_ == 0), last=(n_ == len(cts) - 1)))
                    for kt in range(qi + 1):
                        us.append(dict(rd=[qa_b, qab[slot][g], ksA_b], lhsT=ksA[:, g, kt * 128:(kt + 1) * 128], rhs=qa[:, g, :],
                                       mask=((0, -1, 1) if kt == qi else None), v=vsA[:, kt, g, :], vb=vsA_b, o=o_bank(g, 1), X=1, first=(kt == 0), last=(kt == qi)))
                    k0 = max(0, qi - 4)
                    for kt in range(k0, qi + 1):
                        mk = (0, -1, 1) if kt == qi else ((-1, 1, -1) if kt == qi - 4 else None)
                        us.append(dict(rd=[qa_b, kwT_b], lhsT=kwT[:, g, kt * 128:(kt + 1) * 128], rhs=qa[0:64, g, :], mask=mk,
                                       v=vwA[:, kt, g, :], vb=vwA_b, o=o_bank(g, 2), X=2, first=(kt == k0), last=(kt == qi)))
                    for u in us:
                        u["g"] = g
                    return us

                def out_proj(qi):
                    t0 = qi * 128
                    ht, hb = hts[qi % 3]
                    y, y_b = ys[qi % 2]
                    tp, tpb = PS[7]
                    tpv = tp[:].bitcast(BF16)
                    for k in range(8):
                        kb.op("pe", [y_b, cb_b], [tpb], lambda e, k=k: e.transpose(out=tpv[:, k * 128:(k + 1) * 128], in_=y[:, k * 128:(k + 1) * 128], identity=ident_b))
                    kb.op("act", [tpb], [yT_b], lambda e: e.activation(out=yT[:], in_=tpv.rearrange("p (k t) -> p k t", k=8), func=AF.Copy))

                def out_proj2(qi):
                    t0 = qi * 128
                    ht, hb = hts[qi % 3]
                    for half in range(2):
                        pp, ppb = PS[7]
                        for k in range(8):
                            kb.op("pe", [yT_b, wo_b], [ppb], lambda e, k=k, half=half: e.matmul(pp[:], lhsT=yT[:, k, :], rhs=wo[:, k, half * 512:(half + 1) * 512], start=(k == 0), stop=(k == 7)))
                        kb.op("dve", [ppb, hb], [hb], lambda e, half=half: e.tensor_tensor(out=ht[:, half * 512:(half + 1) * 512], in0=pp[:], in1=ht[:, half * 512:(half + 1) * 512], op=ALU.add))
                    if state.get("fuse_final"):
                        fs, fs_b = fns
                        kb.op("act", [hb], [yT_b, fs_b], lambda e: e.activation(out=yT[:].rearrange("p k t -> p (k t)"), in_=ht[:], func=AF.Square, accum_out=fs[:, 0:1]))
                        kb.op("dve", [fs_b], [fs_b], lambda e: e.tensor_scalar(out=fs[:, 1:2], in0=fs[:, 0:1], scalar1=1.0 / D, scalar2=EPS, op0=ALU.mult, op1=ALU.add))
                        kb.op("act", [fs_b], [fs_b], lambda e: e.activation(out=fs[:, 1:2], in_=fs[:, 1:2], func=AF.Sqrt))
                        kb.op("dve", [fs_b], [fs_b], lambda e: e.reciprocal(out=fs[:, 1:2], in_=fs[:, 1:2]))
                        kb.op("dve", [hb, fs_b, norm_b], [hb], lambda e: e.scalar_tensor_tensor(out=ht[:], in0=ht[:], scalar=fs[:, 1:2], in1=norm_t[:], op0=ALU.mult, op1=ALU.mult))
                        ob_ = kb.buf()
                        outs.append(ob_)
                        kb.dma("pool", [hb], [ob_], out_d[t0:t0 + 128, :], ht[:])
                    else:
                        kb.dma("pool", [hb], [h_b[qi]], hD[t0:t0 + 128, :], ht[:])

                if state.get("fuse_final"):
                    fns = kb.sb("oC_fs", [128, 2], F32)
                    kb.dma("sp", [], [norm_b], norm_t[:], final_norm[:, :])
                LOOK = 2
                load_tile(0)
                imp_front(0)
                imp_back(0)
                deferred = []
                for qi in range(NT):
                    if qi + 1 < NT:
                        load_tile(qi + 1)
                    todo = deferred
                    deferred = []
                    if qi + 1 < NT:
                        for g_ in range(4):
                            todo.append((6 + 8 * g_, lambda qi=qi, g_=g_: imp_front(qi + 1, g_)))
                    units = []
                    for g in range(4):
                        units += units_for(qi, g)
                    n = len(units)
                    if qi + 1 < NT:
                        todo.append((max(62, n - 40), lambda qi=qi: imp_back(qi + 1)))
                    for i in range(n + LOOK):
                        if i < n:
                            emit_qk(units[i])
                        evq_tick()
                        if i >= LOOK:
                            u = units[i - LOOK]
                            emit_pv(u)
                            if i - LOOK + 1 == n or units[i - LOOK + 1]["g"] != u["g"]:
                                combine(qi, u["g"])
                        for (k_, fn) in todo:
                            if k_ == i:
                                fn()
                    for (k_, fn) in todo:
                        if k_ >= n + LOOK:
                            fn()
                    deferred.append((40, lambda qi=qi: out_proj(qi)))
                    deferred.append((46, lambda qi=qi: out_proj2(qi)))
                for (k_, fn) in deferred:
                    fn()
                kb.es = old
            state["src"] = hD

        def final_norm_phase():
            src = state["src"]
            fn_t, fn_b = kb.sb("fn_t", [128, D], F32)
            kb.dma("sp", [], [fn_b], fn_t[:], final_norm[:, :])
            hts = [kb.sb("fn_h%d" % i, [128, D], F32) for i in range(2)]
            fj, fj_b = kb.sb("fn_j", [128, D], F32)
            fs, fs_b = kb.sb("fn_s", [128, 2], F32)
            for ti in range(NT):
                ht, hb = hts[ti % 2]
                kb.dma("sp", [h_b[ti]], [hb], ht[:], src[ti * 128:(ti + 1) * 128, :])
                kb.op("act", [hb], [fj_b, fs_b], lambda e: e.activation(out=fj[:], in_=ht[:], func=AF.Square, accum_out=fs[:, 0:1]))
                kb.op("dve", [fs_b], [fs_b], lambda e: e.tensor_scalar(out=fs[:, 1:2], in0=fs[:, 0:1], scalar1=1.0 / D, scalar2=EPS, op0=ALU.mult, op1=ALU.add))
                kb.op("act", [fs_b], [fs_b], lambda e: e.activation(out=fs[:, 1:2], in_=fs[:, 1:2], func=AF.Sqrt))
                kb.op("dve", [fs_b], [fs_b], lambda e: e.reciprocal(out=fs[:, 1:2], in_=fs[:, 1:2]))
                kb.op("dve", [hb, fs_b, fn_b], [hb], lambda e: e.scalar_tensor_tensor(out=ht[:], in0=ht[:], scalar=fs[:, 1:2], in1=fn_t[:], op0=ALU.mult, op1=ALU.mult))
                ob = kb.buf()
                outs.append(ob)
                kb.dma("pool", [hb], [ob], out_d[ti * 128:(ti + 1) * 128, :], ht[:])

        def cast_weight_dma(src_ap, dst_ap, rows, nm):
            wdram_b[nm] = kb.buf(nm)
            for r in range(0, rows, 128):
                kb.dma("pool", [], [wdram_b[nm]], dst_ap[r:r + 128, :], src_ap[r:r + 128, :], waw=True, max_dma_last_dim=4096)

        def cast_layer(kind, li):
            if kind == "e":
                cast_weight_dma(ew["w_in"][li], ew["w_in_bf"][li], D, "cwi%d" % li)
                cast_weight_dma(ew["w_out"][li], ew["w_out_bf"][li], 2048, "cwo%d" % li)
            else:
                cast_weight_dma(ow["w_in"][li], ow["w_in_bf"][li], D, "owi%d" % li)
                cast_weight_dma(ow["w_out"][li], ow["w_out_bf"][li], D, "owo%d" % li)

        cast_layer(*layers[0])
        for idx, (kind, li) in enumerate(layers):
            if idx + 1 < len(layers):
                cast_layer(*layers[idx + 1])
            with ExitStack() as es2:
                old = kb.es
                kb.es = es2
                state["fuse_final"] = (kind == "o" and idx == len(layers) - 1 and not debug_h)
                if kind == "e":
                    even_layer(li)
                else:
                    odd_layer(li)
                kb.es = old
            kb.barrier()
        if debug_h:
            hts = [kb.sb("dbg_h%d" % i, [128, D], F32) for i in range(2)]
            for ti in range(NT):
                ht, hb = hts[ti % 2]
                kb.dma("sp", [h_b[ti]], [hb], ht[:], state["src"][ti * 128:(ti + 1) * 128, :])
                ob = kb.buf()
                outs.append(ob)
                kb.dma("pool", [hb], [ob], out_d[ti * 128:(ti + 1) * 128, :], ht[:])
        elif not state.get("fuse_final"):
            with ExitStack() as es2:
                old = kb.es
                kb.es = es2
                final_norm_phase()
                kb.es = old
        kb.finish(outs)
    return nc


def make_consts():
    k = np.arange(128)
    ident = np.eye(128, dtype=np.float32)
    L = (k[:, None] <= k[None, :]).astype(np.float32)
    U = (k[:, None] > k[None, :]).astype(np.float32)
    tri = (k[None, :] >= k[:, None]).astype(np.float32)
    ones = np.ones((128, 128), np.float32)
    cf = np.concatenate([ident, L, U, tri, ones], axis=1)
    import ml_dtypes
    cb = np.concatenate([ident, L], axis=1).astype(ml_dtypes.bfloat16)
    return cf, cb


def bc(v):
    v = np.asarray(v, np.float32)
    return np.ascontiguousarray(np.broadcast_to(v[:, None], (v.shape[0], 128) + v.shape[1:]))


def host_inputs(inp, layers, S=4096):
    cf, cb = make_consts()
    m = {"consts_f32": cf, "consts_bf16": cb,
         "final_norm": np.ascontiguousarray(np.broadcast_to(np.asarray(inp["final_norm"], np.float32), (128, D)))}
    if any(l[0] == "e" for l in layers):
        m["e_norm"] = bc(inp["e_norm"])
        m["e_w_in"] = np.ascontiguousarray(inp["e_w_in"], dtype=np.float32)
        m["e_ssd_conv_w"] = np.ascontiguousarray(np.asarray(inp["e_ssd_conv_w"], np.float32).reshape(2, 4, 12, 128).transpose(0, 3, 2, 1))
        m["e_ssd_conv_b"] = np.ascontiguousarray(np.asarray(inp["e_ssd_conv_b"], np.float32).reshape(2, 12, 128).transpose(0, 2, 1))
        m["e_dt_bias"] = bc(inp["e_dt_bias"])
        m["e_a_log"] = bc(inp["e_a_log"])
        m["e_d_skip"] = bc(inp["e_d_skip"])
        m["e_ssd_norm"] = bc(inp["e_ssd_norm"])
        m["e_conf_conv_w"] = np.ascontiguousarray(np.asarray(inp["e_conf_conv_w"], np.float32).reshape(2, 31, 8, 128).transpose(0, 3, 2, 1))
        for nm in ("e_conf_conv_b", "e_conf_ln_g", "e_conf_ln_b"):
            m[nm] = np.ascontiguousarray(np.asarray(inp[nm], np.float32).reshape(2, 8, 128).transpose(0, 2, 1))
        m["e_w_out"] = np.ascontiguousarray(inp["e_w_out"], dtype=np.float32)
    if any(l[0] == "o" for l in layers):
        import ml_dtypes
        m["o_norm"] = bc(inp["o_norm"])
        m["o_w_in"] = np.ascontiguousarray(inp["o_w_in"], dtype=np.float32)
        m["o_gate_bias"] = bc(inp["o_gate_bias"])
        for nm in ("k", "v"):
            m["o_peT_" + nm] = np.ascontiguousarray(np.asarray(inp["o_cmp_pe_" + nm], np.float32).transpose(0, 2, 1))
            m["o_cmp_w1_" + nm] = np.ascontiguousarray(inp["o_cmp_w1_" + nm], dtype=np.float32)
            m["o_cmp_w2_" + nm] = np.ascontiguousarray(inp["o_cmp_w2_" + nm], dtype=np.float32)
        m["o_w_out"] = np.ascontiguousarray(inp["o_w_out"], dtype=np.float32)
        p = np.arange(128)[:, None]
        xx = np.arange(128)[None, :] - 64
        off = (p >= 64).astype(np.int64)
        A = np.zeros((128, 128), np.float32)
        A[(xx == off) | (xx == off - 1)] = 1.0e6
        A[xx > off] = -1.0e9
        m["o_amask"] = A
        E = (np.arange(64)[:, None] == (np.arange(S)[None, :] // 64)).astype(np.float32)
        m["o_emat"] = E.astype(ml_dtypes.bfloat16)
        jj = np.arange(10)[:, None]
        Mq = np.where(jj > (np.arange(128)[None, :] + 1) // 16, NEG, 0.0)
        Bd = (np.arange(288)[None, :] == jj + 256).astype(np.float32)
        m["o_cmsk"] = np.concatenate([Mq, Bd], axis=1).astype(ml_dtypes.bfloat16)
    return m


LAYERS = [("e", 0), ("o", 0), ("e", 1), ("o", 1)]


def kernel(**inputs):
    x = np.asarray(inputs["x"], np.float32)
    B, S, _ = x.shape
    nc = build_program(S, LAYERS)
    shared = host_inputs(inputs, LAYERS, S)
    in_maps = []
    for b in range(B):
        mm = dict(shared)
        mm["x"] = np.ascontiguousarray(x[b])
        in_maps.append(mm)
    res = run_bass_kernel_spmd(nc, in_maps, core_ids=list(range(B)))
    return np.stack([np.asarray(r["out"], np.float32) for r in res.results], axis=0)
```

```python
import numpy as np
from contextlib import ExitStack
import concourse.bass as bass
import concourse.mybir as mybir
from concourse.bass_utils import run_bass_kernel_spmd

F32 = mybir.dt.float32
BF16 = mybir.dt.bfloat16
AF = mybir.ActivationFunctionType
ALU = mybir.AluOpType
AX = mybir.AxisListType

D = 1024
E_IN = 5648
O_IN = 3632
EPS = 1e-6
NEG = -30000.0
ODD_STOP = None
ODD_DBG = 3


class Buf:
    __slots__ = ("name", "w", "r", "psum")

    def __init__(self, name, psum=False):
        self.name = name
        self.psum = psum
        self.w = {}
        self.r = {}


class KB:
    def __init__(self, nc, es, n_dsem=90):
        self.nc = nc
        self.es = es
        self.engs = {"pe": nc.tensor, "act": nc.scalar, "dve": nc.vector, "pool": nc.gpsimd, "sp": nc.sync}
        self.esem = {e: es.enter_context(nc.semaphore("s_" + e)) for e in self.engs}
        self.ecnt = {e: 0 for e in self.engs}
        self.dsem = [es.enter_context(nc.semaphore("d%d" % i)) for i in range(n_dsem)]
        self.dcnt = [0] * n_dsem
        self.dnext = 0
        self.seen = {e: {} for e in self.engs}
        self.nbuf = 0

    def buf(self, name=None):
        self.nbuf += 1
        return Buf(name or ("b%d" % self.nbuf))

    def sb(self, name, shape, dt):
        self.nbuf += 1
        name = "%s_u%d" % (name, self.nbuf)
        t = self.es.enter_context(self.nc.sbuf_tensor(name, list(shape), dt))
        return t, Buf(name)

    def _wait(self, e, key, val):
        if self.seen[e].get(key, 0) >= val:
            return
        sem = self.esem[key] if isinstance(key, str) else self.dsem[key]
        self.engs[e].wait_ge(sem, val)
        self.seen[e][key] = val

    def _deps(self, e, reads, writes, waw):
        for b in reads:
            for k, v in b.w.items():
                self._wait(e, k, v)
            if b.psum:
                for k, v in b.r.items():
                    if k != e:
                        self._wait(e, k, v)
        for b in writes:
            for k, v in b.w.items():
                if k == e or waw:
                    continue
                self._wait(e, k, v)
            for k, v in b.r.items():
                if k == e:
                    continue
                self._wait(e, k, v)

    def op(self, e, reads, writes, fn, waw=False):
        self._deps(e, reads, writes, waw)
        ins = fn(self.engs[e])
        self.ecnt[e] += 1
        c = self.ecnt[e]
        ins.then_inc(self.esem[e], 1)
        for b in reads:
            b.r[e] = c
        for b in writes:
            if waw:
                b.w[e] = c
            else:
                b.w = {e: c}
                b.r = {}
        return ins

    def dma(self, e, reads, writes, out, in_, waw=False, **kw):
        if e == "pool":
            self.pool_q = getattr(self, "pool_q", [])
            if len(self.pool_q) >= 6:
                k_, v_ = self.pool_q.pop(0)
                self._wait(e, k_, v_)
        i = self.dnext
        self.dnext = (i + 1) % len(self.dsem)
        if self.dcnt[i] > 0:
            self._wait(e, i, self.dcnt[i])
        self._deps(e, reads, writes, waw)
        self.dcnt[i] += 16
        v = self.dcnt[i]
        self.engs[e].dma_start(out=out, in_=in_, **kw).then_inc(self.dsem[i], 16)
        if e == "pool":
            self.pool_q.append((i, v))
        for b in reads:
            b.r[i] = v
        for b in writes:
            if waw:
                b.w[i] = v
            else:
                b.w = {i: v}
                b.r = {}

    def barrier(self):
        for e in self.engs:
            for e2 in self.engs:
                if e2 != e and self.ecnt[e2] > 0:
                    self._wait(e, e2, self.ecnt[e2])
            for i, v in enumerate(self.dcnt):
                if v > 0:
                    self._wait(e, i, v)

    def finish(self, bufs):
        for b in bufs:
            for k, v in b.w.items():
                self._wait("sp", k, v)


def build_program(S, layers, debug_h=False):
    nc = bass.Bass("TRN2", target_bir_lowering=False)
    NT = S // 128
    NB = S // 512

    def din(name, shape, dt=F32):
        return nc.dram_tensor(name, list(shape), dt, kind="ExternalInput").ap()

    x_in = din("x", [S, D])
    out_d = nc.dram_tensor("out", [S, D], F32, kind="ExternalOutput").ap()
    hD = nc.dram_tensor("h_scr", [S, D], F32, kind="Internal").ap()
    if debug_h:
        dbg_d = nc.dram_tensor("dbg", [128, 256], F32, kind="ExternalOutput").ap()
    cst = din("consts_f32", [128, 5 * 128])
    cstb = din("consts_bf16", [128, 2 * 128], BF16)
    final_norm = din("final_norm", [128, D])

    n_even = sum(1 for l in layers if l[0] == "e")
    n_odd = sum(1 for l in layers if l[0] == "o")
    ew = {}
    if n_even:
        ew = dict(
            norm=din("e_norm", [2, 128, D]), w_in=din("e_w_in", [2, D, E_IN]),
            conv_w=din("e_ssd_conv_w", [2, 128, 12, 4]), conv_b=din("e_ssd_conv_b", [2, 128, 12]),
            dt_bias=din("e_dt_bias", [2, 128, 16]), a_log=din("e_a_log", [2, 128, 16]),
            d_skip=din("e_d_skip", [2, 128, 16]), ssd_norm=din("e_ssd_norm", [2, 128, D]),
            cconv_w=din("e_conf_conv_w", [2, 128, 8, 31]), cconv_b=din("e_conf_conv_b", [2, 128, 8]),
            ln_g=din("e_conf_ln_g", [2, 128, 8]), ln_b=din("e_conf_ln_b", [2, 128, 8]),
            w_out=din("e_w_out", [2, 2048, D]),
        )
        ew["w_in_bf"] = nc.dram_tensor("e_w_in_bf", [2, D, E_IN], BF16, kind="Internal").ap()
        ew["w_out_bf"] = nc.dram_tensor("e_w_out_bf", [2, 2048, D], BF16, kind="Internal").ap()

    ow = {}
    if n_odd:
        ow = dict(
            norm=din("o_norm", [2, 128, D]), w_in=din("o_w_in", [2, D, O_IN]), gate_bias=din("o_gate_bias", [2, 128, 48]),
            peT_k=din("o_peT_k", [2, 64, 32]), w1_k=din("o_cmp_w1_k", [2, 2048, 256]), w2_k=din("o_cmp_w2_k", [2, 256, 64]),
            peT_v=din("o_peT_v", [2, 64, 32]), w1_v=din("o_cmp_w1_v", [2, 2048, 256]), w2_v=din("o_cmp_w2_v", [2, 256, 64]),
            w_out=din("o_w_out", [2, D, D]), amask=din("o_amask", [128, 128]), emat=din("o_emat", [64, S], BF16), cmsk=din("o_cmsk", [10, 128 + 288], BF16),
        )
        ow["w_in_bf"] = nc.dram_tensor("o_w_in_bf", [2, D, O_IN], BF16, kind="Internal").ap()
        ow["w_out_bf"] = nc.dram_tensor("o_w_out_bf", [2, D, D], BF16, kind="Internal").ap()
        ow["qT_D"] = nc.dram_tensor("o_qT_D", [1024, S], BF16, kind="Internal").ap()
        ow["kT_D"] = nc.dram_tensor("o_kT_D", [4, 256, S], BF16, kind="Internal").ap()
        ow["sz_D"] = nc.dram_tensor("o_sz_D", [S, D], BF16, kind="Internal").ap()

    with ExitStack() as es:
        kb = KB(nc, es)
        cf, cf_b = kb.sb("cf", [128, 5 * 128], F32)
        cb, cb_b = kb.sb("cb", [128, 2 * 128], BF16)
        kb.dma("sp", [], [cf_b], cf[:], cst[:, :])
        kb.dma("sp", [], [cb_b], cb[:], cstb[:, :])
        ident_f = cf[:, 0:128]
        L_f = cf[:, 128:256]
        U_f = cf[:, 256:384]
        tri_f = cf[:, 384:512]
        ones_f = cf[:, 512:640]
        ident_b = cb[:, 0:128]

        PS = []
        for i in range(8):
            t = es.enter_context(nc.psum_tensor("ps%d" % i, [128, 512], F32))
            PS.append((t, Buf("ps%d" % i, psum=True)))

        h_b = [kb.buf("hD%d" % i) for i in range(NT)]
        outs = []

        state = {"src": x_in}

        WB = [kb.sb("wb%d" % i, [128, 8, 512], BF16) for i in range(3)]
        wb_i = [0]

        wdram_b = {}

        def load_w(w_bf_ap, r0, c0, ncols, wkey=None):
            t, b = WB[wb_i[0] % len(WB)]
            wb_i[0] += 1
            src = w_bf_ap[r0:r0 + 1024, c0:c0 + ncols].rearrange("(k p) c -> p k c", p=128)
            kb.dma("sp", [wdram_b[wkey]] if wkey else [], [b], t[:, :, 0:ncols], src)
            return t, b

        def cast_weight(src_ap, dst_ap, rows, cols, nm):
            wdram_b[nm] = kb.buf(nm)
            stg = [kb.sb("%s_s%d" % (nm, i), [128, 2048], F32) for i in range(2)]
            stb = [kb.sb("%s_b%d" % (nm, i), [128, 2048], BF16) for i in range(2)]
            i = 0
            for r in range(0, rows, 128):
                for c in range(0, cols, 2048):
                    w = min(2048, cols - c)
                    s, sb_ = stg[i % 2]
                    d, db_ = stb[i % 2]
                    kb.dma("sp", [], [sb_], s[:, 0:w], src_ap[r:r + 128, c:c + w])
                    kb.op("pool", [sb_], [db_], lambda e: e.tensor_copy(out=d[:, 0:w], in_=s[:, 0:w]))
                    kb.dma("pool", [db_], [wdram_b[nm]], dst_ap[r:r + 128, c:c + w], d[:, 0:w], waw=True)
                    i += 1

        def even_layer(li):
            src = state["src"]
            norm_t, norm_b = kb.sb("e_norm_t", [128, D], F32)
            ssdn_t, ssdn_b = kb.sb("e_ssdn_t", [128, D], F32)
            small, small_b = kb.sb("e_small", [128, 64], F32)
            cw, cw_b = kb.sb("e_cw", [128, 12, 4], F32)
            cbias, cbias_b = kb.sb("e_cb", [128, 12], F32)
            ccw, ccw_b = kb.sb("e_ccw", [128, 8, 31], F32)
            cc4, cc4_b = kb.sb("e_cc4", [128, 3, 8], F32)
            kb.dma("sp", [], [norm_b], norm_t[:], ew["norm"][li])
            kb.dma("sp", [], [ssdn_b], ssdn_t[:], ew["ssd_norm"][li])
            kb.dma("sp", [], [small_b], small[:, 0:16], ew["dt_bias"][li])
            kb.dma("sp", [], [small_b], small[:, 16:32], ew["a_log"][li], waw=True)
            kb.dma("sp", [], [small_b], small[:, 32:48], ew["d_skip"][li], waw=True)
            kb.dma("sp", [], [cw_b], cw[:], ew["conv_w"][li])
            kb.dma("sp", [], [cbias_b], cbias[:], ew["conv_b"][li])
            kb.dma("sp", [], [ccw_b], ccw[:], ew["cconv_w"][li])
            kb.dma("sp", [], [cc4_b], cc4[:, 0, :], ew["cconv_b"][li])
            kb.dma("sp", [], [cc4_b], cc4[:, 1, :], ew["ln_g"][li], waw=True)
            kb.dma("sp", [], [cc4_b], cc4[:, 2, :], ew["ln_b"][li], waw=True)
            kb.op("act", [small_b], [small_b], lambda e: e.activation(out=small[:, 16:32], in_=small[:, 16:32], func=AF.Exp))
            kb.op("dve", [small_b], [small_b], lambda e: e.tensor_scalar(out=small[:, 16:32], in0=small[:, 16:32], scalar1=-1.0, scalar2=None, op0=ALU.mult))
            dtb = small[:, 0:16]
            a_t = small[:, 16:32]
            dsk = small[:, 32:48]
            w_in_bf = ew["w_in_bf"][li]
            w_out_bf = ew["w_out_bf"][li]

            wdt, wdt_b = kb.sb("e_wdt", [128, 8, 16], BF16)
            kb.dma("sp", [wdram_b["cwi%d" % li]], [wdt_b], wdt[:], w_in_bf[0:1024, 2560:2576].rearrange("(k p) c -> p k c", p=128))
            hblk, hblk_b = kb.sb("e_hblk", [128, 4, D], F32)
            hn, hn_b = kb.sb("e_hn", [128, D], BF16)
            st4, st4_b = kb.sb("e_st4", [128, 8], F32)
            hnT, hnT_b = kb.sb("e_hnT", [128, 8, 512], BF16)
            sz, sz_b = kb.sb("e_sz", [128, 4, D], BF16)
            xh, xh_b = kb.sb("e_xh", [128, 12, 516], BF16)
            xh_bs = [kb.buf("xh%d" % j) for j in range(12)]
            cacc, cacc_b = kb.sb("e_cacc", [128, 512], F32)
            caccp, caccp_b = kb.sb("e_caccp", [128, 512], F32)
            uats = [kb.sb("e_uat%d" % i, [128, 512], F32) for i in range(2)]
            xa, xa_b = kb.sb("e_xa", [128, 12, 512], BF16)
            xa_bs = [kb.buf("xa%d" % j) for j in range(12)]
            dt_t, dt_b = kb.sb("e_dt", [128, 4, 16], F32)
            dta_t, dta_b = kb.sb("e_dta", [128, 4, 16], F32)
            sp1, sp1_b = kb.sb("e_sp1", [128, 64], F32)
            sp2, sp2_b = kb.sb("e_sp2", [128, 64], F32)
            xt, xt_b = kb.sb("e_xt", [128, D], BF16)
            Bt, Bt_b = kb.sb("e_Bt", [128, 2, 128], BF16)
            cums, cums_b = kb.sb("e_cums", [128, 64], F32)
            CBm, CBm_b = kb.sb("e_CBm", [128, 2, 128], F32)
            lh = [kb.sb("e_lh%d" % i, [128, 128], F32) for i in range(8)]
            dec = [kb.sb("e_dec%d" % i, [128, 512], F32) for i in range(2)]
            wT = [kb.sb("e_wT%d" % i, [128, 128], BF16) for i in range(4)]
            y1, y1_b = kb.sb("e_y1", [128, D], F32)
            y2, y2_b = kb.sb("e_y2", [128, D], F32)
            junk, junk_b = y2, y2_b
            xw, xw_b = kb.sb("e_xw", [128, D], BF16)
            st32, st32_b = kb.sb("e_st32", [128, 2, 512], F32)
            stbf, stbf_b = kb.sb("e_stbf", [128, 2, 512], BF16)
            ya, ya_b = kb.sb("e_ya", [128, D], BF16)
            yaT, yaT_b = kb.sb("e_yaT", [128, 8, 512], BF16)
            ybT, ybT_b = kb.sb("e_ybT", [128, 8, 512], BF16)
            uh, uh_b = kb.sb("e_uh", [128, 8, 542], BF16)
            dgs = [kb.sb("e_dg%d" % i, [128, 31, 128], BF16) for i in range(2)]
            uh_bs = [kb.buf("uh%d" % j) for j in range(8)]
            uc, uc_b = kb.sb("e_uc", [128, 8, 512], F32)
            uc_bs = [kb.buf("uc%d" % j) for j in range(8)]
            sg, sg_b = kb.sb("e_sg", [128, 512], F32)
            sq, sq_b = kb.sb("e_sq", [128, 512], F32)
            lnm, lnm_b = kb.sb("e_lnm", [128, 512], F32)
            lnr, lnr_b = kb.sb("e_lnr", [128, 512], F32)
            un, un_b = kb.sb("e_un", [128, 512], F32)
            s1, s1_b = kb.sb("e_s1", [128, 512], F32)
            szc, szc_b = kb.sb("e_szc", [128, 512], F32)

            kb.op("pool", [], [st32_b], lambda e: e.memset(st32[:], 0.0))
            kb.op("pool", [], [stbf_b], lambda e: e.memset(stbf[:], 0.0))
            for j in range(12):
                kb.op("pool", [], [xh_bs[j]], lambda e, j=j: e.memset(xh[:, j, 0:3], 0.0))
            for j in range(8):
                kb.op("pool", [], [uh_bs[j]], lambda e, j=j: e.memset(uh[:, j, 0:30], 0.0))

            def rstd_from_ss(ss_ap, n, out_ap, bufs):
                kb.op("dve", bufs, bufs, lambda e: e.tensor_scalar(out=out_ap, in0=ss_ap, scalar1=1.0 / n, scalar2=EPS, op0=ALU.mult, op1=ALU.add))
                kb.op("act", bufs, bufs, lambda e: e.activation(out=out_ap, in_=out_ap, func=AF.Sqrt))
                kb.op("dve", bufs, bufs, lambda e: e.reciprocal(out=out_ap, in_=out_ap))

            def proj_fm(wt, wb_, col0, pt, pb_):
                for k in range(8):
                    kb.op("pe", [wb_, hnT_b], [pb_], lambda e, k=k: e.matmul(pt[:], lhsT=wt[:, k, col0:col0 + 128], rhs=hnT[:, k, :], start=(k == 0), stop=(k == 7)))

            for blk in range(NB):
                t0 = blk * 512
                def emit_diag(j, kks=range(31)):
                    dgt, dgb = dgs[j % 2]
                    for kk in kks:
                        kb.op("pool", [cf_b, ccw_b], [dgb], lambda e, kk=kk: e.tensor_scalar(out=dgt[:, kk, :], in0=ident_f, scalar1=ccw[:, j, kk:kk + 1], scalar2=0.0, op0=ALU.mult, op1=ALU.add), waw=True)

                emit_diag(0)
                for tt in range(4):
                    ti = blk * 4 + tt
                    kb.dma("sp", [h_b[ti]], [hblk_b], hblk[:, tt, :], src[t0 + tt * 128:t0 + (tt + 1) * 128, :], waw=(tt > 0))
                for tt in range(4):
                    kb.op("act", [hblk_b], [junk_b, st4_b], lambda e, tt=tt: e.activation(out=junk[:], in_=hblk[:, tt, :], func=AF.Square, accum_out=st4[:, tt:tt + 1]))
                rstd_from_ss(st4[:, 0:4], float(D), st4[:, 4:8], [st4_b])
                for tt in range(4):
                    kb.op("dve", [hblk_b, st4_b, norm_b], [hn_b], lambda e, tt=tt: e.scalar_tensor_tensor(out=hn[:], in0=hblk[:, tt, :], scalar=st4[:, 4 + tt:5 + tt], in1=norm_t[:], op0=ALU.mult, op1=ALU.mult))
                    pt, pb_ = PS[2]
                    ptb = pt[:].bitcast(BF16)
                    for k in range(8):
                        kb.op("pe", [hn_b, cb_b], [pb_], lambda e, k=k: e.transpose(out=ptb[:, k * 128:(k + 1) * 128], in_=hn[:, k * 128:(k + 1) * 128], identity=ident_b))
                    kb.op("act", [pb_], [hnT_b], lambda e, tt=tt: e.activation(out=hnT[:, :, tt * 128:(tt + 1) * 128], in_=ptb.rearrange("p (k t) -> p k t", k=8), func=AF.Copy), waw=(tt > 0))

                pi = 0
                for half in range(2):
                    wt, wb_ = load_w(w_in_bf, 0, half * 512, 512, wkey="cwi%d" % li)
                    for tt in range(4):
                        pt, pb_ = PS[(0, 1, 4, 5)[pi % 4]]
                        pi += 1
                        for k in range(8):
                            kb.op("pe", [wb_, hnT_b], [pb_], lambda e, k=k, tt=tt: e.matmul(pt[:], lhsT=hnT[:, k, tt * 128:(tt + 1) * 128], rhs=wt[:, k, :], start=(k == 0), stop=(k == 7)))
                        kb.op("act", [pb_], [sz_b], lambda e, tt=tt, half=half: e.activation(out=sz[:, tt, half * 512:(half + 1) * 512], in_=pt[:], func=AF.Silu), waw=True)

                pend = []
                dacc = [(cacc, cacc_b), (lnm, lnm_b), (lnr, lnr_b)]
                dai = 0
                for sl in range(3):
                    wt, wb_ = load_w(w_in_bf, 0, 1024 + sl * 512, 512, wkey="cwi%d" % li)
                    for jj in range(4):
                        j = sl * 4 + jj
                        pt, pb_ = PS[(0, 1, 4, 5)[pi % 4]]
                        pi += 1
                        proj_fm(wt, wb_, jj * 128, pt, pb_)
                        kb.op("act", [pb_], [xh_bs[j]], lambda e, j=j: e.activation(out=xh[:, j, 3:515], in_=pt[:], func=AF.Copy))
                        if j % 3 == 0:
                            ca, ca_b = caccp, caccp_b
                            t2, t2_b = un, un_b
                            kb.op("pool", [xh_bs[j], cw_b], [ca_b], lambda e, j=j: e.tensor_scalar(out=ca[:], in0=xh[:, j, 0:512], scalar1=cw[:, j, 0:1], scalar2=0.0, op0=ALU.mult, op1=ALU.add))
                            for kk in range(1, 4):
                                kb.op("pool", [xh_bs[j], cw_b], [t2_b], lambda e, j=j, kk=kk: e.tensor_scalar(out=t2[:], in0=xh[:, j, kk:kk + 512], scalar1=cw[:, j, kk:kk + 1], scalar2=0.0, op0=ALU.mult, op1=ALU.add))
                                kb.op("pool", [t2_b, ca_b], [ca_b], lambda e: e.tensor_tensor(out=ca[:], in0=ca[:], in1=t2[:], op=ALU.add))
                            delay = 2
                        else:
                            ca, ca_b = dacc[dai % 3]
                            dai += 1
                            kb.op("dve", [xh_bs[j], cw_b], [ca_b], lambda e, j=j, ca=ca: e.tensor_scalar(out=ca[:], in0=xh[:, j, 0:512], scalar1=cw[:, j, 0:1], scalar2=None, op0=ALU.mult))
                            for kk in range(1, 4):
                                kb.op("dve", [xh_bs[j], cw_b, ca_b], [ca_b], lambda e, j=j, kk=kk, ca=ca: e.scalar_tensor_tensor(out=ca[:], in0=xh[:, j, kk:kk + 512], scalar=cw[:, j, kk:kk + 1], in1=ca[:], op0=ALU.mult, op1=ALU.add))
                            delay = 1

                        def fin(j=j, ca=ca, ca_b=ca_b):
                            kb.op("act", [ca_b, cbias_b], [xa_bs[j]], lambda e: e.activation(out=xa[:, j, :], in_=ca[:], func=AF.Silu, bias=cbias[:, j:j + 1]))
                            kb.op("pool", [xh_bs[j]], [xh_bs[j]], lambda e: e.tensor_copy(out=xh[:, j, 0:3], in_=xh[:, j, 512:515]))
                        pend.append((j + delay, fin))
                        for it in list(pend):
                            if it[0] <= j:
                                it[1]()
                                pend.remove(it)
                for it in pend:
                    it[1]()

                wt, wb_ = wdt, wdt_b
                pt, pb_ = PS[pi % 2]
                pi += 1
                for tt in range(4):
                    for k in range(8):
                        kb.op("pe", [wb_, hnT_b], [pb_], lambda e, k=k, tt=tt: e.matmul(pt[:, tt * 16:(tt + 1) * 16], lhsT=hnT[:, k, tt * 128:(tt + 1) * 128], rhs=wt[:, k, 0:16], start=(k == 0), stop=(k == 7)), waw=(tt > 0 or k > 0))
                kb.op("act", [pb_], [sp1_b], lambda e: e.activation(out=sp1[:], in_=pt[:, 0:64], func=AF.Copy))
                kb.op("dve", [sp1_b, small_b], [sp1_b], lambda e: e.tensor_tensor(out=sp1[:].rearrange("p (t h) -> p t h", t=4), in0=sp1[:].rearrange("p (t h) -> p t h", t=4), in1=dtb.unsqueeze(1).broadcast_to([128, 4, 16]), op=ALU.add))
                kb.op("act", [sp1_b], [sp2_b], lambda e: e.activation(out=sp2[:], in_=sp1[:], func=AF.Abs))
                kb.op("act", [sp2_b], [sp2_b], lambda e: e.activation(out=sp2[:], in_=sp2[:], func=AF.Exp, scale=-1.0))
                kb.op("act", [sp2_b], [sp2_b], lambda e: e.activation(out=sp2[:], in_=sp2[:], func=AF.Ln, bias=1.0))
                kb.op("dve", [sp1_b, sp2_b], [dt_b], lambda e: e.scalar_tensor_tensor(out=dt_t[:].rearrange("p t h -> p (t h)"), in0=sp1[:], scalar=0.0, in1=sp2[:], op0=ALU.max, op1=ALU.add))
                kb.op("dve", [dt_b, small_b], [dta_b], lambda e: e.tensor_tensor(out=dta_t[:], in0=dt_t[:], in1=a_t.unsqueeze(1).broadcast_to([128, 4, 16]), op=ALU.mult))

                for sl in range(2):
                    wa, wab = load_w(w_in_bf, 0, 2576 + sl * 512, 512, wkey="cwi%d" % li)
                    wu, wub = load_w(w_in_bf, 0, 3600 + sl * 512, 512, wkey="cwi%d" % li)
                    for jj in range(4):
                        j = sl * 4 + jj
                        pa, pab = PS[(0, 4)[j % 2]]
                        pu, pub = PS[(1, 5)[j % 2]]
                        proj_fm(wa, wab, jj * 128, pa, pab)
                        proj_fm(wu, wub, jj * 128, pu, pub)
                        sgt, sgtb = (sg, sg_b) if j % 2 == 0 else (szc, szc_b)
                        kb.op("act", [pub], [sgtb], lambda e: e.activation(out=sgt[:], in_=pu[:], func=AF.Sigmoid))
                        uat, uatb = uats[j % 2]
                        kb.op("act", [pab], [uatb], lambda e: e.activation(out=uat[:], in_=pa[:], func=AF.Copy))
                        kb.op("dve", [uatb, sgtb], [uh_bs[j]], lambda e, j=j: e.tensor_tensor(out=uh[:, j, 30:542], in0=uat[:], in1=sgt[:], op=ALU.mult))

                def emit_conv_mm(j, kk):
                    dgt, dgb = dgs[j % 2]
                    cp, cpb = PS[j % 2]
                    if j + 1 < 8:
                        emit_diag(j + 1, [kk])
                    kb.op("pe", [dgb, uh_bs[j]], [cpb], lambda e: e.matmul(cp[:], lhsT=dgt[:, kk, :], rhs=uh[:, j, kk:kk + 512], start=(kk == 0), stop=(kk == 30)))
                    if kk == 30:
                        kb.op("act", [cpb, cc4_b], [uc_bs[j]], lambda e: e.activation(out=uc[:, j, :], in_=cp[:], func=AF.Identity, bias=cc4[:, 0, j:j + 1]))
                        kb.op("dve", [uh_bs[j]], [uh_bs[j]], lambda e: e.tensor_copy(out=uh[:, j, 0:30], in_=uh[:, j, 512:542]))

                fillers = [(j, kk) for j in range(8) for kk in range(31)]

                def fill(n):
                    for _ in range(min(n, len(fillers))):
                        j_, kk_ = fillers.pop(0)
                        emit_conv_mm(j_, kk_)

                for tt in range(4):
                    cs = slice(tt * 128, (tt + 1) * 128)
                    pt, pb_ = PS[2]
                    ptb = pt[:].bitcast(BF16)
                    for j in range(8):
                        kb.op("pe", [xa_bs[j], cb_b], [pb_], lambda e, j=j: e.transpose(out=ptb[:, j * 128:(j + 1) * 128], in_=xa[:, j, cs], identity=ident_b))
                    kb.op("act", [pb_], [xt_b], lambda e: e.activation(out=xt[:], in_=ptb, func=AF.Copy))
                    pt3, pb3 = PS[3]
                    pt3b = pt3[:].bitcast(BF16)
                    for g in range(2):
                        kb.op("pe", [xa_bs[8 + g], cb_b], [pb3], lambda e, g=g: e.transpose(out=pt3b[:, g * 128:(g + 1) * 128], in_=xa[:, 8 + g, cs], identity=ident_b))
                    kb.op("act", [pb3], [Bt_b], lambda e: e.activation(out=Bt[:], in_=pt3b[:, 0:256].rearrange("p (g n) -> p g n", g=2), func=AF.Copy))
                    fill(8)
                    pt, pb_ = PS[2]
                    kb.op("pe", [dta_b, cf_b], [pb_], lambda e: e.matmul(pt[:, 0:16], lhsT=L_f, rhs=dta_t[:, tt, :], start=True, stop=True))
                    kb.op("pe", [dta_b, cf_b], [pb_], lambda e: e.matmul(pt[:, 16:32], lhsT=ones_f, rhs=dta_t[:, tt, :], start=True, stop=True))
                    fill(6)
                    kb.op("act", [pb_], [cums_b], lambda e: e.activation(out=cums[:, 0:32], in_=pt[:, 0:32], func=AF.Exp))
                    kb.op("dve", [pb_], [cums_b], lambda e: e.tensor_copy(out=cums[:, 48:64], in_=pt[:, 0:16]))
                    kb.op("dve", [pb_, cums_b], [cums_b], lambda e: e.tensor_tensor(out=cums[:, 48:64], in0=pt[:, 16:32], in1=cums[:, 48:64], op=ALU.subtract))
                    kb.op("act", [cums_b], [cums_b], lambda e: e.activation(out=cums[:, 32:48], in_=cums[:, 48:64], func=AF.Exp))
                    kb.op("dve", [cums_b, dt_b], [cums_b], lambda e: e.tensor_tensor(out=cums[:, 32:48], in0=cums[:, 32:48], in1=dt_t[:, tt, :], op=ALU.mult))
                    ecum = cums[:, 0:16]
                    etot = cums[:, 16:32]
                    toend = cums[:, 32:48]
                    for g in range(2):
                        pt3, pb3 = PS[3]
                        kb.op("pe", [xa_bs[8 + g], xa_bs[10 + g]], [pb3], lambda e, g=g: e.matmul(pt3[:, 0:128], lhsT=xa[:, 8 + g, cs], rhs=xa[:, 10 + g, cs], start=True, stop=True))
                        kb.op("dve", [pb3, cf_b], [CBm_b], lambda e, g=g: e.tensor_tensor(out=CBm[:, g, :], in0=pt3[:, 0:128], in1=tri_f, op=ALU.mult), waw=(g > 0))
                    for g in range(2):
                        pt, pb_ = PS[6 + g]
                        kb.op("pe", [xa_bs[10 + g], stbf_b], [pb_], lambda e, g=g: e.matmul(pt[:], lhsT=xa[:, 10 + g, cs], rhs=stbf[:, g, :], start=True, stop=True))
                        kb.op("dve", [pb_, cums_b], [y1_b], lambda e, g=g: e.tensor_tensor(out=y1[:, g * 512:(g + 1) * 512].rearrange("p (h d) -> p h d", h=8), in0=pt[:].rearrange("p (h d) -> p h d", h=8), in1=ecum[:, g * 8:(g + 1) * 8].unsqueeze(2).broadcast_to([128, 8, 64]), op=ALU.mult), waw=(g > 0))
                    def stage_a(hq):
                        sgp, sgb = PS[4 + hq % 2]
                        dct, dcb = dec[hq % 2]
                        for hh in range(4):
                            h = hq * 4 + hh
                            lt, lb = lh[h % 8]
                            kb.op("pool", [cf_b, dta_b], [lb], lambda e, h=h: e.tensor_scalar(out=lt[:], in0=U_f, scalar1=dta_t[:, tt, h:h + 1], scalar2=0.0, op0=ALU.mult, op1=ALU.add))
                            kb.op("pe", [lb, cf_b], [sgb], lambda e, hh=hh: e.matmul(sgp[:, hh * 128:(hh + 1) * 128], lhsT=lt[:], rhs=L_f, start=True, stop=True), waw=(hh > 0))
                        kb.op("act", [sgb], [dcb], lambda e: e.activation(out=dct[:], in_=sgp[:], func=AF.Exp))

                    def stage_b(hq):
                        dct, dcb = dec[hq % 2]
                        for hh in range(4):
                            h = hq * 4 + hh
                            g = h // 8
                            wt_, wtb = wT[h % 4]
                            kb.op("dve", [dcb, dt_b, CBm_b], [wtb], lambda e, h=h, hh=hh, g=g: e.scalar_tensor_tensor(out=wt_[:], in0=dct[:, hh * 128:(hh + 1) * 128], scalar=dt_t[:, tt, h:h + 1], in1=CBm[:, g, :], op0=ALU.mult, op1=ALU.mult))
                            pt, pb_ = PS[6 + g]
                            hl = h % 8
                            kb.op("pe", [wtb, xt_b], [pb_], lambda e, h=h, hl=hl: e.matmul(pt[:, hl * 64:(hl + 1) * 64], lhsT=wt_[:], rhs=xt[:, h * 64:(h + 1) * 64], start=True, stop=True), waw=(hl > 0))

                    stage_a(0)
                    fill(4)
                    stage_a(1)
                    fill(4)
                    stage_b(0)
                    stage_a(2)
                    fill(6)
                    stage_b(1)
                    stage_a(3)
                    fill(6)
                    stage_b(2)
                    fill(6)
                    stage_b(3)
                    fill(6)
                    for g in range(2):
                        pt, pb_ = PS[6 + g]
                        kb.op("dve", [pb_, y1_b], [y1_b], lambda e, g=g: e.tensor_tensor(out=y1[:, g * 512:(g + 1) * 512], in0=pt[:], in1=y1[:, g * 512:(g + 1) * 512], op=ALU.add))
                    kb.op("dve", [xt_b, small_b], [y2_b], lambda e: e.tensor_tensor(out=y2[:].rearrange("p (h d) -> p h d", h=16), in0=xt[:].rearrange("p (h d) -> p h d", h=16), in1=dsk.unsqueeze(2).broadcast_to([128, 16, 64]), op=ALU.mult))
                    kb.op("dve", [y1_b, y2_b], [y1_b], lambda e: e.tensor_tensor(out=y1[:], in0=y1[:], in1=y2[:], op=ALU.add))
                    kb.op("dve", [xt_b, cums_b], [xw_b], lambda e: e.tensor_tensor(out=xw[:].rearrange("p (h d) -> p h d", h=16), in0=xt[:].rearrange("p (h d) -> p h d", h=16), in1=toend.unsqueeze(2).broadcast_to([128, 16, 64]), op=ALU.mult))
                    for g in range(2):
                        pt3, pb3 = PS[3]
                        kb.op("pe", [Bt_b, xw_b], [pb3], lambda e, g=g: e.matmul(pt3[:], lhsT=Bt[:, g, :], rhs=xw[:, g * 512:(g + 1) * 512], start=True, stop=True))
                        kb.op("dve", [st32_b, cums_b], [st32_b], lambda e, g=g: e.tensor_tensor(out=st32[:, g, :].rearrange("p (h d) -> p h d", h=8), in0=st32[:, g, :].rearrange("p (h d) -> p h d", h=8), in1=etot[:, g * 8:(g + 1) * 8].unsqueeze(2).broadcast_to([128, 8, 64]), op=ALU.mult))
                        kb.op("dve", [pb3, st32_b], [st32_b], lambda e, g=g: e.tensor_tensor(out=st32[:, g, :], in0=pt3[:], in1=st32[:, g, :], op=ALU.add))
                    kb.op("act", [st32_b], [stbf_b], lambda e: e.activation(out=stbf[:], in_=st32[:], func=AF.Copy))
                    fill(8)
                    kb.op("dve", [y1_b, sz_b], [y1_b], lambda e: e.tensor_tensor(out=y1[:], in0=y1[:], in1=sz[:, tt, :], op=ALU.mult))
                    for g in range(2):
                        kb.op("act", [y1_b], [junk_b, st4_b], lambda e, g=g: e.activation(out=junk[:, 0:512], in_=y1[:, g * 512:(g + 1) * 512], func=AF.Square, accum_out=st4[:, g:g + 1]))
                    rstd_from_ss(st4[:, 0:2], 512.0, st4[:, 2:4], [st4_b])
                    for g in range(2):
                        kb.op("dve", [y1_b, st4_b, ssdn_b], [ya_b], lambda e, g=g: e.scalar_tensor_tensor(out=ya[:, g * 512:(g + 1) * 512], in0=y1[:, g * 512:(g + 1) * 512], scalar=st4[:, 2 + g:3 + g], in1=ssdn_t[:, g * 512:(g + 1) * 512], op0=ALU.mult, op1=ALU.mult), waw=(g > 0))
                    pt, pb_ = PS[2]
                    ptb = pt[:].bitcast(BF16)
                    for k in range(8):
                        kb.op("pe", [ya_b, cb_b], [pb_], lambda e, k=k: e.transpose(out=ptb[:, k * 128:(k + 1) * 128], in_=ya[:, k * 128:(k + 1) * 128], identity=ident_b))
                    kb.op("act", [pb_], [yaT_b], lambda e, tt=tt: e.activation(out=yaT[:, :, tt * 128:(tt + 1) * 128], in_=ptb.rearrange("p (k t) -> p k t", k=8), func=AF.Copy), waw=(tt > 0))
                    fill(8)

                fill(len(fillers))
                mean_p, mean_pb = PS[4]
                msq_p, msq_pb = PS[5]
                for j in range(8):
                    sqt, sqtb = (sq, sq_b) if j % 2 == 0 else (s1, s1_b)
                    kb.op("act", [uc_bs[j]], [sqtb], lambda e, j=j: e.activation(out=sqt[:], in_=uc[:, j, :], func=AF.Square))
                    kb.op("pe", [uc_bs[j], cf_b], [mean_pb], lambda e, j=j: e.matmul(mean_p[:], lhsT=ones_f, rhs=uc[:, j, :], start=(j == 0), stop=(j == 7)))
                    kb.op("pe", [sqtb, cf_b], [msq_pb], lambda e, j=j: e.matmul(msq_p[:], lhsT=ones_f, rhs=sqt[:], start=(j == 0), stop=(j == 7)))
                kb.op("act", [mean_pb], [lnm_b], lambda e: e.activation(out=lnm[:], in_=mean_p[:], func=AF.Copy, scale=1.0 / 1024))
                kb.op("act", [lnm_b], [sq_b], lambda e: e.activation(out=sq[:], in_=lnm[:], func=AF.Square))
                kb.op("dve", [msq_pb, sq_b], [lnr_b], lambda e: e.scalar_tensor_tensor(out=lnr[:], in0=msq_p[:], scalar=1.0 / 1024, in1=sq[:], op0=ALU.mult, op1=ALU.subtract))
                kb.op("dve", [lnr_b], [lnr_b], lambda e: e.tensor_scalar(out=lnr[:], in0=lnr[:], scalar1=EPS, scalar2=None, op0=ALU.add))
                kb.op("act", [lnr_b], [lnr_b], lambda e: e.activation(out=lnr[:], in_=lnr[:], func=AF.Sqrt))
                kb.op("dve", [lnr_b], [lnr_b], lambda e: e.reciprocal(out=lnr[:], in_=lnr[:]))
                kb.op("dve", [lnm_b, lnr_b], [lnm_b], lambda e: e.scalar_tensor_tensor(out=lnm[:], in0=lnm[:], scalar=-1.0, in1=lnr[:], op0=ALU.mult, op1=ALU.mult))
                for sl in range(2):
                    wz, wzb = load_w(w_in_bf, 0, 4624 + sl * 512, 512, wkey="cwi%d" % li)
                    for jj in range(4):
                        j = sl * 4 + jj
                        pz, pzb = PS[j % 2]
                        proj_fm(wz, wzb, jj * 128, pz, pzb)
                        kb.op("act", [pzb], [szc_b], lambda e: e.activation(out=szc[:], in_=pz[:], func=AF.Silu))
                        kb.op("dve", [uc_bs[j], lnr_b], [un_b], lambda e, j=j: e.tensor_tensor(out=un[:], in0=uc[:, j, :], in1=lnr[:], op=ALU.mult))
                        kb.op("dve", [un_b, lnm_b], [un_b], lambda e: e.tensor_tensor(out=un[:], in0=un[:], in1=lnm[:], op=ALU.add))
                        kb.op("act", [un_b, cc4_b], [s1_b], lambda e, j=j: e.activation(out=s1[:], in_=un[:], func=AF.Silu, scale=cc4[:, 1, j:j + 1], bias=cc4[:, 2, j:j + 1]))
                        kb.op("dve", [s1_b, szc_b], [ybT_b], lambda e, j=j: e.tensor_tensor(out=ybT[:, j, :], in0=s1[:], in1=szc[:], op=ALU.mult), waw=(j > 0))

                for half in range(2):
                    for kh in range(2):
                        wo, wob = load_w(w_out_bf, kh * 1024, half * 512, 512, wkey="cwo%d" % li)
                        srcT, srcb = (yaT, yaT_b) if kh == 0 else (ybT, ybT_b)
                        for tt in range(4):
                            pt, pb_ = PS[4 + tt]
                            for k in range(8):
                                kb.op("pe", [wob, srcb], [pb_], lambda e, k=k, tt=tt, kh=kh: e.matmul(pt[:], lhsT=srcT[:, k, tt * 128:(tt + 1) * 128], rhs=wo[:, k, :], start=(kh == 0 and k == 0), stop=(kh == 1 and k == 7)))
                    for tt in range(4):
                        pt, pb_ = PS[4 + tt]
                        kb.op("dve", [pb_, hblk_b], [hblk_b], lambda e, tt=tt, half=half: e.tensor_tensor(out=hblk[:, tt, half * 512:(half + 1) * 512], in0=pt[:], in1=hblk[:, tt, half * 512:(half + 1) * 512], op=ALU.add))
                for tt in range(4):
                    ti = blk * 4 + tt
                    kb.dma("pool", [hblk_b], [h_b[ti]], hD[t0 + tt * 128:t0 + (tt + 1) * 128, :], hblk[:, tt, :])
            state["src"] = hD


        def odd_layer(li):
            src = state["src"]
            NSEL = S // 64
            NCMP = S // 16 - 1
            NCT = (NCMP + 127) // 128
            NCP = NCT * 128
            w_in_bf = ow["w_in_bf"][li]
            wk = "owi%d" % li
            ksA, ksA_b = kb.sb("o_ksA", [128, 4, S], BF16)
            kwT, kwT_b = kb.sb("o_kwT", [64, 4, S], BF16)
            vsA, vsA_b = kb.sb("o_vsA", [128, NT, 4, 65], BF16)
            vwA, vwA_b = kb.sb("o_vwA", [128, NT, 4, 65], BF16)
            kcT, kcT_b = kb.sb("o_kcT", [64, 4, NCP], BF16)
            vcA, vcA_b = kb.sb("o_vcA", [128, NCT, 4, 65], BF16)
            gts, gts_b = kb.sb("o_gts", [128, NT, 48], F32)
            wo, wo_b = kb.sb("o_wo", [128, 8, D], BF16)
            norm_t, norm_b = kb.sb("o_norm_t", [128, D], F32)
            gbias, gbias_b = kb.sb("o_gbias", [128, 48], F32)
            Amask, Amask_b = kb.sb("o_Amask", [128, 128], F32)
            kb.dma("sp", [], [norm_b], norm_t[:], ow["norm"][li])
            kb.dma("sp", [], [gbias_b], gbias[:], ow["gate_bias"][li])
            kb.dma("sp", [], [Amask_b], Amask[:], ow["amask"][:, :])
            kb.dma("sp", [wdram_b["owo%d" % li]], [wo_b], wo[:], ow["w_out_bf"][li].rearrange("(k p) c -> p k c", p=128))
            for g in range(4):
                kb.dma("sp", [], [ksA_b], ksA[64:128, g, :], ow["emat"][:, :], waw=True)
            kb.op("pool", [], [vsA_b], lambda e: e.memset(vsA[:], 1.0))
            kb.op("pool", [], [vwA_b], lambda e: e.memset(vwA[:], 1.0))
            kb.op("pool", [], [vcA_b], lambda e: e.memset(vcA[:], 0.0))
            kb.op("pool", [vcA_b], [vcA_b], lambda e: e.memset(vcA[:, :, :, 64:65], 1.0))
            kb.op("pool", [], [kcT_b], lambda e: e.memset(kcT[:], 0.0))

            def rstd_from_ss(ss_ap, n, out_ap, bufs):
                kb.op("dve", bufs, bufs, lambda e: e.tensor_scalar(out=out_ap, in0=ss_ap, scalar1=1.0 / n, scalar2=EPS, op0=ALU.mult, op1=ALU.add))
                kb.op("act", bufs, bufs, lambda e: e.activation(out=out_ap, in_=out_ap, func=AF.Sqrt))
                kb.op("dve", bufs, bufs, lambda e: e.reciprocal(out=out_ap, in_=out_ap))

            qT_D = ow["qT_D"]
            kT_D = ow["kT_D"]
            sz_D = ow["sz_D"]
            qD_b = kb.buf("qD")
            kD_b = kb.buf("kD")
            szD_b = kb.buf("szD")

            with ExitStack() as esA:
                old = kb.es
                kb.es = esA
                wgl, wgl_b = kb.sb("oA_wgl", [128, 8, 48], BF16)
                kb.dma("sp", [wdram_b[wk]], [wgl_b], wgl[:], w_in_bf[0:1024, 2560:2608].rearrange("(k p) c -> p k c", p=128))
                hts = [kb.sb("oA_h%d" % i, [128, D], F32) for i in range(2)]
                hn, hn_b = kb.sb("oA_hn", [128, D], BF16)
                junk, junk_b = kb.sb("oA_junk", [128, D], F32)
                st4, st4_b = kb.sb("oA_st4", [128, 2], F32)
                hnT, hnT_b = kb.sb("oA_hnT", [128, 8, 512], BF16)
                stg = [kb.sb("oA_stg%d" % i, [128, 512], BF16) for i in range(6)]
                gtmp, gtmp_b = kb.sb("oA_gtmp", [128, 192], F32)
                si = [0]
                pi = [0]

                def nps():
                    p = PS[(0, 1, 3, 4, 5, 6)[pi[0] % 6]]
                    pi[0] += 1
                    return p

                def fm_tile(wt, wb_, col0, dst_ap, scale=1.0):
                    pt, pb_ = nps()
                    for k in range(8):
                        kb.op("pe", [wb_, hnT_b], [pb_], lambda e, k=k: e.matmul(pt[:], lhsT=wt[:, k, col0:col0 + 128], rhs=hnT[:, k, :], start=(k == 0), stop=(k == 7)))
                    s_, sb_ = stg[si[0] % 6]
                    si[0] += 1
                    kb.op("act", [pb_], [sb_], lambda e: e.activation(out=s_[:], in_=pt[:], func=AF.Copy, scale=scale))
                    return s_, sb_

                for blk in range(NB):
                    t0 = blk * 512
                    for tt in range(4):
                        ti = blk * 4 + tt
                        ht, hb = hts[ti % 2]
                        kb.dma("sp", [h_b[ti]], [hb], ht[:], src[t0 + tt * 128:t0 + (tt + 1) * 128, :])
                        kb.op("act", [hb], [junk_b, st4_b], lambda e: e.activation(out=junk[:], in_=ht[:], func=AF.Square, accum_out=st4[:, 0:1]))
                        rstd_from_ss(st4[:, 0:1], float(D), st4[:, 1:2], [st4_b])
                        kb.op("dve", [hb, st4_b, norm_b], [hn_b], lambda e: e.scalar_tensor_tensor(out=hn[:], in0=ht[:], scalar=st4[:, 1:2], in1=norm_t[:], op0=ALU.mult, op1=ALU.mult))
                        pt, pb_ = PS[2]
                        ptb = pt[:].bitcast(BF16)
                        for k in range(8):
                            kb.op("pe", [hn_b, cb_b], [pb_], lambda e, k=k: e.transpose(out=ptb[:, k * 128:(k + 1) * 128], in_=hn[:, k * 128:(k + 1) * 128], identity=ident_b))
                        kb.op("act", [pb_], [hnT_b], lambda e, tt=tt: e.activation(out=hnT[:, :, tt * 128:(tt + 1) * 128], in_=ptb.rearrange("p (k t) -> p k t", k=8), func=AF.Copy), waw=(tt > 0))
                    for sl in range(2):
                        wt, wb_ = load_w(w_in_bf, 0, sl * 512, 512, wkey=wk)
                        for jj in range(4):
                            s_, sb_ = fm_tile(wt, wb_, jj * 128, None, scale=0.125)
                            r0 = (sl * 4 + jj) * 128
                            kb.dma("pool", [sb_], [qD_b], qT_D[r0:r0 + 128, t0:t0 + 512], s_[:], waw=True)
                    wt, wb_ = load_w(w_in_bf, 0, 1024, 512, wkey=wk)
                    for jj in range(4):
                        s_, sb_ = fm_tile(wt, wb_, jj * 128, None)
                        kind, half = jj // 2, jj % 2
                        kb.dma("pool", [sb_], [kD_b], kT_D[kind, half * 128:(half + 1) * 128, t0:t0 + 512], s_[:], waw=True)
                    for kind, c0, vA, vA_b in ((2, 1536, vsA, vsA_b), (3, 2048, vwA, vwA_b)):
                        wt, wb_ = load_w(w_in_bf, 0, c0, 512, wkey=wk)
                        for jj in range(2):
                            s_, sb_ = fm_tile(wt, wb_, jj * 128, None)
                            kb.dma("pool", [sb_], [kD_b], kT_D[kind, jj * 128:(jj + 1) * 128, t0:t0 + 512], s_[:], waw=True)
                        for tt in range(4):
                            ti = blk * 4 + tt
                            pt, pb_ = nps()
                            for k in range(8):
                                kb.op("pe", [wb_, hnT_b], [pb_], lambda e, k=k, tt=tt: e.matmul(pt[:, 0:256], lhsT=hnT[:, k, tt * 128:(tt + 1) * 128], rhs=wt[:, k, 256:512], start=(k == 0), stop=(k == 7)))
                            kb.op("act", [pb_], [vA_b], lambda e, ti=ti, vA=vA: e.activation(out=vA[:, ti, :, 0:64], in_=pt[:, 0:256].rearrange("p (g d) -> p g d", g=4), func=AF.Copy), waw=True)
                    wt, wb_ = wgl, wgl_b
                    pt, pb_ = nps()
                    for tt in range(4):
                        for k in range(8):
                            kb.op("pe", [wb_, hnT_b], [pb_], lambda e, k=k, tt=tt: e.matmul(pt[:, tt * 48:(tt + 1) * 48], lhsT=hnT[:, k, tt * 128:(tt + 1) * 128], rhs=wt[:, k, 0:48], start=(k == 0), stop=(k == 7)), waw=(tt > 0 or k > 0))
                    kb.op("act", [pb_], [gtmp_b], lambda e: e.activation(out=gtmp[:], in_=pt[:, 0:192], func=AF.Copy))
                    kb.op("dve", [gtmp_b, gbias_b], [gtmp_b], lambda e: e.tensor_tensor(out=gtmp[:].rearrange("p (t c) -> p t c", t=4), in0=gtmp[:].rearrange("p (t c) -> p t c", t=4), in1=gbias[:].unsqueeze(1).broadcast_to([128, 4, 48]), op=ALU.add))
                    kb.op("act", [gtmp_b], [gts_b], lambda e, blk=blk: e.activation(out=gts[:, blk * 4:(blk + 1) * 4, :], in_=gtmp[:].rearrange("p (t c) -> p t c", t=4), func=AF.Sigmoid), waw=True)
                    for half in range(2):
                        wt, wb_ = load_w(w_in_bf, 0, 2608 + half * 512, 512, wkey=wk)
                        for tt in range(4):
                            pt, pb_ = nps()
                            for k in range(8):
                                kb.op("pe", [wb_, hnT_b], [pb_], lambda e, k=k, tt=tt: e.matmul(pt[:], lhsT=hnT[:, k, tt * 128:(tt + 1) * 128], rhs=wt[:, k, :], start=(k == 0), stop=(k == 7)))
                            s_, sb_ = stg[si[0] % 6]
                            si[0] += 1
                            kb.op("act", [pb_], [sb_], lambda e: e.activation(out=s_[:], in_=pt[:], func=AF.Silu))
                            kb.dma("pool", [sb_], [szD_b], sz_D[t0 + tt * 128:t0 + (tt + 1) * 128, half * 512:(half + 1) * 512], s_[:], waw=True)
                kb.es = old
            kb.barrier()
            if ODD_STOP == "A":
                return
            kb.dma("sp", [kD_b], [ksA_b], ksA[0:64, :, :], kT_D[2].rearrange("(g d) s -> d g s", g=4), waw=True)
            kb.dma("sp", [kD_b], [kwT_b], kwT[:, :, :], kT_D[3].rearrange("(g d) s -> d g s", g=4))

            with ExitStack() as esB:
                old = kb.es
                kb.es = esB
                w1f, w1f_b = kb.sb("oB_w1f", [64, 16, 256], F32)
                w1b, w1b_b = kb.sb("oB_w1b", [64, 32, 256], BF16)
                w2f, w2f_b = kb.sb("oB_w2f", [128, 2, 64], F32)
                w2b, w2b_b = kb.sb("oB_w2b", [128, 2, 64], BF16)
                pef, pef_b = kb.sb("oB_pef", [64, 32], F32)
                peb, peb_b = kb.sb("oB_peb", [64, 32], BF16)
                b1, b1_b = kb.sb("oB_b1", [128, 2], F32)
                tcs = [kb.sb("oB_tc%d" % i, [64, S], BF16) for i in range(2)]
                h1T, h1T_b = kb.sb("oB_h1T", [128, 2, NCP], BF16)
                for kind in range(2):
                    nm = "kv"[kind]
                    for lh_ in range(2):
                        kb.dma("sp", [], [w1f_b], w1f[:], ow["w1_" + nm][li][lh_ * 1024:(lh_ + 1) * 1024, :].rearrange("(l d) j -> d l j", d=64))
                        kb.op("pool", [w1f_b], [w1b_b], lambda e, lh_=lh_: e.tensor_copy(out=w1b[:, lh_ * 16:(lh_ + 1) * 16, :], in_=w1f[:]), waw=(lh_ > 0))
                    kb.dma("sp", [], [w2f_b], w2f[:], ow["w2_" + nm][li].rearrange("(jt p) d -> p jt d", p=128))
                    kb.op("pool", [w2f_b], [w2b_b], lambda e: e.tensor_copy(out=w2b[:], in_=w2f[:]))
                    kb.dma("sp", [], [pef_b], pef[:], ow["peT_" + nm][li])
                    kb.op("pool", [pef_b], [peb_b], lambda e: e.tensor_copy(out=peb[:], in_=pef[:]))
                    pt, pb_ = PS[2]
                    for jt in range(2):
                        for l in range(32):
                            kb.op("pe", [w1b_b, peb_b], [pb_], lambda e, jt=jt, l=l: e.matmul(pt[:, jt:jt + 1], lhsT=w1b[:, l, jt * 128:(jt + 1) * 128], rhs=peb[:, l:l + 1], start=(jt == 0 and l == 0), stop=(l == 31), skip_group_check=True), waw=True)
                    kb.op("dve", [pb_], [b1_b], lambda e: e.tensor_copy(out=b1[:], in_=pt[:, 0:2]))
                    for g in range(4):
                        tc_, tcb = tcs[g % 2]
                        kb.dma("sp", [kD_b], [tcb], tc_[:], kT_D[kind, g * 64:(g + 1) * 64, :])
                        tcv = tc_[:].rearrange("p (n s) -> p n s", s=16)
                        for jt in range(2):
                            pj, pjb = PS[jt]
                            for l in range(32):
                                kb.op("pe", [w1b_b, tcb], [pjb], lambda e, jt=jt, l=l: e.matmul(pj[:, 0:NCMP], lhsT=w1b[:, l, jt * 128:(jt + 1) * 128], rhs=tcv[:, l // 16:l // 16 + NCMP, l % 16], start=(l == 0), stop=(l == 31)))
                            kb.op("act", [pjb, b1_b], [h1T_b], lambda e, jt=jt: e.activation(out=h1T[:, jt, 0:NCMP], in_=pj[:, 0:NCMP], func=AF.Silu, bias=b1[:, jt:jt + 1]), waw=(jt > 0))
                        if kind == 0:
                            po, pob = PS[3]
                            for jt in range(2):
                                kb.op("pe", [w2b_b, h1T_b], [pob], lambda e, jt=jt: e.matmul(po[0:64, 0:NCMP], lhsT=w2b[:, jt, :], rhs=h1T[:, jt, 0:NCMP], start=(jt == 0), stop=(jt == 1)))
                            kb.op("act", [pob], [kcT_b], lambda e, g=g: e.activation(out=kcT[:, g, 0:NCMP], in_=po[0:64, 0:NCMP], func=AF.Copy), waw=True)
                        else:
                            for nt in range(NCT):
                                rows = min(NCMP, (nt + 1) * 128) - nt * 128
                                po, pob = PS[3]
                                for jt in range(2):
                                    kb.op("pe", [w2b_b, h1T_b], [pob], lambda e, jt=jt, nt=nt, rows=rows: e.matmul(po[0:rows, 0:64], lhsT=h1T[:, jt, nt * 128:nt * 128 + rows], rhs=w2b[:, jt, :], start=(jt == 0), stop=(jt == 1)))
                                kb.op("act", [pob], [vcA_b], lambda e, g=g, nt=nt, rows=rows: e.activation(out=vcA[0:rows, nt, g, 0:64], in_=po[0:rows, 0:64], func=AF.Copy), waw=True)
                kb.es = old
            kb.barrier()
            if ODD_STOP == "B":
                return

            with ExitStack() as esC:
                old = kb.es
                kb.es = esC
                qas = [kb.sb("oC_qa%d" % i, [128, 4, 512], BF16) for i in range(2)]
                qab = [[kb.buf("qab%d_%d" % (i, g)) for g in range(4)] for i in range(2)]
                NPT = 4
                pts = [kb.sb("oC_pt%d" % i, [128, 512], BF16) for i in range(NPT)]
                NET = 6
                ets = [kb.sb("oC_et%d" % i, [128, NCP], F32) for i in range(NET)]
                eti = [0]
                rss = [kb.sb("oC_rs%d" % i, [128, 8], F32) for i in range(2)]
                cmsk, cmsk_b = kb.sb("oC_cmsk", [10, 128 + 288], BF16)
                kb.dma("sp", [], [cmsk_b], cmsk[:], ow["cmsk"][:, :])
                pacc, pacc_b = kb.sb("oC_pacc", [128, NCP], F32)
                imp, imp_b = kb.sb("oC_imp", [128, 64], F32)
                imp2, imp2_b = kb.sb("oC_imp2", [128, 64], F32)
                m8, m8_b = kb.sb("oC_m8", [128, 16], F32)
                bts = [kb.sb("oC_bt%d" % g, [128, 128], BF16) for g in range(4)]
                szt = [kb.sb("oC_sz%d" % i, [128, D], BF16) for i in range(2)]
                hts = [kb.sb("oC_h%d" % i, [128, D], F32) for i in range(3)]
                cfs, cfs_b = kb.sb("oC_cfs", [128, 3, 4], F32)
                acc, acc_b = kb.sb("oC_acc", [128, 256], F32)
                ys = [kb.sb("oC_y%d" % i, [128, D], BF16) for i in range(2)]
                yT, yT_b = kb.sb("oC_yT", [128, 8, 128], BF16)
                for g in range(4):
                    kb.op("pool", [], [bts[g][1]], lambda e, g=g: e.memset(bts[g][0][:], 0.0))
                pti = [0]
                sci = [0]

                def sc_bank():
                    p = PS[sci[0] % 4]
                    sci[0] += 1
                    return p

                def o_bank(g, X):
                    return PS[4 + X]

                ob3s = [kb.sb("oC_ob3_%d" % i, [128, 3, 260], F32) for i in range(2)]
                tmp3, tmp3_b = kb.sb("oC_tmp3", [128, 3, 256], F32)

                def load_tile(qi):
                    slot = qi % 2
                    t0 = qi * 128
                    qa, qa_b = qas[slot]
                    kb.dma("sp", [qD_b], [qa_b], qa[0:64, :, :].rearrange("d g (r t) -> d g r t", r=4), qT_D[:, t0:t0 + 128].rearrange("(g r d) t -> d g r t", g=4, r=4))
                    sz_, sz_b = szt[slot]
                    kb.dma("sp", [szD_b], [sz_b], sz_[:], sz_D[t0:t0 + 128, :])
                    ht, hb = hts[qi % 3]
                    kb.dma("sp", [h_b[qi]], [hb], ht[:], src[t0:t0 + 128, :])

                def need_sel(qi):
                    return (qi * 128 + 127) >= 1024 and NSEL > 16

                def imp_front(qi, gsel=None):
                    slot = qi % 2
                    t0 = qi * 128
                    qa, qa_b = qas[slot]
                    if not need_sel(qi):
                        for g in (range(4) if gsel is None else [gsel]):
                            kb.op("pool", [], [qab[slot][g]], lambda e, g=g: e.memset(qa[64:128, g, :], 0.0))
                        return
                    ncol = min(NCP, 8 * (qi + 1))
                    s0 = 128 + 258 - 8 * qi
                    for g in (range(4) if gsel is None else [gsel]):
                        rs4, rs4_b = rss[eti[0] % 2]
                        hets = []
                        for r in range(4):
                            ip, ipb = sc_bank()
                            kb.op("pe", [qa_b, kcT_b], [ipb], lambda e, r=r, g=g: e.matmul(ip[:, 0:ncol], lhsT=qa[0:64, g, r * 128:(r + 1) * 128], rhs=kcT[:, g, 0:ncol], start=True, stop=False))
                            kb.op("pe", [cmsk_b], [ipb], lambda e: e.matmul(ip[:, 0:ncol], lhsT=cmsk[0:10, 0:128], rhs=cmsk[0:10, s0:s0 + ncol], start=False, stop=True), waw=True)
                            et, et_b = ets[eti[0] % NET]
                            eti[0] += 1
                            kb.op("act", [ipb], [et_b, rs4_b], lambda e, r=r: e.activation(out=et[:, 0:ncol], in_=ip[:, 0:ncol], func=AF.Exp, accum_out=rs4[:, r:r + 1]), waw=True)
                            hets.append((et, et_b))
                        kb.op("dve", [rs4_b], [rs4_b], lambda e: e.tensor_scalar(out=rs4[:, 4:8], in0=rs4[:, 0:4], scalar1=1e-30, scalar2=None, op0=ALU.max))
                        kb.op("dve", [rs4_b], [rs4_b], lambda e: e.reciprocal(out=rs4[:, 4:8], in_=rs4[:, 4:8]))
                        if ncol < NCP:
                            kb.op("dve", [], [pacc_b], lambda e: e.memset(pacc[:, ncol:NCP], 0.0))
                        for r in range(4):
                            et, et_b = hets[r]
                            if r == 0:
                                kb.op("dve", [et_b, rs4_b], [pacc_b], lambda e, et=et: e.tensor_scalar(out=pacc[:, 0:ncol], in0=et[:, 0:ncol], scalar1=rs4[:, 4:5], scalar2=None, op0=ALU.mult), waw=True)
                            else:
                                kb.op("dve", [et_b, rs4_b, pacc_b], [pacc_b], lambda e, et=et, r=r: e.scalar_tensor_tensor(out=pacc[:, 0:ncol], in0=et[:, 0:ncol], scalar=rs4[:, 4 + r:5 + r], in1=pacc[:, 0:ncol], op0=ALU.mult, op1=ALU.add))
                        bt, bt_b = bts[g]
                        pv = pacc[:, 0:4 * NSEL].rearrange("p (j i) -> p j i", i=4)
                        kb.op("dve", [pacc_b], [imp_b], lambda e: e.tensor_reduce(out=imp[:, 0:NSEL], in_=pv, axis=AX.X, op=ALU.add))
                        kb.op("dve", [pacc_b, imp_b], [imp_b], lambda e: e.tensor_tensor(out=imp[:, 1:NSEL], in0=imp[:, 1:NSEL], in1=pv[:, 0:NSEL - 1, 3], op=ALU.add))
                        kb.op("dve", [imp_b, Amask_b], [imp_b], lambda e: e.tensor_tensor(out=imp[:, 0:NSEL], in0=imp[:, 0:NSEL], in1=Amask[:, 64 - 2 * qi:64 - 2 * qi + NSEL], op=ALU.add))
                        kb.op("dve", [imp_b], [imp_b], lambda e: e.memset(imp[:, 0:1], 1.0e6))
                        kb.op("dve", [imp_b], [m8_b], lambda e: e.max(out=m8[:, 0:8], in_=imp[:, 0:NSEL]))
                        kb.op("dve", [imp_b, m8_b], [imp2_b], lambda e: e.match_replace(out=imp2[:, 0:NSEL], in_to_replace=m8[:, 0:8], in_values=imp[:, 0:NSEL], imm_value=-2.0e9))
                        kb.op("dve", [imp2_b], [m8_b], lambda e: e.max(out=m8[:, 8:16], in_=imp2[:, 0:NSEL]))
                        kb.op("dve", [imp_b, m8_b], [bt_b], lambda e: e.tensor_scalar(out=bt[:, 64:64 + NSEL], in0=imp[:, 0:NSEL], scalar1=m8[:, 15:16], scalar2=NEG, op0=ALU.is_lt, op1=ALU.mult))

                def imp_back(qi):
                    if not need_sel(qi):
                        return
                    slot = qi % 2
                    qa, qa_b = qas[slot]
                    tp, tpb = PS[7]
                    tpv = tp[:].bitcast(BF16)
                    for g in range(4):
                        bt, bt_b = bts[g]
                        kb.op("pe", [bt_b, cb_b], [tpb], lambda e, g=g: e.transpose(out=tpv[:, g * 128:(g + 1) * 128], in_=bt[:], identity=ident_b), waw=(g > 0))
                    for g in range(4):
                        for r in range(4):
                            kb.op("dve", [tpb], [qab[slot][g]], lambda e, r=r, g=g: e.tensor_copy(out=qa[64:128, g, r * 128:(r + 1) * 128], in_=tpv[64:128, g * 128:(g + 1) * 128]), waw=(r > 0))

                def emit_qk(u):
                    sp_, spb = sc_bank()
                    kb.op("pe", u["rd"], [spb], lambda e: e.matmul(sp_[:], lhsT=u["lhsT"], rhs=u["rhs"], start=True, stop=True))
                    p_, p_b = pts[pti[0] % NPT]
                    pti[0] += 1
                    kb.op("act", [spb], [p_b], lambda e: e.activation(out=p_[:], in_=sp_[:], func=AF.Exp))
                    if u["mask"] is not None:
                        base, cm, step = u["mask"]
                        kb.op("pool", [p_b], [p_b], lambda e: e.affine_select(out=p_[:], in_=p_[:], pattern=[[0, 4], [step, 128]], compare_op=ALU.is_ge, fill=0.0, base=base, channel_multiplier=cm))
                    u["p"] = (p_, p_b)

                def emit_pv(u):
                    p_, p_b = u["p"]
                    o_ps, o_pb = u["o"]
                    for r in range(4):
                        st = u["first"] and r == 0
                        kb.op("pe", [p_b, u["vb"]], [o_pb], lambda e, r=r, st=st: e.matmul(o_ps[:, r * 65:(r + 1) * 65], lhsT=p_[:, r * 128:(r + 1) * 128], rhs=u["v"], start=st, stop=u["last"], skip_group_check=True), waw=not st)
                    if u["last"]:
                        ob, ob_b = ob3s[u["g"] % 2]
                        X_ = u["X"]
                        evq.append([2, lambda: kb.op("act", [o_pb], [ob_b], lambda e: e.activation(out=ob[:, X_, :], in_=o_ps[:, 0:260], func=AF.Copy), waw=True)])

                evq = []

                def evq_tick(force=False):
                    for it in list(evq):
                        it[0] -= 1
                        if it[0] <= 0 or force:
                            it[1]()
                            evq.remove(it)

                def combine(qi, g):
                    evq_tick(force=True)
                    slot = qi % 2
                    sz_, sz_b = szt[slot]
                    y, y_b = ys[qi % 2]
                    ob, ob_b = ob3s[g % 2]
                    gv = gts[:, qi, :].rearrange("p (h x) -> p h x", x=3)[:, 4 * g:4 * g + 4, :].rearrange("p r x -> p x r")
                    ov = ob[:].rearrange("p x (r c) -> p x r c", c=65)
                    kb.op("dve", [ob_b], [cfs_b], lambda e: e.tensor_scalar(out=cfs[:], in0=ov[:, :, :, 64], scalar1=1e-30, scalar2=None, op0=ALU.max))
                    kb.op("dve", [cfs_b], [cfs_b], lambda e: e.reciprocal(out=cfs[:], in_=cfs[:]))
                    kb.op("dve", [cfs_b, gts_b], [cfs_b], lambda e: e.tensor_tensor(out=cfs[:], in0=cfs[:], in1=gv, op=ALU.mult))
                    kb.op("dve", [ob_b, cfs_b], [tmp3_b], lambda e: e.tensor_tensor(out=tmp3[:].rearrange("p x (r d) -> p x r d", r=4), in0=ov[:, :, :, 0:64], in1=cfs[:].unsqueeze(3).broadcast_to([128, 3, 4, 64]), op=ALU.mult))
                    kb.op("dve", [tmp3_b], [acc_b], lambda e: e.tensor_tensor(out=acc[:], in0=tmp3[:, 0, :], in1=tmp3[:, 1, :], op=ALU.add))
                    kb.op("dve", [tmp3_b, acc_b], [acc_b], lambda e: e.tensor_tensor(out=acc[:], in0=acc[:], in1=tmp3[:, 2, :], op=ALU.add))
                    kb.op("dve", [acc_b, sz_b], [y_b], lambda e, g=g: e.tensor_tensor(out=y[:, g * 256:(g + 1) * 256], in0=acc[:], in1=sz_[:, g * 256:(g + 1) * 256], op=ALU.mult), waw=(g > 0))

                def units_for(qi, g):
                    slot = qi % 2
                    t0 = qi * 128
                    qa, qa_b = qas[slot]
                    us = []
                    cts = [ct for ct in range(NCT) if t0 + 127 - 2048 * ct - 31 >= 0]
                    for n_, ct in enumerate(cts):
                        base = t0 - 2048 * ct - 31
                        mk = None if base - 16 * 127 >= 0 else (base, -16, 1)
                        us.append(dict(rd=[qa_b, kcT_b], lhsT=kcT[:, g, ct * 128:(ct + 1) * 128], rhs=qa[0:64, g, :], mask=mk,
                                       v=vcA[:, ct, g, :], vb=vcA_b, o=o_bank(g, 0), X=0, first=(n_ == 0), last=(n_ == len(cts) - 1)))
                    for kt in range(qi + 1):
                        us.append(dict(rd=[qa_b, qab[slot][g], ksA_b], lhsT=ksA[:, g, kt * 128:(kt + 1) * 128], rhs=qa[:, g, :],
                                       mask=((0, -1, 1) if kt == qi else None), v=vsA[:, kt, g, :], vb=vsA_b, o=o_bank(g, 1), X=1, first=(kt == 0), last=(kt == qi)))
                    k0 = max(0, qi - 4)
                    for kt in range(k0, qi + 1):
                        mk = (0, -1, 1) if kt == qi else ((-1, 1, -1) if kt == qi - 4 else None)
                        us.append(dict(rd=[qa_b, kwT_b], lhsT=kwT[:, g, kt * 128:(kt + 1) * 128], rhs=qa[0:64, g, :], mask=mk,
                                       v=vwA[:, kt, g, :], vb=vwA_b, o=o_bank(g, 2), X=2, first=(kt == k0), last=(kt == qi)))
                    for u in us:
                        u["g"] = g
                    return us

                def out_proj(qi):
                    t0 = qi * 128
                    ht, hb = hts[qi % 3]
                    y, y_b = ys[qi % 2]
                    tp, tpb = PS[7]
                    tpv = tp[:].bitcast(BF16)
                    for k in range(8):
                        kb.op("pe", [y_b, cb_b], [tpb], lambda e, k=k: e.transpose(out=tpv[:, k * 128:(k + 1) * 128], in_=y[:, k * 128:(k + 1) * 128], identity=ident_b))
                    kb.op("act", [tpb], [yT_b], lambda e: e.activation(out=yT[:], in_=tpv.rearrange("p (k t) -> p k t", k=8), func=AF.Copy))

                def out_proj2(qi):
                    t0 = qi * 128
                    ht, hb = hts[qi % 3]
                    for half in range(2):
                        pp, ppb = PS[7]
                        for k in range(8):
                            kb.op("pe", [yT_b, wo_b], [ppb], lambda e, k=k, half=half: e.matmul(pp[:], lhsT=yT[:, k, :], rhs=wo[:, k, half * 512:(half + 1) * 512], start=(k == 0), stop=(k == 7)))
                        kb.op("dve", [ppb, hb], [hb], lambda e, half=half: e.tensor_tensor(out=ht[:, half * 512:(half + 1) * 512], in0=pp[:], in1=ht[:, half * 512:(half + 1) * 512], op=ALU.add))
                    if state.get("fuse_final"):
                        fs, fs_b = fns
                        kb.op("act", [hb], [yT_b, fs_b], lambda e: e.activation(out=yT[:].rearrange("p k t -> p (k t)"), in_=ht[:], func=AF.Square, accum_out=fs[:, 0:1]))
                        kb.op("dve", [fs_b], [fs_b], lambda e: e.tensor_scalar(out=fs[:, 1:2], in0=fs[:, 0:1], scalar1=1.0 / D, scalar2=EPS, op0=ALU.mult, op1=ALU.add))
                        kb.op("act", [fs_b], [fs_b], lambda e: e.activation(out=fs[:, 1:2], in_=fs[:, 1:2], func=AF.Sqrt))
                        kb.op("dve", [fs_b], [fs_b], lambda e: e.reciprocal(out=fs[:, 1:2], in_=fs[:, 1:2]))
                        kb.op("dve", [hb, fs_b, norm_b], [hb], lambda e: e.scalar_tensor_tensor(out=ht[:], in0=ht[:], scalar=fs[:, 1:2], in1=norm_t[:], op0=ALU.mult, op1=ALU.mult))
                        ob_ = kb.buf()
                        outs.append(ob_)
                        kb.dma("pool", [hb], [ob_], out_d[t0:t0 + 128, :], ht[:])
                    else:
                        kb.dma("pool", [hb], [h_b[qi]], hD[t0:t0 + 128, :], ht[:])

                if state.get("fuse_final"):
                    fns = kb.sb("oC_fs", [128, 2], F32)
                    kb.dma("sp", [], [norm_b], norm_t[:], final_norm[:, :])
                LOOK = 2
                load_tile(0)
                imp_front(0)
                imp_back(0)
                deferred = []
                for qi in range(NT):
                    if qi + 1 < NT:
                        load_tile(qi + 1)
                    todo = deferred
                    deferred = []
                    if qi + 1 < NT:
                        for g_ in range(4):
                            todo.append((6 + 8 * g_, lambda qi=qi, g_=g_: imp_front(qi + 1, g_)))
                    units = []
                    for g in range(4):
                        units += units_for(qi, g)
                    n = len(units)
                    if qi + 1 < NT:
                        todo.append((max(62, n - 40), lambda qi=qi: imp_back(qi + 1)))
                    for i in range(n + LOOK):
                        if i < n:
                            emit_qk(units[i])
                        evq_tick()
                        if i >= LOOK:
                            u = units[i - LOOK]
                            emit_pv(u)
                            if i - LOOK + 1 == n or units[i - LOOK + 1]["g"] != u["g"]:
                                combine(qi, u["g"])
                        for (k_, fn) in todo:
                            if k_ == i:
                                fn()
                    for (k_, fn) in todo:
                        if k_ >= n + LOOK:
                            fn()
                    deferred.append((40, lambda qi=qi: out_proj(qi)))
                    deferred.append((46, lambda qi=qi: out_proj2(qi)))
                for (k_, fn) in deferred:
                    fn()
                kb.es = old
            state["src"] = hD

        def final_norm_phase():
            src = state["src"]
            fn_t, fn_b = kb.sb("fn_t", [128, D], F32)
            kb.dma("sp", [], [fn_b], fn_t[:], final_norm[:, :])
            hts = [kb.sb("fn_h%d" % i, [128, D], F32) for i in range(2)]
            fj, fj_b = kb.sb("fn_j", [128, D], F32)
            fs, fs_b = kb.sb("fn_s", [128, 2], F32)
            for ti in range(NT):
                ht, hb = hts[ti % 2]
                kb.dma("sp", [h_b[ti]], [hb], ht[:], src[ti * 128:(ti + 1) * 128, :])
                kb.op("act", [hb], [fj_b, fs_b], lambda e: e.activation(out=fj[:], in_=ht[:], func=AF.Square, accum_out=fs[:, 0:1]))
                kb.op("dve", [fs_b], [fs_b], lambda e: e.tensor_scalar(out=fs[:, 1:2], in0=fs[:, 0:1], scalar1=1.0 / D, scalar2=EPS, op0=ALU.mult, op1=ALU.add))
                kb.op("act", [fs_b], [fs_b], lambda e: e.activation(out=fs[:, 1:2], in_=fs[:, 1:2], func=AF.Sqrt))
                kb.op("dve", [fs_b], [fs_b], lambda e: e.reciprocal(out=fs[:, 1:2], in_=fs[:, 1:2]))
                kb.op("dve", [hb, fs_b, fn_b], [hb], lambda e: e.scalar_tensor_tensor(out=ht[:], in0=ht[:], scalar=fs[:, 1:2], in1=fn_t[:], op0=ALU.mult, op1=ALU.mult))
                ob = kb.buf()
                outs.append(ob)
                kb.dma("pool", [hb], [ob], out_d[ti * 128:(ti + 1) * 128, :], ht[:])

        def cast_weight_dma(src_ap, dst_ap, rows, nm):
            wdram_b[nm] = kb.buf(nm)
            for r in range(0, rows, 128):
                kb.dma("pool", [], [wdram_b[nm]], dst_ap[r:r + 128, :], src_ap[r:r + 128, :], waw=True, max_dma_last_dim=4096)

        def cast_layer(kind, li):
            if kind == "e":
                cast_weight_dma(ew["w_in"][li], ew["w_in_bf"][li], D, "cwi%d" % li)
                cast_weight_dma(ew["w_out"][li], ew["w_out_bf"][li], 2048, "cwo%d" % li)
            else:
                cast_weight_dma(ow["w_in"][li], ow["w_in_bf"][li], D, "owi%d" % li)
                cast_weight_dma(ow["w_out"][li], ow["w_out_bf"][li], D, "owo%d" % li)

        cast_layer(*layers[0])
        for idx, (kind, li) in enumerate(layers):
            if idx + 1 < len(layers):
                cast_layer(*layers[idx + 1])
            with ExitStack() as es2:
                old = kb.es
                kb.es = es2
                state["fuse_final"] = (kind == "o" and idx == len(layers) - 1 and not debug_h)
                if kind == "e":
                    even_layer(li)
                else:
                    odd_layer(li)
                kb.es = old
            kb.barrier()
        if debug_h:
            hts = [kb.sb("dbg_h%d" % i, [128, D], F32) for i in range(2)]
            for ti in range(NT):
                ht, hb = hts[ti % 2]
                kb.dma("sp", [h_b[ti]], [hb], ht[:], state["src"][ti * 128:(ti + 1) * 128, :])
                ob = kb.buf()
                outs.append(ob)
                kb.dma("pool", [hb], [ob], out_d[ti * 128:(ti + 1) * 128, :], ht[:])
        elif not state.get("fuse_final"):
            with ExitStack() as es2:
                old = kb.es
                kb.es = es2
                final_norm_phase()
                kb.es = old
        kb.finish(outs)
    return nc


def make_consts():
    k = np.arange(128)
    ident = np.eye(128, dtype=np.float32)
    L = (k[:, None] <= k[None, :]).astype(np.float32)
    U = (k[:, None] > k[None, :]).astype(np.float32)
    tri = (k[None, :] >= k[:, None]).astype(np.float32)
    ones = np.ones((128, 128), np.float32)
    cf = np.concatenate([ident, L, U, tri, ones], axis=1)
    import ml_dtypes
    cb = np.concatenate([ident, L], axis=1).astype(ml_dtypes.bfloat16)
    return cf, cb


def bc(v):
    v = np.asarray(v, np.float32)
    return np.ascontiguousarray(np.broadcast_to(v[:, None], (v.shape[0], 128) + v.shape[1:]))


def host_inputs(inp, layers, S=4096):
    cf, cb = make_consts()
    m = {"consts_f32": cf, "consts_bf16": cb,
         "final_norm": np.ascontiguousarray(np.broadcast_to(np.asarray(inp["final_norm"], np.float32), (128, D)))}
    if any(l[0] == "e" for l in layers):
        m["e_norm"] = bc(inp["e_norm"])
        m["e_w_in"] = np.ascontiguousarray(inp["e_w_in"], dtype=np.float32)
        m["e_ssd_conv_w"] = np.ascontiguousarray(np.asarray(inp["e_ssd_conv_w"], np.float32).reshape(2, 4, 12, 128).transpose(0, 3, 2, 1))
        m["e_ssd_conv_b"] = np.ascontiguousarray(np.asarray(inp["e_ssd_conv_b"], np.float32).reshape(2, 12, 128).transpose(0, 2, 1))
        m["e_dt_bias"] = bc(inp["e_dt_bias"])
        m["e_a_log"] = bc(inp["e_a_log"])
        m["e_d_skip"] = bc(inp["e_d_skip"])
        m["e_ssd_norm"] = bc(inp["e_ssd_norm"])
        m["e_conf_conv_w"] = np.ascontiguousarray(np.asarray(inp["e_conf_conv_w"], np.float32).reshape(2, 31, 8, 128).transpose(0, 3, 2, 1))
        for nm in ("e_conf_conv_b", "e_conf_ln_g", "e_conf_ln_b"):
            m[nm] = np.ascontiguousarray(np.asarray(inp[nm], np.float32).reshape(2, 8, 128).transpose(0, 2, 1))
        m["e_w_out"] = np.ascontiguousarray(inp["e_w_out"], dtype=np.float32)
    if any(l[0] == "o" for l in layers):
        import ml_dtypes
        m["o_norm"] = bc(inp["o_norm"])
        m["o_w_in"] = np.ascontiguousarray(inp["o_w_in"], dtype=np.float32)
        m["o_gate_bias"] = bc(inp["o_gate_bias"])
        for nm in ("k", "v"):
            m["o_peT_" + nm] = np.ascontiguousarray(np.asarray(inp["o_cmp_pe_" + nm], np.float32).transpose(0, 2, 1))
            m["o_cmp_w1_" + nm] = np.ascontiguousarray(inp["o_cmp_w1_" + nm], dtype=np.float32)
            m["o_cmp_w2_" + nm] = np.ascontiguousarray(inp["o_cmp_w2_" + nm], dtype=np.float32)
        m["o_w_out"] = np.ascontiguousarray(inp["o_w_out"], dtype=np.float32)
        p = np.arange(128)[:, None]
        xx = np.arange(128)[None, :] - 64
        off = (p >= 64).astype(np.int64)
        A = np.zeros((128, 128), np.float32)
        A[(xx == off) | (xx == off - 1)] = 1.0e6
        A[xx > off] = -1.0e9
        m["o_amask"] = A
        E = (np.arange(64)[:, None] == (np.arange(S)[None, :] // 64)).astype(np.float32)
        m["o_emat"] = E.astype(ml_dtypes.bfloat16)
        jj = np.arange(10)[:, None]
        Mq = np.where(jj > (np.arange(128)[None, :] + 1) // 16, NEG, 0.0)
        Bd = (np.arange(288)[None, :] == jj + 256).astype(np.float32)
        m["o_cmsk"] = np.concatenate([Mq, Bd], axis=1).astype(ml_dtypes.bfloat16)
    return m


LAYERS = [("e", 0), ("o", 0), ("e", 1), ("o", 1)]


def kernel(**inputs):
    x = np.asarray(inputs["x"], np.float32)
    B, S, _ = x.shape
    nc = build_program(S, LAYERS)
    shared = host_inputs(inputs, LAYERS, S)
    in_maps = []
    for b in range(B):
        mm = dict(shared)
        mm["x"] = np.ascontiguousarray(x[b])
        in_maps.append(mm)
    res = run_bass_kernel_spmd(nc, in_maps, core_ids=list(range(B)))
    return np.stack([np.asarray(r["out"], np.float32) for r in res.results], axis=0)
```

```python
import numpy as np
from contextlib import ExitStack
import concourse.bass as bass
import concourse.mybir as mybir
from concourse.bass_utils import run_bass_kernel_spmd

F32 = mybir.dt.float32
BF16 = mybir.dt.bfloat16
AF = mybir.ActivationFunctionType
ALU = mybir.AluOpType
AX = mybir.AxisListType

D = 1024
E_IN = 5648
O_IN = 3632
EPS = 1e-6
NEG = -30000.0
ODD_STOP = None
ODD_DBG = 3


class Buf:
    __slots__ = ("name", "w", "r", "psum")

    def __init__(self, name, psum=False):
        self.name = name
        self.psum = psum
        self.w = {}
        self.r = {}


class KB:
    def __init__(self, nc, es, n_dsem=90):
        self.nc = nc
        self.es = es
        self.engs = {"pe": nc.tensor, "act": nc.scalar, "dve": nc.vector, "pool": nc.gpsimd, "sp": nc.sync}
        self.esem = {e: es.enter_context(nc.semaphore("s_" + e)) for e in self.engs}
        self.ecnt = {e: 0 for e in self.engs}
        self.dsem = [es.enter_context(nc.semaphore("d%d" % i)) for i in range(n_dsem)]
        self.dcnt = [0] * n_dsem
        self.dnext = 0
        self.seen = {e: {} for e in self.engs}
        self.nbuf = 0

    def buf(self, name=None):
        self.nbuf += 1
        return Buf(name or ("b%d" % self.nbuf))

    def sb(self, name, shape, dt):
        self.nbuf += 1
        name = "%s_u%d" % (name, self.nbuf)
        t = self.es.enter_context(self.nc.sbuf_tensor(name, list(shape), dt))
        return t, Buf(name)

    def _wait(self, e, key, val):
        if self.seen[e].get(key, 0) >= val:
            return
        sem = self.esem[key] if isinstance(key, str) else self.dsem[key]
        self.engs[e].wait_ge(sem, val)
        self.seen[e][key] = val

    def _deps(self, e, reads, writes, waw):
        for b in reads:
            for k, v in b.w.items():
                self._wait(e, k, v)
            if b.psum:
                for k, v in b.r.items():
                    if k != e:
                        self._wait(e, k, v)
        for b in writes:
            for k, v in b.w.items():
                if k == e or waw:
                    continue
                self._wait(e, k, v)
            for k, v in b.r.items():
                if k == e:
                    continue
                self._wait(e, k, v)

    def op(self, e, reads, writes, fn, waw=False):
        self._deps(e, reads, writes, waw)
        ins = fn(self.engs[e])
        self.ecnt[e] += 1
        c = self.ecnt[e]
        ins.then_inc(self.esem[e], 1)
        for b in reads:
            b.r[e] = c
        for b in writes:
            if waw:
                b.w[e] = c
            else:
                b.w = {e: c}
                b.r = {}
        return ins

    def dma(self, e, reads, writes, out, in_, waw=False, **kw):
        if e == "pool":
            self.pool_q = getattr(self, "pool_q", [])
            if len(self.pool_q) >= 6:
                k_, v_ = self.pool_q.pop(0)
                self._wait(e, k_, v_)
        i = self.dnext
        self.dnext = (i + 1) % len(self.dsem)
        if self.dcnt[i] > 0:
            self._wait(e, i, self.dcnt[i])
        self._deps(e, reads, writes, waw)
        self.dcnt[i] += 16
        v = self.dcnt[i]
        self.engs[e].dma_start(out=out, in_=in_, **kw).then_inc(self.dsem[i], 16)
        if e == "pool":
            self.pool_q.append((i, v))
        for b in reads:
            b.r[i] = v
        for b in writes:
            if waw:
                b.w[i] = v
            else:
                b.w = {i: v}
                b.r = {}

    def barrier(self):
        for e in self.engs:
            for e2 in self.engs:
                if e2 != e and self.ecnt[e2] > 0:
                    self._wait(e, e2, self.ecnt[e2])
            for i, v in enumerate(self.dcnt):
                if v > 0:
                    self._wait(e, i, v)

    def finish(self, bufs):
        for b in bufs:
            for k, v in b.w.items():
                self._wait("sp", k, v)


def build_program(S, layers, debug_h=False):
    nc = bass.Bass("TRN2", target_bir_lowering=False)
    NT = S // 128
    NB = S // 512

    def din(name, shape, dt=F32):
        return nc.dram_tensor(name, list(shape), dt, kind="ExternalInput").ap()

    x_in = din("x", [S, D])
    out_d = nc.dram_tensor("out", [S, D], F32, kind="ExternalOutput").ap()
    hD = nc.dram_tensor("h_scr", [S, D], F32, kind="Internal").ap()
    if debug_h:
        dbg_d = nc.dram_tensor("dbg", [128, 256], F32, kind="ExternalOutput").ap()
    cst = din("consts_f32", [128, 5 * 128])
    cstb = din("consts_bf16", [128, 2 * 128], BF16)
    final_norm = din("final_norm", [128, D])

    n_even = sum(1 for l in layers if l[0] == "e")
    n_odd = sum(1 for l in layers if l[0] == "o")
    ew = {}
    if n_even:
        ew = dict(
            norm=din("e_norm", [2, 128, D]), w_in=din("e_w_in", [2, D, E_IN]),
            conv_w=din("e_ssd_conv_w", [2, 128, 12, 4]), conv_b=din("e_ssd_conv_b", [2, 128, 12]),
            dt_bias=din("e_dt_bias", [2, 128, 16]), a_log=din("e_a_log", [2, 128, 16]),
            d_skip=din("e_d_skip", [2, 128, 16]), ssd_norm=din("e_ssd_norm", [2, 128, D]),
            cconv_w=din("e_conf_conv_w", [2, 128, 8, 31]), cconv_b=din("e_conf_conv_b", [2, 128, 8]),
            ln_g=din("e_conf_ln_g", [2, 128, 8]), ln_b=din("e_conf_ln_b", [2, 128, 8]),
            w_out=din("e_w_out", [2, 2048, D]),
        )
        ew["w_in_bf"] = nc.dram_tensor("e_w_in_bf", [2, D, E_IN], BF16, kind="Internal").ap()
        ew["w_out_bf"] = nc.dram_tensor("e_w_out_bf", [2, 2048, D], BF16, kind="Internal").ap()

    ow = {}
    if n_odd:
        ow = dict(
            norm=din("o_norm", [2, 128, D]), w_in=din("o_w_in", [2, D, O_IN]), gate_bias=din("o_gate_bias", [2, 128, 48]),
            peT_k=din("o_peT_k", [2, 64, 32]), w1_k=din("o_cmp_w1_k", [2, 2048, 256]), w2_k=din("o_cmp_w2_k", [2, 256, 64]),
            peT_v=din("o_peT_v", [2, 64, 32]), w1_v=din("o_cmp_w1_v", [2, 2048, 256]), w2_v=din("o_cmp_w2_v", [2, 256, 64]),
            w_out=din("o_w_out", [2, D, D]), amask=din("o_amask", [128, 128]), emat=din("o_emat", [64, S], BF16), cmsk=din("o_cmsk", [10, 128 + 288], BF16),
        )
        ow["w_in_bf"] = nc.dram_tensor("o_w_in_bf", [2, D, O_IN], BF16, kind="Internal").ap()
        ow["w_out_bf"] = nc.dram_tensor("o_w_out_bf", [2, D, D], BF16, kind="Internal").ap()
        ow["qT_D"] = nc.dram_tensor("o_qT_D", [1024, S], BF16, kind="Internal").ap()
        ow["kT_D"] = nc.dram_tensor("o_kT_D", [4, 256, S], BF16, kind="Internal").ap()
        ow["sz_D"] = nc.dram_tensor("o_sz_D", [S, D], BF16, kind="Internal").ap()

    with ExitStack() as es:
        kb = KB(nc, es)
        cf, cf_b = kb.sb("cf", [128, 5 * 128], F32)
        cb, cb_b = kb.sb("cb", [128, 2 * 128], BF16)
        kb.dma("sp", [], [cf_b], cf[:], cst[:, :])
        kb.dma("sp", [], [cb_b], cb[:], cstb[:, :])
        ident_f = cf[:, 0:128]
        L_f = cf[:, 128:256]
        U_f = cf[:, 256:384]
        tri_f = cf[:, 384:512]
        ones_f = cf[:, 512:640]
        ident_b = cb[:, 0:128]

        PS = []
        for i in range(8):
            t = es.enter_context(nc.psum_tensor("ps%d" % i, [128, 512], F32))
            PS.append((t, Buf("ps%d" % i, psum=True)))

        h_b = [kb.buf("hD%d" % i) for i in range(NT)]
        outs = []

        state = {"src": x_in}

        WB = [kb.sb("wb%d" % i, [128, 8, 512], BF16) for i in range(3)]
        wb_i = [0]

        wdram_b = {}

        def load_w(w_bf_ap, r0, c0, ncols, wkey=None):
            t, b = WB[wb_i[0] % len(WB)]
            wb_i[0] += 1
            src = w_bf_ap[r0:r0 + 1024, c0:c0 + ncols].rearrange("(k p) c -> p k c", p=128)
            kb.dma("sp", [wdram_b[wkey]] if wkey else [], [b], t[:, :, 0:ncols], src)
            return t, b

        def cast_weight(src_ap, dst_ap, rows, cols, nm):
            wdram_b[nm] = kb.buf(nm)
            stg = [kb.sb("%s_s%d" % (nm, i), [128, 2048], F32) for i in range(2)]
            stb = [kb.sb("%s_b%d" % (nm, i), [128, 2048], BF16) for i in range(2)]
            i = 0
            for r in range(0, rows, 128):
                for c in range(0, cols, 2048):
                    w = min(2048, cols - c)
                    s, sb_ = stg[i % 2]
                    d, db_ = stb[i % 2]
                    kb.dma("sp", [], [sb_], s[:, 0:w], src_ap[r:r + 128, c:c + w])
                    kb.op("pool", [sb_], [db_], lambda e: e.tensor_copy(out=d[:, 0:w], in_=s[:, 0:w]))
                    kb.dma("pool", [db_], [wdram_b[nm]], dst_ap[r:r + 128, c:c + w], d[:, 0:w], waw=True)
                    i += 1

        def even_layer(li):
            src = state["src"]
            norm_t, norm_b = kb.sb("e_norm_t", [128, D], F32)
            ssdn_t, ssdn_b = kb.sb("e_ssdn_t", [128, D], F32)
            small, small_b = kb.sb("e_small", [128, 64], F32)
            cw, cw_b = kb.sb("e_cw", [128, 12, 4], F32)
            cbias, cbias_b = kb.sb("e_cb", [128, 12], F32)
            ccw, ccw_b = kb.sb("e_ccw", [128, 8, 31], F32)
            cc4, cc4_b = kb.sb("e_cc4", [128, 3, 8], F32)
            kb.dma("sp", [], [norm_b], norm_t[:], ew["norm"][li])
            kb.dma("sp", [], [ssdn_b], ssdn_t[:], ew["ssd_norm"][li])
            kb.dma("sp", [], [small_b], small[:, 0:16], ew["dt_bias"][li])
            kb.dma("sp", [], [small_b], small[:, 16:32], ew["a_log"][li], waw=True)
            kb.dma("sp", [], [small_b], small[:, 32:48], ew["d_skip"][li], waw=True)
            kb.dma("sp", [], [cw_b], cw[:], ew["conv_w"][li])
            kb.dma("sp", [], [cbias_b], cbias[:], ew["conv_b"][li])
            kb.dma("sp", [], [ccw_b], ccw[:], ew["cconv_w"][li])
            kb.dma("sp", [], [cc4_b], cc4[:, 0, :], ew["cconv_b"][li])
            kb.dma("sp", [], [cc4_b], cc4[:, 1, :], ew["ln_g"][li], waw=True)
            kb.dma("sp", [], [cc4_b], cc4[:, 2, :], ew["ln_b"][li], waw=True)
            kb.op("act", [small_b], [small_b], lambda e: e.activation(out=small[:, 16:32], in_=small[:, 16:32], func=AF.Exp))
            kb.op("dve", [small_b], [small_b], lambda e: e.tensor_scalar(out=small[:, 16:32], in0=small[:, 16:32], scalar1=-1.0, scalar2=None, op0=ALU.mult))
            dtb = small[:, 0:16]
            a_t = small[:, 16:32]
            dsk = small[:, 32:48]
            w_in_bf = ew["w_in_bf"][li]
            w_out_bf = ew["w_out_bf"][li]

            wdt, wdt_b = kb.sb("e_wdt", [128, 8, 16], BF16)
            kb.dma("sp", [wdram_b["cwi%d" % li]], [wdt_b], wdt[:], w_in_bf[0:1024, 2560:2576].rearrange("(k p) c -> p k c", p=128))
            hblk, hblk_b = kb.sb("e_hblk", [128, 4, D], F32)
            hn, hn_b = kb.sb("e_hn", [128, D], BF16)
            st4, st4_b = kb.sb("e_st4", [128, 8], F32)
            hnT, hnT_b = kb.sb("e_hnT", [128, 8, 512], BF16)
            sz, sz_b = kb.sb("e_sz", [128, 4, D], BF16)
            xh, xh_b = kb.sb("e_xh", [128, 12, 516], BF16)
            xh_bs = [kb.buf("xh%d" % j) for j in range(12)]
            cacc, cacc_b = kb.sb("e_cacc", [128, 512], F32)
            caccp, caccp_b = kb.sb("e_caccp", [128, 512], F32)
            uats = [kb.sb("e_uat%d" % i, [128, 512], F32) for i in range(2)]
            xa, xa_b = kb.sb("e_xa", [128, 12, 512], BF16)
            xa_bs = [kb.buf("xa%d" % j) for j in range(12)]
            dt_t, dt_b = kb.sb("e_dt", [128, 4, 16], F32)
            dta_t, dta_b = kb.sb("e_dta", [128, 4, 16], F32)
            sp1, sp1_b = kb.sb("e_sp1", [128, 64], F32)
            sp2, sp2_b = kb.sb("e_sp2", [128, 64], F32)
            xt, xt_b = kb.sb("e_xt", [128, D], BF16)
            Bt, Bt_b = kb.sb("e_Bt", [128, 2, 128], BF16)
            cums, cums_b = kb.sb("e_cums", [128, 64], F32)
            CBm, CBm_b = kb.sb("e_CBm", [128, 2, 128], F32)
            lh = [kb.sb("e_lh%d" % i, [128, 128], F32) for i in range(8)]
            dec = [kb.sb("e_dec%d" % i, [128, 512], F32) for i in range(2)]
            wT = [kb.sb("e_wT%d" % i, [128, 128], BF16) for i in range(4)]
            y1, y1_b = kb.sb("e_y1", [128, D], F32)
            y2, y2_b = kb.sb("e_y2", [128, D], F32)
            junk, junk_b = y2, y2_b
            xw, xw_b = kb.sb("e_xw", [128, D], BF16)
            st32, st32_b = kb.sb("e_st32", [128, 2, 512], F32)
            stbf, stbf_b = kb.sb("e_stbf", [128, 2, 512], BF16)
            ya, ya_b = kb.sb("e_ya", [128, D], BF16)
            yaT, yaT_b = kb.sb("e_yaT", [128, 8, 512], BF16)
            ybT, ybT_b = kb.sb("e_ybT", [128, 8, 512], BF16)
            uh, uh_b = kb.sb("e_uh", [128, 8, 542], BF16)
            dgs = [kb.sb("e_dg%d" % i, [128, 31, 128], BF16) for i in range(2)]
            uh_bs = [kb.buf("uh%d" % j) for j in range(8)]
            uc, uc_b = kb.sb("e_uc", [128, 8, 512], F32)
            uc_bs = [kb.buf("uc%d" % j) for j in range(8)]
            sg, sg_b = kb.sb("e_sg", [128, 512], F32)
            sq, sq_b = kb.sb("e_sq", [128, 512], F32)
            lnm, lnm_b = kb.sb("e_lnm", [128, 512], F32)
            lnr, lnr_b = kb.sb("e_lnr", [128, 512], F32)
            un, un_b = kb.sb("e_un", [128, 512], F32)
            s1, s1_b = kb.sb("e_s1", [128, 512], F32)
            szc, szc_b = kb.sb("e_szc", [128, 512], F32)

            kb.op("pool", [], [st32_b], lambda e: e.memset(st32[:], 0.0))
            kb.op("pool", [], [stbf_b], lambda e: e.memset(stbf[:], 0.0))
            for j in range(12):
                kb.op("pool", [], [xh_bs[j]], lambda e, j=j: e.memset(xh[:, j, 0:3], 0.0))
            for j in range(8):
                kb.op("pool", [], [uh_bs[j]], lambda e, j=j: e.memset(uh[:, j, 0:30], 0.0))

            def rstd_from_ss(ss_ap, n, out_ap, bufs):
                kb.op("dve", bufs, bufs, lambda e: e.tensor_scalar(out=out_ap, in0=ss_ap, scalar1=1.0 / n, scalar2=EPS, op0=ALU.mult, op1=ALU.add))
                kb.op("act", bufs, bufs, lambda e: e.activation(out=out_ap, in_=out_ap, func=AF.Sqrt))
                kb.op("dve", bufs, bufs, lambda e: e.reciprocal(out=out_ap, in_=out_ap))

            def proj_fm(wt, wb_, col0, pt, pb_):
                for k in range(8):
                    kb.op("pe", [wb_, hnT_b], [pb_], lambda e, k=k: e.matmul(pt[:], lhsT=wt[:, k, col0:col0 + 128], rhs=hnT[:, k, :], start=(k == 0), stop=(k == 7)))

            for blk in range(NB):
                t0 = blk * 512
                def emit_diag(j, kks=range(31)):
                    dgt, dgb = dgs[j % 2]
                    for kk in kks:
                        kb.op("pool", [cf_b, ccw_b], [dgb], lambda e, kk=kk: e.tensor_scalar(out=dgt[:, kk, :], in0=ident_f, scalar1=ccw[:, j, kk:kk + 1], scalar2=0.0, op0=ALU.mult, op1=ALU.add), waw=True)

                emit_diag(0)
                for tt in range(4):
                    ti = blk * 4 + tt
                    kb.dma("sp", [h_b[ti]], [hblk_b], hblk[:, tt, :], src[t0 + tt * 128:t0 + (tt + 1) * 128, :], waw=(tt > 0))
                for tt in range(4):
                    kb.op("act", [hblk_b], [junk_b, st4_b], lambda e, tt=tt: e.activation(out=junk[:], in_=hblk[:, tt, :], func=AF.Square, accum_out=st4[:, tt:tt + 1]))
                rstd_from_ss(st4[:, 0:4], float(D), st4[:, 4:8], [st4_b])
                for tt in range(4):
                    kb.op("dve", [hblk_b, st4_b, norm_b], [hn_b], lambda e, tt=tt: e.scalar_tensor_tensor(out=hn[:], in0=hblk[:, tt, :], scalar=st4[:, 4 + tt:5 + tt], in1=norm_t[:], op0=ALU.mult, op1=ALU.mult))
                    pt, pb_ = PS[2]
                    ptb = pt[:].bitcast(BF16)
                    for k in range(8):
                        kb.op("pe", [hn_b, cb_b], [pb_], lambda e, k=k: e.transpose(out=ptb[:, k * 128:(k + 1) * 128], in_=hn[:, k * 128:(k + 1) * 128], identity=ident_b))
                    kb.op("act", [pb_], [hnT_b], lambda e, tt=tt: e.activation(out=hnT[:, :, tt * 128:(tt + 1) * 128], in_=ptb.rearrange("p (k t) -> p k t", k=8), func=AF.Copy), waw=(tt > 0))

                pi = 0
                for half in range(2):
                    wt, wb_ = load_w(w_in_bf, 0, half * 512, 512, wkey="cwi%d" % li)
                    for tt in range(4):
                        pt, pb_ = PS[(0, 1, 4, 5)[pi % 4]]
                        pi += 1
                        for k in range(8):
                            kb.op("pe", [wb_, hnT_b], [pb_], lambda e, k=k, tt=tt: e.matmul(pt[:], lhsT=hnT[:, k, tt * 128:(tt + 1) * 128], rhs=wt[:, k, :], start=(k == 0), stop=(k == 7)))
                        kb.op("act", [pb_], [sz_b], lambda e, tt=tt, half=half: e.activation(out=sz[:, tt, half * 512:(half + 1) * 512], in_=pt[:], func=AF.Silu), waw=True)

                pend = []
                dacc = [(cacc, cacc_b), (lnm, lnm_b), (lnr, lnr_b)]
                dai = 0
                for sl in range(3):
                    wt, wb_ = load_w(w_in_bf, 0, 1024 + sl * 512, 512, wkey="cwi%d" % li)
                    for jj in range(4):
                        j = sl * 4 + jj
                        pt, pb_ = PS[(0, 1, 4, 5)[pi % 4]]
                        pi += 1
                        proj_fm(wt, wb_, jj * 128, pt, pb_)
                        kb.op("act", [pb_], [xh_bs[j]], lambda e, j=j: e.activation(out=xh[:, j, 3:515], in_=pt[:], func=AF.Copy))
                        if j % 3 == 0:
                            ca, ca_b = caccp, caccp_b
                            t2, t2_b = un, un_b
                            kb.op("pool", [xh_bs[j], cw_b], [ca_b], lambda e, j=j: e.tensor_scalar(out=ca[:], in0=xh[:, j, 0:512], scalar1=cw[:, j, 0:1], scalar2=0.0, op0=ALU.mult, op1=ALU.add))
                            for kk in range(1, 4):
                                kb.op("pool", [xh_bs[j], cw_b], [t2_b], lambda e, j=j, kk=kk: e.tensor_scalar(out=t2[:], in0=xh[:, j, kk:kk + 512], scalar1=cw[:, j, kk:kk + 1], scalar2=0.0, op0=ALU.mult, op1=ALU.add))
                                kb.op("pool", [t2_b, ca_b], [ca_b], lambda e: e.tensor_tensor(out=ca[:], in0=ca[:], in1=t2[:], op=ALU.add))
                            delay = 2
                        else:
                            ca, ca_b = dacc[dai % 3]
                            dai += 1
                            kb.op("dve", [xh_bs[j], cw_b], [ca_b], lambda e, j=j, ca=ca: e.tensor_scalar(out=ca[:], in0=xh[:, j, 0:512], scalar1=cw[:, j, 0:1], scalar2=None, op0=ALU.mult))
                            for kk in range(1, 4):
                                kb.op("dve", [xh_bs[j], cw_b, ca_b], [ca_b], lambda e, j=j, kk=kk, ca=ca: e.scalar_tensor_tensor(out=ca[:], in0=xh[:, j, kk:kk + 512], scalar=cw[:, j, kk:kk + 1], in1=ca[:], op0=ALU.mult, op1=ALU.add))
                            delay = 1

                        def fin(j=j, ca=ca, ca_b=ca_b):
                            kb.op("act", [ca_b, cbias_b], [xa_bs[j]], lambda e: e.activation(out=xa[:, j, :], in_=ca[:], func=AF.Silu, bias=cbias[:, j:j + 1]))
                            kb.op("pool", [xh_bs[j]], [xh_bs[j]], lambda e: e.tensor_copy(out=xh[:, j, 0:3], in_=xh[:, j, 512:515]))
                        pend.append((j + delay, fin))
                        for it in list(pend):
                            if it[0] <= j:
                                it[1]()
                                pend.remove(it)
                for it in pend:
                    it[1]()

                wt, wb_ = wdt, wdt_b
                pt, pb_ = PS[pi % 2]
                pi += 1
                for tt in range(4):
                    for k in range(8):
                        kb.op("pe", [wb_, hnT_b], [pb_], lambda e, k=k, tt=tt: e.matmul(pt[:, tt * 16:(tt + 1) * 16], lhsT=hnT[:, k, tt * 128:(tt + 1) * 128], rhs=wt[:, k, 0:16], start=(k == 0), stop=(k == 7)), waw=(tt > 0 or k > 0))
                kb.op("act", [pb_], [sp1_b], lambda e: e.activation(out=sp1[:], in_=pt[:, 0:64], func=AF.Copy))
                kb.op("dve", [sp1_b, small_b], [sp1_b], lambda e: e.tensor_tensor(out=sp1[:].rearrange("p (t h) -> p t h", t=4), in0=sp1[:].rearrange("p (t h) -> p t h", t=4), in1=dtb.unsqueeze(1).broadcast_to([128, 4, 16]), op=ALU.add))
                kb.op("act", [sp1_b], [sp2_b], lambda e: e.activation(out=sp2[:], in_=sp1[:], func=AF.Abs))
                kb.op("act", [sp2_b], [sp2_b], lambda e: e.activation(out=sp2[:], in_=sp2[:], func=AF.Exp, scale=-1.0))
                kb.op("act", [sp2_b], [sp2_b], lambda e: e.activation(out=sp2[:], in_=sp2[:], func=AF.Ln, bias=1.0))
                kb.op("dve", [sp1_b, sp2_b], [dt_b], lambda e: e.scalar_tensor_tensor(out=dt_t[:].rearrange("p t h -> p (t h)"), in0=sp1[:], scalar=0.0, in1=sp2[:], op0=ALU.max, op1=ALU.add))
                kb.op("dve", [dt_b, small_b], [dta_b], lambda e: e.tensor_tensor(out=dta_t[:], in0=dt_t[:], in1=a_t.unsqueeze(1).broadcast_to([128, 4, 16]), op=ALU.mult))

                for sl in range(2):
                    wa, wab = load_w(w_in_bf, 0, 2576 + sl * 512, 512, wkey="cwi%d" % li)
                    wu, wub = load_w(w_in_bf, 0, 3600 + sl * 512, 512, wkey="cwi%d" % li)
                    for jj in range(4):
                        j = sl * 4 + jj
                        pa, pab = PS[(0, 4)[j % 2]]
                        pu, pub = PS[(1, 5)[j % 2]]
                        proj_fm(wa, wab, jj * 128, pa, pab)
                        proj_fm(wu, wub, jj * 128, pu, pub)
                        sgt, sgtb = (sg, sg_b) if j % 2 == 0 else (szc, szc_b)
                        kb.op("act", [pub], [sgtb], lambda e: e.activation(out=sgt[:], in_=pu[:], func=AF.Sigmoid))
                        uat, uatb = uats[j % 2]
                        kb.op("act", [pab], [uatb], lambda e: e.activation(out=uat[:], in_=pa[:], func=AF.Copy))
                        kb.op("dve", [uatb, sgtb], [uh_bs[j]], lambda e, j=j: e.tensor_tensor(out=uh[:, j, 30:542], in0=uat[:], in1=sgt[:], op=ALU.mult))

                def emit_conv_mm(j, kk):
                    dgt, dgb = dgs[j % 2]
                    cp, cpb = PS[j % 2]
                    if j + 1 < 8:
                        emit_diag(j + 1, [kk])
                    kb.op("pe", [dgb, uh_bs[j]], [cpb], lambda e: e.matmul(cp[:], lhsT=dgt[:, kk, :], rhs=uh[:, j, kk:kk + 512], start=(kk == 0), stop=(kk == 30)))
                    if kk == 30:
                        kb.op("act", [cpb, cc4_b], [uc_bs[j]], lambda e: e.activation(out=uc[:, j, :], in_=cp[:], func=AF.Identity, bias=cc4[:, 0, j:j + 1]))
                        kb.op("dve", [uh_bs[j]], [uh_bs[j]], lambda e: e.tensor_copy(out=uh[:, j, 0:30], in_=uh[:, j, 512:542]))

                fillers = [(j, kk) for j in range(8) for kk in range(31)]

                def fill(n):
                    for _ in range(min(n, len(fillers))):
                        j_, kk_ = fillers.pop(0)
                        emit_conv_mm(j_, kk_)

                for tt in range(4):
                    cs = slice(tt * 128, (tt + 1) * 128)
                    pt, pb_ = PS[2]
                    ptb = pt[:].bitcast(BF16)
                    for j in range(8):
                        kb.op("pe", [xa_bs[j], cb_b], [pb_], lambda e, j=j: e.transpose(out=ptb[:, j * 128:(j + 1) * 128], in_=xa[:, j, cs], identity=ident_b))
                    kb.op("act", [pb_], [xt_b], lambda e: e.activation(out=xt[:], in_=ptb, func=AF.Copy))
                    pt3, pb3 = PS[3]
                    pt3b = pt3[:].bitcast(BF16)
                    for g in range(2):
                        kb.op("pe", [xa_bs[8 + g], cb_b], [pb3], lambda e, g=g: e.transpose(out=pt3b[:, g * 128:(g + 1) * 128], in_=xa[:, 8 + g, cs], identity=ident_b))
                    kb.op("act", [pb3], [Bt_b], lambda e: e.activation(out=Bt[:], in_=pt3b[:, 0:256].rearrange("p (g n) -> p g n", g=2), func=AF.Copy))
                    fill(8)
                    pt, pb_ = PS[2]
                    kb.op("pe", [dta_b, cf_b], [pb_], lambda e: e.matmul(pt[:, 0:16], lhsT=L_f, rhs=dta_t[:, tt, :], start=True, stop=True))
                    kb.op("pe", [dta_b, cf_b], [pb_], lambda e: e.matmul(pt[:, 16:32], lhsT=ones_f, rhs=dta_t[:, tt, :], start=True, stop=True))
                    fill(6)
                    kb.op("act", [pb_], [cums_b], lambda e: e.activation(out=cums[:, 0:32], in_=pt[:, 0:32], func=AF.Exp))
                    kb.op("dve", [pb_], [cums_b], lambda e: e.tensor_copy(out=cums[:, 48:64], in_=pt[:, 0:16]))
                    kb.op("dve", [pb_, cums_b], [cums_b], lambda e: e.tensor_tensor(out=cums[:, 48:64], in0=pt[:, 16:32], in1=cums[:, 48:64], op=ALU.subtract))
                    kb.op("act", [cums_b], [cums_b], lambda e: e.activation(out=cums[:, 32:48], in_=cums[:, 48:64], func=AF.Exp))
                    kb.op("dve", [cums_b, dt_b], [cums_b], lambda e: e.tensor_tensor(out=cums[:, 32:48], in0=cums[:, 32:48], in1=dt_t[:, tt, :], op=ALU.mult))
                    ecum = cums[:, 0:16]
                    etot = cums[:, 16:32]
                    toend = cums[:, 32:48]
                    for g in range(2):
                        pt3, pb3 = PS[3]
                        kb.op("pe", [xa_bs[8 + g], xa_bs[10 + g]], [pb3], lambda e, g=g: e.matmul(pt3[:, 0:128], lhsT=xa[:, 8 + g, cs], rhs=xa[:, 10 + g, cs], start=True, stop=True))
                        kb.op("dve", [pb3, cf_b], [CBm_b], lambda e, g=g: e.tensor_tensor(out=CBm[:, g, :], in0=pt3[:, 0:128], in1=tri_f, op=ALU.mult), waw=(g > 0))
                    for g in range(2):
                        pt, pb_ = PS[6 + g]
                        kb.op("pe", [xa_bs[10 + g], stbf_b], [pb_], lambda e, g=g: e.matmul(pt[:], lhsT=xa[:, 10 + g, cs], rhs=stbf[:, g, :], start=True, stop=True))
                        kb.op("dve", [pb_, cums_b], [y1_b], lambda e, g=g: e.tensor_tensor(out=y1[:, g * 512:(g + 1) * 512].rearrange("p (h d) -> p h d", h=8), in0=pt[:].rearrange("p (h d) -> p h d", h=8), in1=ecum[:, g * 8:(g + 1) * 8].unsqueeze(2).broadcast_to([128, 8, 64]), op=ALU.mult), waw=(g > 0))
                    def stage_a(hq):
                        sgp, sgb = PS[4 + hq % 2]
                        dct, dcb = dec[hq % 2]
                        for hh in range(4):
                            h = hq * 4 + hh
                            lt, lb = lh[h % 8]
                            kb.op("pool", [cf_b, dta_b], [lb], lambda e, h=h: e.tensor_scalar(out=lt[:], in0=U_f, scalar1=dta_t[:, tt, h:h + 1], scalar2=0.0, op0=ALU.mult, op1=ALU.add))
                            kb.op("pe", [lb, cf_b], [sgb], lambda e, hh=hh: e.matmul(sgp[:, hh * 128:(hh + 1) * 128], lhsT=lt[:], rhs=L_f, start=True, stop=True), waw=(hh > 0))
                        kb.op("act", [sgb], [dcb], lambda e: e.activation(out=dct[:], in_=sgp[:], func=AF.Exp))

                    def stage_b(hq):
                        dct, dcb = dec[hq % 2]
                        for hh in range(4):
                            h = hq * 4 + hh
                            g = h // 8
                            wt_, wtb = wT[h % 4]
                            kb.op("dve", [dcb, dt_b, CBm_b], [wtb], lambda e, h=h, hh=hh, g=g: e.scalar_tensor_tensor(out=wt_[:], in0=dct[:, hh * 128:(hh + 1) * 128], scalar=dt_t[:, tt, h:h + 1], in1=CBm[:, g, :], op0=ALU.mult, op1=ALU.mult))
                            pt, pb_ = PS[6 + g]
                            hl = h % 8
                            kb.op("pe", [wtb, xt_b], [pb_], lambda e, h=h, hl=hl: e.matmul(pt[:, hl * 64:(hl + 1) * 64], lhsT=wt_[:], rhs=xt[:, h * 64:(h + 1) * 64], start=True, stop=True), waw=(hl > 0))

                    stage_a(0)
                    fill(4)
                    stage_a(1)
                    fill(4)
                    stage_b(0)
                    stage_a(2)
                    fill(6)
                    stage_b(1)
                    stage_a(3)
                    fill(6)
                    stage_b(2)
                    fill(6)
                    stage_b(3)
                    fill(6)
                    for g in range(2):
                        pt, pb_ = PS[6 + g]
                        kb.op("dve", [pb_, y1_b], [y1_b], lambda e, g=g: e.tensor_tensor(out=y1[:, g * 512:(g + 1) * 512], in0=pt[:], in1=y1[:, g * 512:(g + 1) * 512], op=ALU.add))
                    kb.op("dve", [xt_b, small_b], [y2_b], lambda e: e.tensor_tensor(out=y2[:].rearrange("p (h d) -> p h d", h=16), in0=xt[:].rearrange("p (h d) -> p h d", h=16), in1=dsk.unsqueeze(2).broadcast_to([128, 16, 64]), op=ALU.mult))
                    kb.op("dve", [y1_b, y2_b], [y1_b], lambda e: e.tensor_tensor(out=y1[:], in0=y1[:], in1=y2[:], op=ALU.add))
                    kb.op("dve", [xt_b, cums_b], [xw_b], lambda e: e.tensor_tensor(out=xw[:].rearrange("p (h d) -> p h d", h=16), in0=xt[:].rearrange("p (h d) -> p h d", h=16), in1=toend.unsqueeze(2).broadcast_to([128, 16, 64]), op=ALU.mult))
                    for g in range(2):
                        pt3, pb3 = PS[3]
                        kb.op("pe", [Bt_b, xw_b], [pb3], lambda e, g=g: e.matmul(pt3[:], lhsT=Bt[:, g, :], rhs=xw[:, g * 512:(g + 1) * 512], start=True, stop=True))
                        kb.op("dve", [st32_b, cums_b], [st32_b], lambda e, g=g: e.tensor_tensor(out=st32[:, g, :].rearrange("p (h d) -> p h d", h=8), in0=st32[:, g, :].rearrange("p (h d) -> p h d", h=8), in1=etot[:, g * 8:(g + 1) * 8].unsqueeze(2).broadcast_to([128, 8, 64]), op=ALU.mult))
                        kb.op("dve", [pb3, st32_b], [st32_b], lambda e, g=g: e.tensor_tensor(out=st32[:, g, :], in0=pt3[:], in1=st32[:, g, :], op=ALU.add))
                    kb.op("act", [st32_b], [stbf_b], lambda e: e.activation(out=stbf[:], in_=st32[:], func=AF.Copy))
                    fill(8)
                    kb.op("dve", [y1_b, sz_b], [y1_b], lambda e: e.tensor_tensor(out=y1[:], in0=y1[:], in1=sz[:, tt, :], op=ALU.mult))
                    for g in range(2):
                        kb.op("act", [y1_b], [junk_b, st4_b], lambda e, g=g: e.activation(out=junk[:, 0:512], in_=y1[:, g * 512:(g + 1) * 512], func=AF.Square, accum_out=st4[:, g:g + 1]))
                    rstd_from_ss(st4[:, 0:2], 512.0, st4[:, 2:4], [st4_b])
                    for g in range(2):
                        kb.op("dve", [y1_b, st4_b, ssdn_b], [ya_b], lambda e, g=g: e.scalar_tensor_tensor(out=ya[:, g * 512:(g + 1) * 512], in0=y1[:, g * 512:(g + 1) * 512], scalar=st4[:, 2 + g:3 + g], in1=ssdn_t[:, g * 512:(g + 1) * 512], op0=ALU.mult, op1=ALU.mult), waw=(g > 0))
                    pt, pb_ = PS[2]
                    ptb = pt[:].bitcast(BF16)
                    for k in range(8):
                        kb.op("pe", [ya_b, cb_b], [pb_], lambda e, k=k: e.transpose(out=ptb[:, k * 128:(k + 1) * 128], in_=ya[:, k * 128:(k + 1) * 128], identity=ident_b))
                    kb.op("act", [pb_], [yaT_b], lambda e, tt=tt: e.activation(out=yaT[:, :, tt * 128:(tt + 1) * 128], in_=ptb.rearrange("p (k t) -> p k t", k=8), func=AF.Copy), waw=(tt > 0))
                    fill(8)

                fill(len(fillers))
                mean_p, mean_pb = PS[4]
                msq_p, msq_pb = PS[5]
                for j in range(8):
                    sqt, sqtb = (sq, sq_b) if j % 2 == 0 else (s1, s1_b)
                    kb.op("act", [uc_bs[j]], [sqtb], lambda e, j=j: e.activation(out=sqt[:], in_=uc[:, j, :], func=AF.Square))
                    kb.op("pe", [uc_bs[j], cf_b], [mean_pb], lambda e, j=j: e.matmul(mean_p[:], lhsT=ones_f, rhs=uc[:, j, :], start=(j == 0), stop=(j == 7)))
                    kb.op("pe", [sqtb, cf_b], [msq_pb], lambda e, j=j: e.matmul(msq_p[:], lhsT=ones_f, rhs=sqt[:], start=(j == 0), stop=(j == 7)))
                kb.op("act", [mean_pb], [lnm_b], lambda e: e.activation(out=lnm[:], in_=mean_p[:], func=AF.Copy, scale=1.0 / 1024))
                kb.op("act", [lnm_b], [sq_b], lambda e: e.activation(out=sq[:], in_=lnm[:], func=AF.Square))
                kb.op("dve", [msq_pb, sq_b], [lnr_b], lambda e: e.scalar_tensor_tensor(out=lnr[:], in0=msq_p[:], scalar=1.0 / 1024, in1=sq[:], op0=ALU.mult, op1=ALU.subtract))
                kb.op("dve", [lnr_b], [lnr_b], lambda e: e.tensor_scalar(out=lnr[:], in0=lnr[:], scalar1=EPS, scalar2=None, op0=ALU.add))
                kb.op("act", [lnr_b], [lnr_b], lambda e: e.activation(out=lnr[:], in_=lnr[:], func=AF.Sqrt))
                kb.op("dve", [lnr_b], [lnr_b], lambda e: e.reciprocal(out=lnr[:], in_=lnr[:]))
                kb.op("dve", [lnm_b, lnr_b], [lnm_b], lambda e: e.scalar_tensor_tensor(out=lnm[:], in0=lnm[:], scalar=-1.0, in1=lnr[:], op0=ALU.mult, op1=ALU.mult))
                for sl in range(2):
                    wz, wzb = load_w(w_in_bf, 0, 4624 + sl * 512, 512, wkey="cwi%d" % li)
                    for jj in range(4):
                        j = sl * 4 + jj
                        pz, pzb = PS[j % 2]
                        proj_fm(wz, wzb, jj * 128, pz, pzb)
                        kb.op("act", [pzb], [szc_b], lambda e: e.activation(out=szc[:], in_=pz[:], func=AF.Silu))
                        kb.op("dve", [uc_bs[j], lnr_b], [un_b], lambda e, j=j: e.tensor_tensor(out=un[:], in0=uc[:, j, :], in1=lnr[:], op=ALU.mult))
                        kb.op("dve", [un_b, lnm_b], [un_b], lambda e: e.tensor_tensor(out=un[:], in0=un[:], in1=lnm[:], op=ALU.add))
                        kb.op("act", [un_b, cc4_b], [s1_b], lambda e, j=j: e.activation(out=s1[:], in_=un[:], func=AF.Silu, scale=cc4[:, 1, j:j + 1], bias=cc4[:, 2, j:j + 1]))
                        kb.op("dve", [s1_b, szc_b], [ybT_b], lambda e, j=j: e.tensor_tensor(out=ybT[:, j, :], in0=s1[:], in1=szc[:], op=ALU.mult), waw=(j > 0))

                for half in range(2):
                    for kh in range(2):
                        wo, wob = load_w(w_out_bf, kh * 1024, half * 512, 512, wkey="cwo%d" % li)
                        srcT, srcb = (yaT, yaT_b) if kh == 0 else (ybT, ybT_b)
                        for tt in range(4):
                            pt, pb_ = PS[4 + tt]
                            for k in range(8):
                                kb.op("pe", [wob, srcb], [pb_], lambda e, k=k, tt=tt, kh=kh: e.matmul(pt[:], lhsT=srcT[:, k, tt * 128:(tt + 1) * 128], rhs=wo[:, k, :], start=(kh == 0 and k == 0), stop=(kh == 1 and k == 7)))
                    for tt in range(4):
                        pt, pb_ = PS[4 + tt]
                        kb.op("dve", [pb_, hblk_b], [hblk_b], lambda e, tt=tt, half=half: e.tensor_tensor(out=hblk[:, tt, half * 512:(half + 1) * 512], in0=pt[:], in1=hblk[:, tt, half * 512:(half + 1) * 512], op=ALU.add))
                for tt in range(4):
                    ti = blk * 4 + tt
                    kb.dma("pool", [hblk_b], [h_b[ti]], hD[t0 + tt * 128:t0 + (tt + 1) * 128, :], hblk[:, tt, :])
            state["src"] = hD


        def odd_layer(li):
            src = state["src"]
            NSEL = S // 64
            NCMP = S // 16 - 1
            NCT = (NCMP + 127) // 128
            NCP = NCT * 128
            w_in_bf = ow["w_in_bf"][li]
            wk = "owi%d" % li
            ksA, ksA_b = kb.sb("o_ksA", [128, 4, S], BF16)
            kwT, kwT_b = kb.sb("o_kwT", [64, 4, S], BF16)
            vsA, vsA_b = kb.sb("o_vsA", [128, NT, 4, 65], BF16)
            vwA, vwA_b = kb.sb("o_vwA", [128, NT, 4, 65], BF16)
            kcT, kcT_b = kb.sb("o_kcT", [64, 4, NCP], BF16)
            vcA, vcA_b = kb.sb("o_vcA", [128, NCT, 4, 65], BF16)
            gts, gts_b = kb.sb("o_gts", [128, NT, 48], F32)
            wo, wo_b = kb.sb("o_wo", [128, 8, D], BF16)
            norm_t, norm_b = kb.sb("o_norm_t", [128, D], F32)
            gbias, gbias_b = kb.sb("o_gbias", [128, 48], F32)
            Amask, Amask_b = kb.sb("o_Amask", [128, 128], F32)
            kb.dma("sp", [], [norm_b], norm_t[:], ow["norm"][li])
            kb.dma("sp", [], [gbias_b], gbias[:], ow["gate_bias"][li])
            kb.dma("sp", [], [Amask_b], Amask[:], ow["amask"][:, :])
            kb.dma("sp", [wdram_b["owo%d" % li]], [wo_b], wo[:], ow["w_out_bf"][li].rearrange("(k p) c -> p k c", p=128))
            for g in range(4):
                kb.dma("sp", [], [ksA_b], ksA[64:128, g, :], ow["emat"][:, :], waw=True)
            kb.op("pool", [], [vsA_b], lambda e: e.memset(vsA[:], 1.0))
            kb.op("pool", [], [vwA_b], lambda e: e.memset(vwA[:], 1.0))
            kb.op("pool", [], [vcA_b], lambda e: e.memset(vcA[:], 0.0))
            kb.op("pool", [vcA_b], [vcA_b], lambda e: e.memset(vcA[:, :, :, 64:65], 1.0))
            kb.op("pool", [], [kcT_b], lambda e: e.memset(kcT[:], 0.0))

            def rstd_from_ss(ss_ap, n, out_ap, bufs):
                kb.op("dve", bufs, bufs, lambda e: e.tensor_scalar(out=out_ap, in0=ss_ap, scalar1=1.0 / n, scalar2=EPS, op0=ALU.mult, op1=ALU.add))
                kb.op("act", bufs, bufs, lambda e: e.activation(out=out_ap, in_=out_ap, func=AF.Sqrt))
                kb.op("dve", bufs, bufs, lambda e: e.reciprocal(out=out_ap, in_=out_ap))

            qT_D = ow["qT_D"]
            kT_D = ow["kT_D"]
            sz_D = ow["sz_D"]
            qD_b = kb.buf("qD")
            kD_b = kb.buf("kD")
            szD_b = kb.buf("szD")

            with ExitStack() as esA:
                old = kb.es
                kb.es = esA
                wgl, wgl_b = kb.sb("oA_wgl", [128, 8, 48], BF16)
                kb.dma("sp", [wdram_b[wk]], [wgl_b], wgl[:], w_in_bf[0:1024, 2560:2608].rearrange("(k p) c -> p k c", p=128))
                hts = [kb.sb("oA_h%d" % i, [128, D], F32) for i in range(2)]
                hn, hn_b = kb.sb("oA_hn", [128, D], BF16)
                junk, junk_b = kb.sb("oA_junk", [128, D], F32)
                st4, st4_b = kb.sb("oA_st4", [128, 2], F32)
                hnT, hnT_b = kb.sb("oA_hnT", [128, 8, 512], BF16)
                stg = [kb.sb("oA_stg%d" % i, [128, 512], BF16) for i in range(6)]
                gtmp, gtmp_b = kb.sb("oA_gtmp", [128, 192], F32)
                si = [0]
                pi = [0]

                def nps():
                    p = PS[(0, 1, 3, 4, 5, 6)[pi[0] % 6]]
                    pi[0] += 1
                    return p

                def fm_tile(wt, wb_, col0, dst_ap, scale=1.0):
                    pt, pb_ = nps()
                    for k in range(8):
                        kb.op("pe", [wb_, hnT_b], [pb_], lambda e, k=k: e.matmul(pt[:], lhsT=wt[:, k, col0:col0 + 128], rhs=hnT[:, k, :], start=(k == 0), stop=(k == 7)))
                    s_, sb_ = stg[si[0] % 6]
                    si[0] += 1
                    kb.op("act", [pb_], [sb_], lambda e: e.activation(out=s_[:], in_=pt[:], func=AF.Copy, scale=scale))
                    return s_, sb_

                for blk in range(NB):
                    t0 = blk * 512
                    for tt in range(4):
                        ti = blk * 4 + tt
                        ht, hb = hts[ti % 2]
                        kb.dma("sp", [h_b[ti]], [hb], ht[:], src[t0 + tt * 128:t0 + (tt + 1) * 128, :])
                        kb.op("act", [hb], [junk_b, st4_b], lambda e: e.activation(out=junk[:], in_=ht[:], func=AF.Square, accum_out=st4[:, 0:1]))
                        rstd_from_ss(st4[:, 0:1], float(D), st4[:, 1:2], [st4_b])
                        kb.op("dve", [hb, st4_b, norm_b], [hn_b], lambda e: e.scalar_tensor_tensor(out=hn[:], in0=ht[:], scalar=st4[:, 1:2], in1=norm_t[:], op0=ALU.mult, op1=ALU.mult))
                        pt, pb_ = PS[2]
                        ptb = pt[:].bitcast(BF16)
                        for k in range(8):
                            kb.op("pe", [hn_b, cb_b], [pb_], lambda e, k=k: e.transpose(out=ptb[:, k * 128:(k + 1) * 128], in_=hn[:, k * 128:(k + 1) * 128], identity=ident_b))
                        kb.op("act", [pb_], [hnT_b], lambda e, tt=tt: e.activation(out=hnT[:, :, tt * 128:(tt + 1) * 128], in_=ptb.rearrange("p (k t) -> p k t", k=8), func=AF.Copy), waw=(tt > 0))
                    for sl in range(2):
                        wt, wb_ = load_w(w_in_bf, 0, sl * 512, 512, wkey=wk)
                        for jj in range(4):
                            s_, sb_ = fm_tile(wt, wb_, jj * 128, None, scale=0.125)
                            r0 = (sl * 4 + jj) * 128
                            kb.dma("pool", [sb_], [qD_b], qT_D[r0:r0 + 128, t0:t0 + 512], s_[:], waw=True)
                    wt, wb_ = load_w(w_in_bf, 0, 1024, 512, wkey=wk)
                    for jj in range(4):
                        s_, sb_ = fm_tile(wt, wb_, jj * 128, None)
                        kind, half = jj // 2, jj % 2
                        kb.dma("pool", [sb_], [kD_b], kT_D[kind, half * 128:(half + 1) * 128, t0:t0 + 512], s_[:], waw=True)
                    for kind, c0, vA, vA_b in ((2, 1536, vsA, vsA_b), (3, 2048, vwA, vwA_b)):
                        wt, wb_ = load_w(w_in_bf, 0, c0, 512, wkey=wk)
                        for jj in range(2):
                            s_, sb_ = fm_tile(wt, wb_, jj * 128, None)
                            kb.dma("pool", [sb_], [kD_b], kT_D[kind, jj * 128:(jj + 1) * 128, t0:t0 + 512], s_[:], waw=True)
                        for tt in range(4):
                            ti = blk * 4 + tt
                            pt, pb_ = nps()
                            for k in range(8):
                                kb.op("pe", [wb_, hnT_b], [pb_], lambda e, k=k, tt=tt: e.matmul(pt[:, 0:256], lhsT=hnT[:, k, tt * 128:(tt + 1) * 128], rhs=wt[:, k, 256:512], start=(k == 0), stop=(k == 7)))
                            kb.op("act", [pb_], [vA_b], lambda e, ti=ti, vA=vA: e.activation(out=vA[:, ti, :, 0:64], in_=pt[:, 0:256].rearrange("p (g d) -> p g d", g=4), func=AF.Copy), waw=True)
                    wt, wb_ = wgl, wgl_b
                    pt, pb_ = nps()
                    for tt in range(4):
                        for k in range(8):
                            kb.op("pe", [wb_, hnT_b], [pb_], lambda e, k=k, tt=tt: e.matmul(pt[:, tt * 48:(tt + 1) * 48], lhsT=hnT[:, k, tt * 128:(tt + 1) * 128], rhs=wt[:, k, 0:48], start=(k == 0), stop=(k == 7)), waw=(tt > 0 or k > 0))
                    kb.op("act", [pb_], [gtmp_b], lambda e: e.activation(out=gtmp[:], in_=pt[:, 0:192], func=AF.Copy))
                    kb.op("dve", [gtmp_b, gbias_b], [gtmp_b], lambda e: e.tensor_tensor(out=gtmp[:].rearrange("p (t c) -> p t c", t=4), in0=gtmp[:].rearrange("p (t c) -> p t c", t=4), in1=gbias[:].unsqueeze(1).broadcast_to([128, 4, 48]), op=ALU.add))
                    kb.op("act", [gtmp_b], [gts_b], lambda e, blk=blk: e.activation(out=gts[:, blk * 4:(blk + 1) * 4, :], in_=gtmp[:].rearrange("p (t c) -> p t c", t=4), func=AF.Sigmoid), waw=True)
                    for half in range(2):
                        wt, wb_ = load_w(w_in_bf, 0, 2608 + half * 512, 512, wkey=wk)
                        for tt in range(4):
                            pt, pb_ = nps()
                            for k in range(8):
                                kb.op("pe", [wb_, hnT_b], [pb_], lambda e, k=k, tt=tt: e.matmul(pt[:], lhsT=hnT[:, k, tt * 128:(tt + 1) * 128], rhs=wt[:, k, :], start=(k == 0), stop=(k == 7)))
                            s_, sb_ = stg[si[0] % 6]
                            si[0] += 1
                            kb.op("act", [pb_], [sb_], lambda e: e.activation(out=s_[:], in_=pt[:], func=AF.Silu))
                            kb.dma("pool", [sb_], [szD_b], sz_D[t0 + tt * 128:t0 + (tt + 1) * 128, half * 512:(half + 1) * 512], s_[:], waw=True)
                kb.es = old
            kb.barrier()
            if ODD_STOP == "A":
                return
            kb.dma("sp", [kD_b], [ksA_b], ksA[0:64, :, :], kT_D[2].rearrange("(g d) s -> d g s", g=4), waw=True)
            kb.dma("sp", [kD_b], [kwT_b], kwT[:, :, :], kT_D[3].rearrange("(g d) s -> d g s", g=4))

            with ExitStack() as esB:
                old = kb.es
                kb.es = esB
                w1f, w1f_b = kb.sb("oB_w1f", [64, 16, 256], F32)
                w1b, w1b_b = kb.sb("oB_w1b", [64, 32, 256], BF16)
                w2f, w2f_b = kb.sb("oB_w2f", [128, 2, 64], F32)
                w2b, w2b_b = kb.sb("oB_w2b", [128, 2, 64], BF16)
                pef, pef_b = kb.sb("oB_pef", [64, 32], F32)
                peb, peb_b = kb.sb("oB_peb", [64, 32], BF16)
                b1, b1_b = kb.sb("oB_b1", [128, 2], F32)
                tcs = [kb.sb("oB_tc%d" % i, [64, S], BF16) for i in range(2)]
                h1T, h1T_b = kb.sb("oB_h1T", [128, 2, NCP], BF16)
                for kind in range(2):
                    nm = "kv"[kind]
                    for lh_ in range(2):
                        kb.dma("sp", [], [w1f_b], w1f[:], ow["w1_" + nm][li][lh_ * 1024:(lh_ + 1) * 1024, :].rearrange("(l d) j -> d l j", d=64))
                        kb.op("pool", [w1f_b], [w1b_b], lambda e, lh_=lh_: e.tensor_copy(out=w1b[:, lh_ * 16:(lh_ + 1) * 16, :], in_=w1f[:]), waw=(lh_ > 0))
                    kb.dma("sp", [], [w2f_b], w2f[:], ow["w2_" + nm][li].rearrange("(jt p) d -> p jt d", p=128))
                    kb.op("pool", [w2f_b], [w2b_b], lambda e: e.tensor_copy(out=w2b[:], in_=w2f[:]))
                    kb.dma("sp", [], [pef_b], pef[:], ow["peT_" + nm][li])
                    kb.op("pool", [pef_b], [peb_b], lambda e: e.tensor_copy(out=peb[:], in_=pef[:]))
                    pt, pb_ = PS[2]
                    for jt in range(2):
                        for l in range(32):
                            kb.op("pe", [w1b_b, peb_b], [pb_], lambda e, jt=jt, l=l: e.matmul(pt[:, jt:jt + 1], lhsT=w1b[:, l, jt * 128:(jt + 1) * 128], rhs=peb[:, l:l + 1], start=(jt == 0 and l == 0), stop=(l == 31), skip_group_check=True), waw=True)
                    kb.op("dve", [pb_], [b1_b], lambda e: e.tensor_copy(out=b1[:], in_=pt[:, 0:2]))
                    for g in range(4):
                        tc_, tcb = tcs[g % 2]
                        kb.dma("sp", [kD_b], [tcb], tc_[:], kT_D[kind, g * 64:(g + 1) * 64, :])
                        tcv = tc_[:].rearrange("p (n s) -> p n s", s=16)
                        for jt in range(2):
                            pj, pjb = PS[jt]
                            for l in range(32):
                                kb.op("pe", [w1b_b, tcb], [pjb], lambda e, jt=jt, l=l: e.matmul(pj[:, 0:NCMP], lhsT=w1b[:, l, jt * 128:(jt + 1) * 128], rhs=tcv[:, l // 16:l // 16 + NCMP, l % 16], start=(l == 0), stop=(l == 31)))
                            kb.op("act", [pjb, b1_b], [h1T_b], lambda e, jt=jt: e.activation(out=h1T[:, jt, 0:NCMP], in_=pj[:, 0:NCMP], func=AF.Silu, bias=b1[:, jt:jt + 1]), waw=(jt > 0))
                        if kind == 0:
                            po, pob = PS[3]
                            for jt in range(2):
                                kb.op("pe", [w2b_b, h1T_b], [pob], lambda e, jt=jt: e.matmul(po[0:64, 0:NCMP], lhsT=w2b[:, jt, :], rhs=h1T[:, jt, 0:NCMP], start=(jt == 0), stop=(jt == 1)))
                            kb.op("act", [pob], [kcT_b], lambda e, g=g: e.activation(out=kcT[:, g, 0:NCMP], in_=po[0:64, 0:NCMP], func=AF.Copy), waw=True)
                        else:
                            for nt in range(NCT):
                                rows = min(NCMP, (nt + 1) * 128) - nt * 128
                                po, pob = PS[3]
                                for jt in range(2):
                                    kb.op("pe", [w2b_b, h1T_b], [pob], lambda e, jt=jt, nt=nt, rows=rows: e.matmul(po[0:rows, 0:64], lhsT=h1T[:, jt, nt * 128:nt * 128 + rows], rhs=w2b[:, jt, :], start=(jt == 0), stop=(jt == 1)))
                                kb.op("act", [pob], [vcA_b], lambda e, g=g, nt=nt, rows=rows: e.activation(out=vcA[0:rows, nt, g, 0:64], in_=po[0:rows, 0:64], func=AF.Copy), waw=True)
                kb.es = old
            kb.barrier()
            if ODD_STOP == "B":
                return

            with ExitStack() as esC:
                old = kb.es
                kb.es = esC
                qas = [kb.sb("oC_qa%d" % i, [128, 4, 512], BF16) for i in range(2)]
                qab = [[kb.buf("qab%d_%d" % (i, g)) for g in range(4)] for i in range(2)]
                NPT = 4
                pts = [kb.sb("oC_pt%d" % i, [128, 512], BF16) for i in range(NPT)]
                NET = 6
                ets = [kb.sb("oC_et%d" % i, [128, NCP], F32) for i in range(NET)]
                eti = [0]
                rss = [kb.sb("oC_rs%d" % i, [128, 8], F32) for i in range(2)]
                cmsk, cmsk_b = kb.sb("oC_cmsk", [10, 128 + 288], BF16)
                kb.dma("sp", [], [cmsk_b], cmsk[:], ow["cmsk"][:, :])
                pacc, pacc_b = kb.sb("oC_pacc", [128, NCP], F32)
                imp, imp_b = kb.sb("oC_imp", [128, 64], F32)
                imp2, imp2_b = kb.sb("oC_imp2", [128, 64], F32)
                m8, m8_b = kb.sb("oC_m8", [128, 16], F32)
                bts = [kb.sb("oC_bt%d" % g, [128, 128], BF16) for g in range(4)]
                szt = [kb.sb("oC_sz%d" % i, [128, D], BF16) for i in range(2)]
                hts = [kb.sb("oC_h%d" % i, [128, D], F32) for i in range(3)]
                cfs, cfs_b = kb.sb("oC_cfs", [128, 3, 4], F32)
                acc, acc_b = kb.sb("oC_acc", [128, 256], F32)
                ys = [kb.sb("oC_y%d" % i, [128, D], BF16) for i in range(2)]
                yT, yT_b = kb.sb("oC_yT", [128, 8, 128], BF16)
                for g in range(4):
                    kb.op("pool", [], [bts[g][1]], lambda e, g=g: e.memset(bts[g][0][:], 0.0))
                pti = [0]
                sci = [0]

                def sc_bank():
                    p = PS[sci[0] % 4]
                    sci[0] += 1
                    return p

                def o_bank(g, X):
                    return PS[4 + X]

                ob3s = [kb.sb("oC_ob3_%d" % i, [128, 3, 260], F32) for i in range(2)]
                tmp3, tmp3_b = kb.sb("oC_tmp3", [128, 3, 256], F32)

                def load_tile(qi):
                    slot = qi % 2
                    t0 = qi * 128
                    qa, qa_b = qas[slot]
                    kb.dma("sp", [qD_b], [qa_b], qa[0:64, :, :].rearrange("d g (r t) -> d g r t", r=4), qT_D[:, t0:t0 + 128].rearrange("(g r d) t -> d g r t", g=4, r=4))
                    sz_, sz_b = szt[slot]
                    kb.dma("sp", [szD_b], [sz_b], sz_[:], sz_D[t0:t0 + 128, :])
                    ht, hb = hts[qi % 3]
                    kb.dma("sp", [h_b[qi]], [hb], ht[:], src[t0:t0 + 128, :])

                def need_sel(qi):
                    return (qi * 128 + 127) >= 1024 and NSEL > 16

                def imp_front(qi, gsel=None):
                    slot = qi % 2
                    t0 = qi * 128
                    qa, qa_b = qas[slot]
                    if not need_sel(qi):
                        for g in (range(4) if gsel is None else [gsel]):
                            kb.op("pool", [], [qab[slot][g]], lambda e, g=g: e.memset(qa[64:128, g, :], 0.0))
                        return
                    ncol = min(NCP, 8 * (qi + 1))
                    s0 = 128 + 258 - 8 * qi
                    for g in (range(4) if gsel is None else [gsel]):
                        rs4, rs4_b = rss[eti[0] % 2]
                        hets = []
                        for r in range(4):
                            ip, ipb = sc_bank()
                            kb.op("pe", [qa_b, kcT_b], [ipb], lambda e, r=r, g=g: e.matmul(ip[:, 0:ncol], lhsT=qa[0:64, g, r * 128:(r + 1) * 128], rhs=kcT[:, g, 0:ncol], start=True, stop=False))
                            kb.op("pe", [cmsk_b], [ipb], lambda e: e.matmul(ip[:, 0:ncol], lhsT=cmsk[0:10, 0:128], rhs=cmsk[0:10, s0:s0 + ncol], start=False, stop=True), waw=True)
                            et, et_b = ets[eti[0] % NET]
                            eti[0] += 1
                            kb.op("act", [ipb], [et_b, rs4_b], lambda e, r=r: e.activation(out=et[:, 0:ncol], in_=ip[:, 0:ncol], func=AF.Exp, accum_out=rs4[:, r:r + 1]), waw=True)
                            hets.append((et, et_b))
                        kb.op("dve", [rs4_b], [rs4_b], lambda e: e.tensor_scalar(out=rs4[:, 4:8], in0=rs4[:, 0:4], scalar1=1e-30, scalar2=None, op0=ALU.max))
                        kb.op("dve", [rs4_b], [rs4_b], lambda e: e.reciprocal(out=rs4[:, 4:8], in_=rs4[:, 4:8]))
                        if ncol < NCP:
                            kb.op("dve", [], [pacc_b], lambda e: e.memset(pacc[:, ncol:NCP], 0.0))
                        for r in range(4):
                            et, et_b = hets[r]
                            if r == 0:
                                kb.op("dve", [et_b, rs4_b], [pacc_b], lambda e, et=et: e.tensor_scalar(out=pacc[:, 0:ncol], in0=et[:, 0:ncol], scalar1=rs4[:, 4:5], scalar2=None, op0=ALU.mult), waw=True)
                            else:
                                kb.op("dve", [et_b, rs4_b, pacc_b], [pacc_b], lambda e, et=et, r=r: e.scalar_tensor_tensor(out=pacc[:, 0:ncol], in0=et[:, 0:ncol], scalar=rs4[:, 4 + r:5 + r], in1=pacc[:, 0:ncol], op0=ALU.mult, op1=ALU.add))
                        bt, bt_b = bts[g]
                        pv = pacc[:, 0:4 * NSEL].rearrange("p (j i) -> p j i", i=4)
                        kb.op("dve", [pacc_b], [imp_b], lambda e: e.tensor_reduce(out=imp[:, 0:NSEL], in_=pv, axis=AX.X, op=ALU.add))
                        kb.op("dve", [pacc_b, imp_b], [imp_b], lambda e: e.tensor_tensor(out=imp[:, 1:NSEL], in0=imp[:, 1:NSEL], in1=pv[:, 0:NSEL - 1, 3], op=ALU.add))
                        kb.op("dve", [imp_b, Amask_b], [imp_b], lambda e: e.tensor_tensor(out=imp[:, 0:NSEL], in0=imp[:, 0:NSEL], in1=Amask[:, 64 - 2 * qi:64 - 2 * qi + NSEL], op=ALU.add))
                        kb.op("dve", [imp_b], [imp_b], lambda e: e.memset(imp[:, 0:1], 1.0e6))
                        kb.op("dve", [imp_b], [m8_b], lambda e: e.max(out=m8[:, 0:8], in_=imp[:, 0:NSEL]))
                        kb.op("dve", [imp_b, m8_b], [imp2_b], lambda e: e.match_replace(out=imp2[:, 0:NSEL], in_to_replace=m8[:, 0:8], in_values=imp[:, 0:NSEL], imm_value=-2.0e9))
                        kb.op("dve", [imp2_b], [m8_b], lambda e: e.max(out=m8[:, 8:16], in_=imp2[:, 0:NSEL]))
                        kb.op("dve", [imp_b, m8_b], [bt_b], lambda e: e.tensor_scalar(out=bt[:, 64:64 + NSEL], in0=imp[:, 0:NSEL], scalar1=m8[:, 15:16], scalar2=NEG, op0=ALU.is_lt, op1=ALU.mult))

                def imp_back(qi):
                    if not need_sel(qi):
                        return
                    slot = qi % 2
                    qa, qa_b = qas[slot]
                    tp, tpb = PS[7]
                    tpv = tp[:].bitcast(BF16)
                    for g in range(4):
                        bt, bt_b = bts[g]
                        kb.op("pe", [bt_b, cb_b], [tpb], lambda e, g=g: e.transpose(out=tpv[:, g * 128:(g + 1) * 128], in_=bt[:], identity=ident_b), waw=(g > 0))
                    for g in range(4):
                        for r in range(4):
                            kb.op("dve", [tpb], [qab[slot][g]], lambda e, r=r, g=g: e.tensor_copy(out=qa[64:128, g, r * 128:(r + 1) * 128], in_=tpv[64:128, g * 128:(g + 1) * 128]), waw=(r > 0))

                def emit_qk(u):
                    sp_, spb = sc_bank()
                    kb.op("pe", u["rd"], [spb], lambda e: e.matmul(sp_[:], lhsT=u["lhsT"], rhs=u["rhs"], start=True, stop=True))
                    p_, p_b = pts[pti[0] % NPT]
                    pti[0] += 1
                    kb.op("act", [spb], [p_b], lambda e: e.activation(out=p_[:], in_=sp_[:], func=AF.Exp))
                    if u["mask"] is not None:
                        base, cm, step = u["mask"]
                        kb.op("pool", [p_b], [p_b], lambda e: e.affine_select(out=p_[:], in_=p_[:], pattern=[[0, 4], [step, 128]], compare_op=ALU.is_ge, fill=0.0, base=base, channel_multiplier=cm))
                    u["p"] = (p_, p_b)

                def emit_pv(u):
                    p_, p_b = u["p"]
                    o_ps, o_pb = u["o"]
                    for r in range(4):
                        st = u["first"] and r == 0
                        kb.op("pe", [p_b, u["vb"]], [o_pb], lambda e, r=r, st=st: e.matmul(o_ps[:, r * 65:(r + 1) * 65], lhsT=p_[:, r * 128:(r + 1) * 128], rhs=u["v"], start=st, stop=u["last"], skip_group_check=True), waw=not st)
                    if u["last"]:
                        ob, ob_b = ob3s[u["g"] % 2]
                        X_ = u["X"]
                        evq.append([2, lambda: kb.op("act", [o_pb], [ob_b], lambda e: e.activation(out=ob[:, X_, :], in_=o_ps[:, 0:260], func=AF.Copy), waw=True)])

                evq = []

                def evq_tick(force=False):
                    for it in list(evq):
                        it[0] -= 1
                        if it[0] <= 0 or force:
                            it[1]()
                            evq.remove(it)

                def combine(qi, g):
                    evq_tick(force=True)
                    slot = qi % 2
                    sz_, sz_b = szt[slot]
                    y, y_b = ys[qi % 2]
                    ob, ob_b = ob3s[g % 2]
                    gv = gts[:, qi, :].rearrange("p (h x) -> p h x", x=3)[:, 4 * g:4 * g + 4, :].rearrange("p r x -> p x r")
                    ov = ob[:].rearrange("p x (r c) -> p x r c", c=65)
                    kb.op("dve", [ob_b], [cfs_b], lambda e: e.tensor_scalar(out=cfs[:], in0=ov[:, :, :, 64], scalar1=1e-30, scalar2=None, op0=ALU.max))
                    kb.op("dve", [cfs_b], [cfs_b], lambda e: e.reciprocal(out=cfs[:], in_=cfs[:]))
                    kb.op("dve", [cfs_b, gts_b], [cfs_b], lambda e: e.tensor_tensor(out=cfs[:], in0=cfs[:], in1=gv, op=ALU.mult))
                    kb.op("dve", [ob_b, cfs_b], [tmp3_b], lambda e: e.tensor_tensor(out=tmp3[:].rearrange("p x (r d) -> p x r d", r=4), in0=ov[:, :, :, 0:64], in1=cfs[:].unsqueeze(3).broadcast_to([128, 3, 4, 64]), op=ALU.mult))
                    kb.op("dve", [tmp3_b], [acc_b], lambda e: e.tensor_tensor(out=acc[:], in0=tmp3[:, 0, :], in1=tmp3[:, 1, :], op=ALU.add))
                    kb.op("dve", [tmp3_b, acc_b], [acc_b], lambda e: e.tensor_tensor(out=acc[:], in0=acc[:], in1=tmp3[:, 2, :], op=ALU.add))
                    kb.op("dve", [acc_b, sz_b], [y_b], lambda e, g=g: e.tensor_tensor(out=y[:, g * 256:(g + 1) * 256], in0=acc[:], in1=sz_[:, g * 256:(g + 1) * 256], op=ALU.mult), waw=(g > 0))

                def units_for(qi, g):
                    slot = qi % 2
                    t0 = qi * 128
                    qa, qa_b = qas[slot]
                    us = []
                    cts = [ct for ct in range(NCT) if t0 + 127 - 2048 * ct - 31 >= 0]
                    for n_, ct in enumerate(cts):
                        base = t0 - 2048 * ct - 31
                        mk = None if base - 16 * 127 >= 0 else (base, -16, 1)
                        us.append(dict(rd=[qa_b, kcT_b], lhsT=kcT[:, g, ct * 128:(ct + 1) * 128], rhs=qa[0:64, g, :], mask=mk,
                                       v=vcA[:, ct, g, :], vb=vcA_b, o=o_bank(g, 0), X=0, first=(n_ == 0), last=(n_ == len(cts) - 1)))
                    for kt in range(qi + 1):
                        us.append(dict(rd=[qa_b, qab[slot][g], ksA_b], lhsT=ksA[:, g, kt * 128:(kt + 1) * 128], rhs=qa[:, g, :],
                                       mask=((0, -1, 1) if kt == qi else None), v=vsA[:, kt, g, :], vb=vsA_b, o=o_bank(g, 1), X=1, first=(kt == 0), last=(kt == qi)))
                    k0 = max(0, qi - 4)
                    for kt in range(k0, qi + 1):
                        mk = (0, -1, 1) if kt == qi else ((-1, 1, -1) if kt == qi - 4 else None)
                        us.append(dict(rd=[qa_b, kwT_b], lhsT=kwT[:, g, kt * 128:(kt + 1) * 128], rhs=qa[0:64, g, :], mask=mk,
                                       v=vwA[:, kt, g, :], vb=vwA_b, o=o_bank(g, 2), X=2, first=(kt == k0), last=(kt == qi)))
                    for u in us:
                        u["g"] = g
                    return us

                def out_proj(qi):
                    t0 = qi * 128
                    ht, hb = hts[qi % 3]
                    y, y_b = ys[qi % 2]
                    tp, tpb = PS[7]
                    tpv = tp[:].bitcast(BF16)
                    for k in range(8):
                        kb.op("pe", [y_b, cb_b], [tpb], lambda e, k=k: e.transpose(out=tpv[:, k * 128:(k + 1) * 128], in_=y[:, k * 128:(k + 1) * 128], identity=ident_b))
                    kb.op("act", [tpb], [yT_b], lambda e: e.activation(out=yT[:], in_=tpv.rearrange("p (k t) -> p k t", k=8), func=AF.Copy))

                def out_proj2(qi):
                    t0 = qi * 128
                    ht, hb = hts[qi % 3]
                    for half in range(2):
                        pp, ppb = PS[7]
                        for k in range(8):
                            kb.op("pe", [yT_b, wo_b], [ppb], lambda e, k=k, half=half: e.matmul(pp[:], lhsT=yT[:, k, :], rhs=wo[:, k, half * 512:(half + 1) * 512], start=(k == 0), stop=(k == 7)))
                        kb.op("dve", [ppb, hb], [hb], lambda e, half=half: e.tensor_tensor(out=ht[:, half * 512:(half + 1) * 512], in0=pp[:], in1=ht[:, half * 512:(half + 1) * 512], op=ALU.add))
                    if state.get("fuse_final"):
                        fs, fs_b = fns
                        kb.op("act", [hb], [yT_b, fs_b], lambda e: e.activation(out=yT[:].rearrange("p k t -> p (k t)"), in_=ht[:], func=AF.Square, accum_out=fs[:, 0:1]))
                        kb.op("dve", [fs_b], [fs_b], lambda e: e.tensor_scalar(out=fs[:, 1:2], in0=fs[:, 0:1], scalar1=1.0 / D, scalar2=EPS, op0=ALU.mult, op1=ALU.add))
                        kb.op("act", [fs_b], [fs_b], lambda e: e.activation(out=fs[:, 1:2], in_=fs[:, 1:2], func=AF.Sqrt))
                        kb.op("dve", [fs_b], [fs_b], lambda e: e.reciprocal(out=fs[:, 1:2], in_=fs[:, 1:2]))
                        kb.op("dve", [hb, fs_b, norm_b], [hb], lambda e: e.scalar_tensor_tensor(out=ht[:], in0=ht[:], scalar=fs[:, 1:2], in1=norm_t[:], op0=ALU.mult, op1=ALU.mult))
                        ob_ = kb.buf()
                        outs.append(ob_)
                        kb.dma("pool", [hb], [ob_], out_d[t0:t0 + 128, :], ht[:])
                    else:
                        kb.dma("pool", [hb], [h_b[qi]], hD[t0:t0 + 128, :], ht[:])

                if state.get("fuse_final"):
                    fns = kb.sb("oC_fs", [128, 2], F32)
                    kb.dma("sp", [], [norm_b], norm_t[:], final_norm[:, :])
                LOOK = 3
                load_tile(0)
                imp_front(0)
                imp_back(0)
                deferred = []
                for qi in range(NT):
                    if qi + 1 < NT:
                        load_tile(qi + 1)
                    todo = deferred
                    deferred = []
                    if qi + 1 < NT:
                        for g_ in range(4):
                            todo.append((6 + 8 * g_, lambda qi=qi, g_=g_: imp_front(qi + 1, g_)))
                    units = []
                    for g in range(4):
                        units += units_for(qi, g)
                    n = len(units)
                    if qi + 1 < NT:
                        todo.append((max(62, n - 40), lambda qi=qi: imp_back(qi + 1)))
                    for i in range(n + LOOK):
                        if i < n:
                            emit_qk(units[i])
                        evq_tick()
                        if i >= LOOK:
                            u = units[i - LOOK]
                            emit_pv(u)
                            if i - LOOK + 1 == n or units[i - LOOK + 1]["g"] != u["g"]:
                                combine(qi, u["g"])
                        for (k_, fn) in todo:
                            if k_ == i:
                                fn()
                    for (k_, fn) in todo:
                        if k_ >= n + LOOK:
                            fn()
                    deferred.append((40, lambda qi=qi: out_proj(qi)))
                    deferred.append((46, lambda qi=qi: out_proj2(qi)))
                for (k_, fn) in deferred:
                    fn()
                kb.es = old
            state["src"] = hD

        def final_norm_phase():
            src = state["src"]
            fn_t, fn_b = kb.sb("fn_t", [128, D], F32)
            kb.dma("sp", [], [fn_b], fn_t[:], final_norm[:, :])
            hts = [kb.sb("fn_h%d" % i, [128, D], F32) for i in range(2)]
            fj, fj_b = kb.sb("fn_j", [128, D], F32)
            fs, fs_b = kb.sb("fn_s", [128, 2], F32)
            for ti in range(NT):
                ht, hb = hts[ti % 2]
                kb.dma("sp", [h_b[ti]], [hb], ht[:], src[ti * 128:(ti + 1) * 128, :])
                kb.op("act", [hb], [fj_b, fs_b], lambda e: e.activation(out=fj[:], in_=ht[:], func=AF.Square, accum_out=fs[:, 0:1]))
                kb.op("dve", [fs_b], [fs_b], lambda e: e.tensor_scalar(out=fs[:, 1:2], in0=fs[:, 0:1], scalar1=1.0 / D, scalar2=EPS, op0=ALU.mult, op1=ALU.add))
                kb.op("act", [fs_b], [fs_b], lambda e: e.activation(out=fs[:, 1:2], in_=fs[:, 1:2], func=AF.Sqrt))
                kb.op("dve", [fs_b], [fs_b], lambda e: e.reciprocal(out=fs[:, 1:2], in_=fs[:, 1:2]))
                kb.op("dve", [hb, fs_b, fn_b], [hb], lambda e: e.scalar_tensor_tensor(out=ht[:], in0=ht[:], scalar=fs[:, 1:2], in1=fn_t[:], op0=ALU.mult, op1=ALU.mult))
                ob = kb.buf()
                outs.append(ob)
                kb.dma("pool", [hb], [ob], out_d[ti * 128:(ti + 1) * 128, :], ht[:])

        def cast_weight_dma(src_ap, dst_ap, rows, nm):
            wdram_b[nm] = kb.buf(nm)
            for r in range(0, rows, 128):
                kb.dma("pool", [], [wdram_b[nm]], dst_ap[r:r + 128, :], src_ap[r:r + 128, :], waw=True, max_dma_last_dim=4096)

        def cast_layer(kind, li):
            if kind == "e":
                cast_weight_dma(ew["w_in"][li], ew["w_in_bf"][li], D, "cwi%d" % li)
                cast_weight_dma(ew["w_out"][li], ew["w_out_bf"][li], 2048, "cwo%d" % li)
            else:
                cast_weight_dma(ow["w_in"][li], ow["w_in_bf"][li], D, "owi%d" % li)
                cast_weight_dma(ow["w_out"][li], ow["w_out_bf"][li], D, "owo%d" % li)

        cast_layer(*layers[0])
        for idx, (kind, li) in enumerate(layers):
            if idx + 1 < len(layers):
                cast_layer(*layers[idx + 1])
            with ExitStack() as es2:
                old = kb.es
                kb.es = es2
                state["fuse_final"] = (kind == "o" and idx == len(layers) - 1 and not debug_h)
                if kind == "e":
                    even_layer(li)
                else:
                    odd_layer(li)
                kb.es = old
            kb.barrier()
        if debug_h:
            hts = [kb.sb("dbg_h%d" % i, [128, D], F32) for i in range(2)]
            for ti in range(NT):
                ht, hb = hts[ti % 2]
                kb.dma("sp", [h_b[ti]], [hb], ht[:], state["src"][ti * 128:(ti + 1) * 128, :])
                ob = kb.buf()
                outs.append(ob)
                kb.dma("pool", [hb], [ob], out_d[ti * 128:(ti + 1) * 128, :], ht[:])
        elif not state.get("fuse_final"):
            with ExitStack() as es2:
                old = kb.es
                kb.es = es2
                final_norm_phase()
                kb.es = old
        kb.finish(outs)
    return nc


def make_consts():
    k = np.arange(128)
    ident = np.eye(128, dtype=np.float32)
    L = (k[:, None] <= k[None, :]).astype(np.float32)
    U = (k[:, None] > k[None, :]).astype(np.float32)
    tri = (k[None, :] >= k[:, None]).astype(np.float32)
    ones = np.ones((128, 128), np.float32)
    cf = np.concatenate([ident, L, U, tri, ones], axis=1)
    import ml_dtypes
    cb = np.concatenate([ident, L], axis=1).astype(ml_dtypes.bfloat16)
    return cf, cb


def bc(v):
    v = np.asarray(v, np.float32)
    return np.ascontiguousarray(np.broadcast_to(v[:, None], (v.shape[0], 128) + v.shape[1:]))


def host_inputs(inp, layers, S=4096):
    cf, cb = make_consts()
    m = {"consts_f32": cf, "consts_bf16": cb,
         "final_norm": np.ascontiguousarray(np.broadcast_to(np.asarray(inp["final_norm"], np.float32), (128, D)))}
    if any(l[0] == "e" for l in layers):
        m["e_norm"] = bc(inp["e_norm"])
        m["e_w_in"] = np.ascontiguousarray(inp["e_w_in"], dtype=np.float32)
        m["e_ssd_conv_w"] = np.ascontiguousarray(np.asarray(inp["e_ssd_conv_w"], np.float32).reshape(2, 4, 12, 128).transpose(0, 3, 2, 1))
        m["e_ssd_conv_b"] = np.ascontiguousarray(np.asarray(inp["e_ssd_conv_b"], np.float32).reshape(2, 12, 128).transpose(0, 2, 1))
        m["e_dt_bias"] = bc(inp["e_dt_bias"])
        m["e_a_log"] = bc(inp["e_a_log"])
        m["e_d_skip"] = bc(inp["e_d_skip"])
        m["e_ssd_norm"] = bc(inp["e_ssd_norm"])
        m["e_conf_conv_w"] = np.ascontiguousarray(np.asarray(inp["e_conf_conv_w"], np.float32).reshape(2, 31, 8, 128).transpose(0, 3, 2, 1))
        for nm in ("e_conf_conv_b", "e_conf_ln_g", "e_conf_ln_b"):
            m[nm] = np.ascontiguousarray(np.asarray(inp[nm], np.float32).reshape(2, 8, 128).transpose(0, 2, 1))
        m["e_w_out"] = np.ascontiguousarray(inp["e_w_out"], dtype=np.float32)
    if any(l[0] == "o" for l in layers):
        import ml_dtypes
        m["o_norm"] = bc(inp["o_norm"])
        m["o_w_in"] = np.ascontiguousarray(inp["o_w_in"], dtype=np.float32)
        m["o_gate_bias"] = bc(inp["o_gate_bias"])
        for nm in ("k", "v"):
            m["o_peT_" + nm] = np.ascontiguousarray(np.asarray(inp["o_cmp_pe_" + nm], np.float32).transpose(0, 2, 1))
            m["o_cmp_w1_" + nm] = np.ascontiguousarray(inp["o_cmp_w1_" + nm], dtype=np.float32)
            m["o_cmp_w2_" + nm] = np.ascontiguousarray(inp["o_cmp_w2_" + nm], dtype=np.float32)
        m["o_w_out"] = np.ascontiguousarray(inp["o_w_out"], dtype=np.float32)
        p = np.arange(128)[:, None]
        xx = np.arange(128)[None, :] - 64
        off = (p >= 64).astype(np.int64)
        A = np.zeros((128, 128), np.float32)
        A[(xx == off) | (xx == off - 1)] = 1.0e6
        A[xx > off] = -1.0e9
        m["o_amask"] = A
        E = (np.arange(64)[:, None] == (np.arange(S)[None, :] // 64)).astype(np.float32)
        m["o_emat"] = E.astype(ml_dtypes.bfloat16)
        jj = np.arange(10)[:, None]
        Mq = np.where(jj > (np.arange(128)[None, :] + 1) // 16, NEG, 0.0)
        Bd = (np.arange(288)[None, :] == jj + 256).astype(np.float32)
        m["o_cmsk"] = np.concatenate([Mq, Bd], axis=1).astype(ml_dtypes.bfloat16)
    return m


LAYERS = [("e", 0), ("o", 0), ("e", 1), ("o", 1)]


def kernel(**inputs):
    x = np.asarray(inputs["x"], np.float32)
    B, S, _ = x.shape
    nc = build_program(S, LAYERS)
    shared = host_inputs(inputs, LAYERS, S)
    in_maps = []
    for b in range(B):
        mm = dict(shared)
        mm["x"] = np.ascontiguousarray(x[b])
        in_maps.append(mm)
    res = run_bass_kernel_spmd(nc, in_maps, core_ids=list(range(B)))
    return np.stack([np.asarray(r["out"], np.float32) for r in res.results], axis=0)
```

```python
import numpy as np
from contextlib import ExitStack
import concourse.bass as bass
import concourse.mybir as mybir
from concourse.bass_utils import run_bass_kernel_spmd

F32 = mybir.dt.float32
BF16 = mybir.dt.bfloat16
AF = mybir.ActivationFunctionType
ALU = mybir.AluOpType
AX = mybir.AxisListType

D = 1024
E_IN = 5648
O_IN = 3632
EPS = 1e-6
NEG = -30000.0
ODD_STOP = None
ODD_DBG = 3


class Buf:
    __slots__ = ("name", "w", "r", "psum")

    def __init__(self, name, psum=False):
        self.name = name
        self.psum = psum
        self.w = {}
        self.r = {}


class KB:
    def __init__(self, nc, es, n_dsem=90):
        self.nc = nc
        self.es = es
        self.engs = {"pe": nc.tensor, "act": nc.scalar, "dve": nc.vector, "pool": nc.gpsimd, "sp": nc.sync}
        self.esem = {e: es.enter_context(nc.semaphore("s_" + e)) for e in self.engs}
        self.ecnt = {e: 0 for e in self.engs}
        self.dsem = [es.enter_context(nc.semaphore("d%d" % i)) for i in range(n_dsem)]
        self.dcnt = [0] * n_dsem
        self.dnext = 0
        self.seen = {e: {} for e in self.engs}
        self.nbuf = 0

    def buf(self, name=None):
        self.nbuf += 1
        return Buf(name or ("b%d" % self.nbuf))

    def sb(self, name, shape, dt):
        self.nbuf += 1
        name = "%s_u%d" % (name, self.nbuf)
        t = self.es.enter_context(self.nc.sbuf_tensor(name, list(shape), dt))
        return t, Buf(name)

    def _wait(self, e, key, val):
        if self.seen[e].get(key, 0) >= val:
            return
        sem = self.esem[key] if isinstance(key, str) else self.dsem[key]
        self.engs[e].wait_ge(sem, val)
        self.seen[e][key] = val

    def _deps(self, e, reads, writes, waw):
        for b in reads:
            for k, v in b.w.items():
                self._wait(e, k, v)
            if b.psum:
                for k, v in b.r.items():
                    if k != e:
                        self._wait(e, k, v)
        for b in writes:
            for k, v in b.w.items():
                if k == e or waw:
                    continue
                self._wait(e, k, v)
            for k, v in b.r.items():
                if k == e:
                    continue
                self._wait(e, k, v)

    def op(self, e, reads, writes, fn, waw=False):
        self._deps(e, reads, writes, waw)
        ins = fn(self.engs[e])
        self.ecnt[e] += 1
        c = self.ecnt[e]
        ins.then_inc(self.esem[e], 1)
        for b in reads:
            b.r[e] = c
        for b in writes:
            if waw:
                b.w[e] = c
            else:
                b.w = {e: c}
                b.r = {}
        return ins

    def dma(self, e, reads, writes, out, in_, waw=False, **kw):
        if e == "pool":
            self.pool_q = getattr(self, "pool_q", [])
            if len(self.pool_q) >= 6:
                k_, v_ = self.pool_q.pop(0)
                self._wait(e, k_, v_)
        i = self.dnext
        self.dnext = (i + 1) % len(self.dsem)
        if self.dcnt[i] > 0:
            self._wait(e, i, self.dcnt[i])
        self._deps(e, reads, writes, waw)
        self.dcnt[i] += 16
        v = self.dcnt[i]
        self.engs[e].dma_start(out=out, in_=in_, **kw).then_inc(self.dsem[i], 16)
        if e == "pool":
            self.pool_q.append((i, v))
        for b in reads:
            b.r[i] = v
        for b in writes:
            if waw:
                b.w[i] = v
            else:
                b.w = {i: v}
                b.r = {}

    def barrier(self):
        for e in self.engs:
            for e2 in self.engs:
                if e2 != e and self.ecnt[e2] > 0:
                    self._wait(e, e2, self.ecnt[e2])
            for i, v in enumerate(self.dcnt):
                if v > 0:
                    self._wait(e, i, v)

    def finish(self, bufs):
        for b in bufs:
            for k, v in b.w.items():
                self._wait("sp", k, v)


def build_program(S, layers, debug_h=False):
    nc = bass.Bass("TRN2", target_bir_lowering=False)
    NT = S // 128
    NB = S // 512

    def din(name, shape, dt=F32):
        return nc.dram_tensor(name, list(shape), dt, kind="ExternalInput").ap()

    x_in = din("x", [S, D])
    out_d = nc.dram_tensor("out", [S, D], F32, kind="ExternalOutput").ap()
    hD = nc.dram_tensor("h_scr", [S, D], F32, kind="Internal").ap()
    if debug_h:
        dbg_d = nc.dram_tensor("dbg", [128, 256], F32, kind="ExternalOutput").ap()
    cst = din("consts_f32", [128, 5 * 128])
    cstb = din("consts_bf16", [128, 2 * 128], BF16)
    final_norm = din("final_norm", [128, D])

    n_even = sum(1 for l in layers if l[0] == "e")
    n_odd = sum(1 for l in layers if l[0] == "o")
    ew = {}
    if n_even:
        ew = dict(
            norm=din("e_norm", [2, 128, D]), w_in=din("e_w_in", [2, D, E_IN]),
            conv_w=din("e_ssd_conv_w", [2, 128, 12, 4]), conv_b=din("e_ssd_conv_b", [2, 128, 12]),
            dt_bias=din("e_dt_bias", [2, 128, 16]), a_log=din("e_a_log", [2, 128, 16]),
            d_skip=din("e_d_skip", [2, 128, 16]), ssd_norm=din("e_ssd_norm", [2, 128, D]),
            cconv_w=din("e_conf_conv_w", [2, 128, 8, 31]), cconv_b=din("e_conf_conv_b", [2, 128, 8]),
            ln_g=din("e_conf_ln_g", [2, 128, 8]), ln_b=din("e_conf_ln_b", [2, 128, 8]),
            w_out=din("e_w_out", [2, 2048, D]),
        )
        ew["w_in_bf"] = nc.dram_tensor("e_w_in_bf", [2, D, E_IN], BF16, kind="Internal").ap()
        ew["w_out_bf"] = nc.dram_tensor("e_w_out_bf", [2, 2048, D], BF16, kind="Internal").ap()

    ow = {}
    if n_odd:
        ow = dict(
            norm=din("o_norm", [2, 128, D]), w_in=din("o_w_in", [2, D, O_IN]), gate_bias=din("o_gate_bias", [2, 128, 48]),
            peT_k=din("o_peT_k", [2, 64, 32]), w1_k=din("o_cmp_w1_k", [2, 2048, 256]), w2_k=din("o_cmp_w2_k", [2, 256, 64]),
            peT_v=din("o_peT_v", [2, 64, 32]), w1_v=din("o_cmp_w1_v", [2, 2048, 256]), w2_v=din("o_cmp_w2_v", [2, 256, 64]),
            w_out=din("o_w_out", [2, D, D]), amask=din("o_amask", [128, 128]), emat=din("o_emat", [64, S], BF16), cmsk=din("o_cmsk", [10, 128 + 288], BF16),
        )
        ow["w_in_bf"] = nc.dram_tensor("o_w_in_bf", [2, D, O_IN], BF16, kind="Internal").ap()
        ow["w_out_bf"] = nc.dram_tensor("o_w_out_bf", [2, D, D], BF16, kind="Internal").ap()
        ow["qT_D"] = nc.dram_tensor("o_qT_D", [1024, S], BF16, kind="Internal").ap()
        ow["kT_D"] = nc.dram_tensor("o_kT_D", [4, 256, S], BF16, kind="Internal").ap()
        ow["sz_D"] = nc.dram_tensor("o_sz_D", [S, D], BF16, kind="Internal").ap()

    with ExitStack() as es:
        kb = KB(nc, es)
        cf, cf_b = kb.sb("cf", [128, 5 * 128], F32)
        cb, cb_b = kb.sb("cb", [128, 2 * 128], BF16)
        kb.dma("sp", [], [cf_b], cf[:], cst[:, :])
        kb.dma("sp", [], [cb_b], cb[:], cstb[:, :])
        ident_f = cf[:, 0:128]
        L_f = cf[:, 128:256]
        U_f = cf[:, 256:384]
        tri_f = cf[:, 384:512]
        ones_f = cf[:, 512:640]
        ident_b = cb[:, 0:128]

        PS = []
        for i in range(8):
            t = es.enter_context(nc.psum_tensor("ps%d" % i, [128, 512], F32))
            PS.append((t, Buf("ps%d" % i, psum=True)))

        h_b = [kb.buf("hD%d" % i) for i in range(NT)]
        outs = []

        state = {"src": x_in}

        WB = [kb.sb("wb%d" % i, [128, 8, 512], BF16) for i in range(3)]
        wb_i = [0]

        wdram_b = {}

        def load_w(w_bf_ap, r0, c0, ncols, wkey=None):
            t, b = WB[wb_i[0] % len(WB)]
            wb_i[0] += 1
            src = w_bf_ap[r0:r0 + 1024, c0:c0 + ncols].rearrange("(k p) c -> p k c", p=128)
            kb.dma("sp", [wdram_b[wkey]] if wkey else [], [b], t[:, :, 0:ncols], src)
            return t, b

        def cast_weight(src_ap, dst_ap, rows, cols, nm):
            wdram_b[nm] = kb.buf(nm)
            stg = [kb.sb("%s_s%d" % (nm, i), [128, 2048], F32) for i in range(2)]
            stb = [kb.sb("%s_b%d" % (nm, i), [128, 2048], BF16) for i in range(2)]
            i = 0
            for r in range(0, rows, 128):
                for c in range(0, cols, 2048):
                    w = min(2048, cols - c)
                    s, sb_ = stg[i % 2]
                    d, db_ = stb[i % 2]
                    kb.dma("sp", [], [sb_], s[:, 0:w], src_ap[r:r + 128, c:c + w])
                    kb.op("pool", [sb_], [db_], lambda e: e.tensor_copy(out=d[:, 0:w], in_=s[:, 0:w]))
                    kb.dma("pool", [db_], [wdram_b[nm]], dst_ap[r:r + 128, c:c + w], d[:, 0:w], waw=True)
                    i += 1

        def even_layer(li):
            src = state["src"]
            norm_t, norm_b = kb.sb("e_norm_t", [128, D], F32)
            ssdn_t, ssdn_b = kb.sb("e_ssdn_t", [128, D], F32)
            small, small_b = kb.sb("e_small", [128, 64], F32)
            cw, cw_b = kb.sb("e_cw", [128, 12, 4], F32)
            cbias, cbias_b = kb.sb("e_cb", [128, 12], F32)
            ccw, ccw_b = kb.sb("e_ccw", [128, 8, 31], F32)
            cc4, cc4_b = kb.sb("e_cc4", [128, 3, 8], F32)
            kb.dma("sp", [], [norm_b], norm_t[:], ew["norm"][li])
            kb.dma("sp", [], [ssdn_b], ssdn_t[:], ew["ssd_norm"][li])
            kb.dma("sp", [], [small_b], small[:, 0:16], ew["dt_bias"][li])
            kb.dma("sp", [], [small_b], small[:, 16:32], ew["a_log"][li], waw=True)
            kb.dma("sp", [], [small_b], small[:, 32:48], ew["d_skip"][li], waw=True)
            kb.dma("sp", [], [cw_b], cw[:], ew["conv_w"][li])
            kb.dma("sp", [], [cbias_b], cbias[:], ew["conv_b"][li])
            kb.dma("sp", [], [ccw_b], ccw[:], ew["cconv_w"][li])
            kb.dma("sp", [], [cc4_b], cc4[:, 0, :], ew["cconv_b"][li])
            kb.dma("sp", [], [cc4_b], cc4[:, 1, :], ew["ln_g"][li], waw=True)
            kb.dma("sp", [], [cc4_b], cc4[:, 2, :], ew["ln_b"][li], waw=True)
            kb.op("act", [small_b], [small_b], lambda e: e.activation(out=small[:, 16:32], in_=small[:, 16:32], func=AF.Exp))
            kb.op("dve", [small_b], [small_b], lambda e: e.tensor_scalar(out=small[:, 16:32], in0=small[:, 16:32], scalar1=-1.0, scalar2=None, op0=ALU.mult))
            dtb = small[:, 0:16]
            a_t = small[:, 16:32]
            dsk = small[:, 32:48]
            w_in_bf = ew["w_in_bf"][li]
            w_out_bf = ew["w_out_bf"][li]

            wdt, wdt_b = kb.sb("e_wdt", [128, 8, 16], BF16)
            kb.dma("sp", [wdram_b["cwi%d" % li]], [wdt_b], wdt[:], w_in_bf[0:1024, 2560:2576].rearrange("(k p) c -> p k c", p=128))
            hblk, hblk_b = kb.sb("e_hblk", [128, 4, D], F32)
            hn, hn_b = kb.sb("e_hn", [128, D], BF16)
            st4, st4_b = kb.sb("e_st4", [128, 8], F32)
            hnT, hnT_b = kb.sb("e_hnT", [128, 8, 512], BF16)
            sz, sz_b = kb.sb("e_sz", [128, 4, D], BF16)
            xh, xh_b = kb.sb("e_xh", [128, 12, 516], BF16)
            xh_bs = [kb.buf("xh%d" % j) for j in range(12)]
            cacc, cacc_b = kb.sb("e_cacc", [128, 512], F32)
            caccp, caccp_b = kb.sb("e_caccp", [128, 512], F32)
            uats = [kb.sb("e_uat%d" % i, [128, 512], F32) for i in range(2)]
            xa, xa_b = kb.sb("e_xa", [128, 12, 512], BF16)
            xa_bs = [kb.buf("xa%d" % j) for j in range(12)]
            dt_t, dt_b = kb.sb("e_dt", [128, 4, 16], F32)
            dta_t, dta_b = kb.sb("e_dta", [128, 4, 16], F32)
            sp1, sp1_b = kb.sb("e_sp1", [128, 64], F32)
            sp2, sp2_b = kb.sb("e_sp2", [128, 64], F32)
            xt, xt_b = kb.sb("e_xt", [128, D], BF16)
            Bt, Bt_b = kb.sb("e_Bt", [128, 2, 128], BF16)
            cums, cums_b = kb.sb("e_cums", [128, 64], F32)
            CBm, CBm_b = kb.sb("e_CBm", [128, 2, 128], F32)
            lh = [kb.sb("e_lh%d" % i, [128, 128], F32) for i in range(8)]
            dec = [kb.sb("e_dec%d" % i, [128, 512], F32) for i in range(2)]
            wT = [kb.sb("e_wT%d" % i, [128, 128], BF16) for i in range(4)]
            y1, y1_b = kb.sb("e_y1", [128, D], F32)
            y2, y2_b = kb.sb("e_y2", [128, D], F32)
            junk, junk_b = y2, y2_b
            xw, xw_b = kb.sb("e_xw", [128, D], BF16)
            st32, st32_b = kb.sb("e_st32", [128, 2, 512], F32)
            stbf, stbf_b = kb.sb("e_stbf", [128, 2, 512], BF16)
            ya, ya_b = kb.sb("e_ya", [128, D], BF16)
            yaT, yaT_b = kb.sb("e_yaT", [128, 8, 512], BF16)
            ybT, ybT_b = kb.sb("e_ybT", [128, 8, 512], BF16)
            uh, uh_b = kb.sb("e_uh", [128, 8, 542], BF16)
            dgs = [kb.sb("e_dg%d" % i, [128, 31, 128], BF16) for i in range(2)]
            uh_bs = [kb.buf("uh%d" % j) for j in range(8)]
            uc, uc_b = kb.sb("e_uc", [128, 8, 512], F32)
            uc_bs = [kb.buf("uc%d" % j) for j in range(8)]
            sg, sg_b = kb.sb("e_sg", [128, 512], F32)
            sq, sq_b = kb.sb("e_sq", [128, 512], F32)
            lnm, lnm_b = kb.sb("e_lnm", [128, 512], F32)
            lnr, lnr_b = kb.sb("e_lnr", [128, 512], F32)
            un, un_b = kb.sb("e_un", [128, 512], F32)
            s1, s1_b = kb.sb("e_s1", [128, 512], F32)
            szc, szc_b = kb.sb("e_szc", [128, 512], F32)

            kb.op("pool", [], [st32_b], lambda e: e.memset(st32[:], 0.0))
            kb.op("pool", [], [stbf_b], lambda e: e.memset(stbf[:], 0.0))
            for j in range(12):
                kb.op("pool", [], [xh_bs[j]], lambda e, j=j: e.memset(xh[:, j, 0:3], 0.0))
            for j in range(8):
                kb.op("pool", [], [uh_bs[j]], lambda e, j=j: e.memset(uh[:, j, 0:30], 0.0))

            def rstd_from_ss(ss_ap, n, out_ap, bufs):
                kb.op("dve", bufs, bufs, lambda e: e.tensor_scalar(out=out_ap, in0=ss_ap, scalar1=1.0 / n, scalar2=EPS, op0=ALU.mult, op1=ALU.add))
                kb.op("act", bufs, bufs, lambda e: e.activation(out=out_ap, in_=out_ap, func=AF.Sqrt))
                kb.op("dve", bufs, bufs, lambda e: e.reciprocal(out=out_ap, in_=out_ap))

            def proj_fm(wt, wb_, col0, pt, pb_):
                for k in range(8):
                    kb.op("pe", [wb_, hnT_b], [pb_], lambda e, k=k: e.matmul(pt[:], lhsT=wt[:, k, col0:col0 + 128], rhs=hnT[:, k, :], start=(k == 0), stop=(k == 7)))

            for blk in range(NB):
                t0 = blk * 512
                def emit_diag(j, kks=range(31)):
                    dgt, dgb = dgs[j % 2]
                    for kk in kks:
                        kb.op("pool", [cf_b, ccw_b], [dgb], lambda e, kk=kk: e.tensor_scalar(out=dgt[:, kk, :], in0=ident_f, scalar1=ccw[:, j, kk:kk + 1], scalar2=0.0, op0=ALU.mult, op1=ALU.add), waw=True)

                emit_diag(0)
                for tt in range(4):
                    ti = blk * 4 + tt
                    kb.dma("sp", [h_b[ti]], [hblk_b], hblk[:, tt, :], src[t0 + tt * 128:t0 + (tt + 1) * 128, :], waw=(tt > 0))
                for tt in range(4):
                    kb.op("act", [hblk_b], [junk_b, st4_b], lambda e, tt=tt: e.activation(out=junk[:], in_=hblk[:, tt, :], func=AF.Square, accum_out=st4[:, tt:tt + 1]))
                rstd_from_ss(st4[:, 0:4], float(D), st4[:, 4:8], [st4_b])
                for tt in range(4):
                    kb.op("dve", [hblk_b, st4_b, norm_b], [hn_b], lambda e, tt=tt: e.scalar_tensor_tensor(out=hn[:], in0=hblk[:, tt, :], scalar=st4[:, 4 + tt:5 + tt], in1=norm_t[:], op0=ALU.mult, op1=ALU.mult))
                    pt, pb_ = PS[2]
                    ptb = pt[:].bitcast(BF16)
                    for k in range(8):
                        kb.op("pe", [hn_b, cb_b], [pb_], lambda e, k=k: e.transpose(out=ptb[:, k * 128:(k + 1) * 128], in_=hn[:, k * 128:(k + 1) * 128], identity=ident_b))
                    kb.op("act", [pb_], [hnT_b], lambda e, tt=tt: e.activation(out=hnT[:, :, tt * 128:(tt + 1) * 128], in_=ptb.rearrange("p (k t) -> p k t", k=8), func=AF.Copy), waw=(tt > 0))

                pi = 0
                for half in range(2):
                    wt, wb_ = load_w(w_in_bf, 0, half * 512, 512, wkey="cwi%d" % li)
                    for tt in range(4):
                        pt, pb_ = PS[(0, 1, 4, 5)[pi % 4]]
                        pi += 1
                        for k in range(8):
                            kb.op("pe", [wb_, hnT_b], [pb_], lambda e, k=k, tt=tt: e.matmul(pt[:], lhsT=hnT[:, k, tt * 128:(tt + 1) * 128], rhs=wt[:, k, :], start=(k == 0), stop=(k == 7)))
                        kb.op("act", [pb_], [sz_b], lambda e, tt=tt, half=half: e.activation(out=sz[:, tt, half * 512:(half + 1) * 512], in_=pt[:], func=AF.Silu), waw=True)

                pend = []
                dacc = [(cacc, cacc_b), (lnm, lnm_b), (lnr, lnr_b)]
                dai = 0
                for sl in range(3):
                    wt, wb_ = load_w(w_in_bf, 0, 1024 + sl * 512, 512, wkey="cwi%d" % li)
                    for jj in range(4):
                        j = sl * 4 + jj
                        pt, pb_ = PS[(0, 1, 4, 5)[pi % 4]]
                        pi += 1
                        proj_fm(wt, wb_, jj * 128, pt, pb_)
                        kb.op("act", [pb_], [xh_bs[j]], lambda e, j=j: e.activation(out=xh[:, j, 3:515], in_=pt[:], func=AF.Copy))
                        if j % 3 == 0:
                            ca, ca_b = caccp, caccp_b
                            t2, t2_b = un, un_b
                            kb.op("pool", [xh_bs[j], cw_b], [ca_b], lambda e, j=j: e.tensor_scalar(out=ca[:], in0=xh[:, j, 0:512], scalar1=cw[:, j, 0:1], scalar2=0.0, op0=ALU.mult, op1=ALU.add))
                            for kk in range(1, 4):
                                kb.op("pool", [xh_bs[j], cw_b], [t2_b], lambda e, j=j, kk=kk: e.tensor_scalar(out=t2[:], in0=xh[:, j, kk:kk + 512], scalar1=cw[:, j, kk:kk + 1], scalar2=0.0, op0=ALU.mult, op1=ALU.add))
                                kb.op("pool", [t2_b, ca_b], [ca_b], lambda e: e.tensor_tensor(out=ca[:], in0=ca[:], in1=t2[:], op=ALU.add))
                            delay = 2
                        else:
                            ca, ca_b = dacc[dai % 3]
                            dai += 1
                            kb.op("dve", [xh_bs[j], cw_b], [ca_b], lambda e, j=j, ca=ca: e.tensor_scalar(out=ca[:], in0=xh[:, j, 0:512], scalar1=cw[:, j, 0:1], scalar2=None, op0=ALU.mult))
                            for kk in range(1, 4):
                                kb.op("dve", [xh_bs[j], cw_b, ca_b], [ca_b], lambda e, j=j, kk=kk, ca=ca: e.scalar_tensor_tensor(out=ca[:], in0=xh[:, j, kk:kk + 512], scalar=cw[:, j, kk:kk + 1], in1=ca[:], op0=ALU.mult, op1=ALU.add))
                            delay = 1

                        def fin(j=j, ca=ca, ca_b=ca_b):
                            kb.op("act", [ca_b, cbias_b], [xa_bs[j]], lambda e: e.activation(out=xa[:, j, :], in_=ca[:], func=AF.Silu, bias=cbias[:, j:j + 1]))
                            kb.op("pool", [xh_bs[j]], [xh_bs[j]], lambda e: e.tensor_copy(out=xh[:, j, 0:3], in_=xh[:, j, 512:515]))
                        pend.append((j + delay, fin))
                        for it in list(pend):
                            if it[0] <= j:
                                it[1]()
                                pend.remove(it)
                for it in pend:
                    it[1]()

                wt, wb_ = wdt, wdt_b
                pt, pb_ = PS[pi % 2]
                pi += 1
                for tt in range(4):
                    for k in range(8):
                        kb.op("pe", [wb_, hnT_b], [pb_], lambda e, k=k, tt=tt: e.matmul(pt[:, tt * 16:(tt + 1) * 16], lhsT=hnT[:, k, tt * 128:(tt + 1) * 128], rhs=wt[:, k, 0:16], start=(k == 0), stop=(k == 7)), waw=(tt > 0 or k > 0))
                kb.op("act", [pb_], [sp1_b], lambda e: e.activation(out=sp1[:], in_=pt[:, 0:64], func=AF.Copy))
                kb.op("dve", [sp1_b, small_b], [sp1_b], lambda e: e.tensor_tensor(out=sp1[:].rearrange("p (t h) -> p t h", t=4), in0=sp1[:].rearrange("p (t h) -> p t h", t=4), in1=dtb.unsqueeze(1).broadcast_to([128, 4, 16]), op=ALU.add))
                kb.op("act", [sp1_b], [sp2_b], lambda e: e.activation(out=sp2[:], in_=sp1[:], func=AF.Abs))
                kb.op("act", [sp2_b], [sp2_b], lambda e: e.activation(out=sp2[:], in_=sp2[:], func=AF.Exp, scale=-1.0))
                kb.op("act", [sp2_b], [sp2_b], lambda e: e.activation(out=sp2[:], in_=sp2[:], func=AF.Ln, bias=1.0))
                kb.op("dve", [sp1_b, sp2_b], [dt_b], lambda e: e.scalar_tensor_tensor(out=dt_t[:].rearrange("p t h -> p (t h)"), in0=sp1[:], scalar=0.0, in1=sp2[:], op0=ALU.max, op1=ALU.add))
                kb.op("dve", [dt_b, small_b], [dta_b], lambda e: e.tensor_tensor(out=dta_t[:], in0=dt_t[:], in1=a_t.unsqueeze(1).broadcast_to([128, 4, 16]), op=ALU.mult))

                for sl in range(2):
                    wa, wab = load_w(w_in_bf, 0, 2576 + sl * 512, 512, wkey="cwi%d" % li)
                    wu, wub = load_w(w_in_bf, 0, 3600 + sl * 512, 512, wkey="cwi%d" % li)
                    for jj in range(4):
                        j = sl * 4 + jj
                        pa, pab = PS[(0, 4)[j % 2]]
                        pu, pub = PS[(1, 5)[j % 2]]
                        proj_fm(wa, wab, jj * 128, pa, pab)
                        proj_fm(wu, wub, jj * 128, pu, pub)
                        sgt, sgtb = (sg, sg_b) if j % 2 == 0 else (szc, szc_b)
                        kb.op("act", [pub], [sgtb], lambda e: e.activation(out=sgt[:], in_=pu[:], func=AF.Sigmoid))
                        uat, uatb = uats[j % 2]
                        kb.op("act", [pab], [uatb], lambda e: e.activation(out=uat[:], in_=pa[:], func=AF.Copy))
                        kb.op("dve", [uatb, sgtb], [uh_bs[j]], lambda e, j=j: e.tensor_tensor(out=uh[:, j, 30:542], in0=uat[:], in1=sgt[:], op=ALU.mult))

                def emit_conv_mm(j, kk):
                    dgt, dgb = dgs[j % 2]
                    cp, cpb = PS[j % 2]
                    if j + 1 < 8:
                        emit_diag(j + 1, [kk])
                    kb.op("pe", [dgb, uh_bs[j]], [cpb], lambda e: e.matmul(cp[:], lhsT=dgt[:, kk, :], rhs=uh[:, j, kk:kk + 512], start=(kk == 0), stop=(kk == 30)))
                    if kk == 30:
                        kb.op("act", [cpb, cc4_b], [uc_bs[j]], lambda e: e.activation(out=uc[:, j, :], in_=cp[:], func=AF.Identity, bias=cc4[:, 0, j:j + 1]))
                        kb.op("dve", [uh_bs[j]], [uh_bs[j]], lambda e: e.tensor_copy(out=uh[:, j, 0:30], in_=uh[:, j, 512:542]))

                fillers = [(j, kk) for j in range(8) for kk in range(31)]

                def fill(n):
                    for _ in range(min(n, len(fillers))):
                        j_, kk_ = fillers.pop(0)
                        emit_conv_mm(j_, kk_)

                for tt in range(4):
                    cs = slice(tt * 128, (tt + 1) * 128)
                    pt, pb_ = PS[2]
                    ptb = pt[:].bitcast(BF16)
                    for j in range(8):
                        kb.op("pe", [xa_bs[j], cb_b], [pb_], lambda e, j=j: e.transpose(out=ptb[:, j * 128:(j + 1) * 128], in_=xa[:, j, cs], identity=ident_b))
                    kb.op("act", [pb_], [xt_b], lambda e: e.activation(out=xt[:], in_=ptb, func=AF.Copy))
                    pt3, pb3 = PS[3]
                    pt3b = pt3[:].bitcast(BF16)
                    for g in range(2):
                        kb.op("pe", [xa_bs[8 + g], cb_b], [pb3], lambda e, g=g: e.transpose(out=pt3b[:, g * 128:(g + 1) * 128], in_=xa[:, 8 + g, cs], identity=ident_b))
                    kb.op("act", [pb3], [Bt_b], lambda e: e.activation(out=Bt[:], in_=pt3b[:, 0:256].rearrange("p (g n) -> p g n", g=2), func=AF.Copy))
                    fill(8)
                    pt, pb_ = PS[2]
                    kb.op("pe", [dta_b, cf_b], [pb_], lambda e: e.matmul(pt[:, 0:16], lhsT=L_f, rhs=dta_t[:, tt, :], start=True, stop=True))
                    kb.op("pe", [dta_b, cf_b], [pb_], lambda e: e.matmul(pt[:, 16:32], lhsT=ones_f, rhs=dta_t[:, tt, :], start=True, stop=True))
                    fill(6)
                    kb.op("act", [pb_], [cums_b], lambda e: e.activation(out=cums[:, 0:32], in_=pt[:, 0:32], func=AF.Exp))
                    kb.op("dve", [pb_], [cums_b], lambda e: e.tensor_copy(out=cums[:, 48:64], in_=pt[:, 0:16]))
                    kb.op("dve", [pb_, cums_b], [cums_b], lambda e: e.tensor_tensor(out=cums[:, 48:64], in0=pt[:, 16:32], in1=cums[:, 48:64], op=ALU.subtract))
                    kb.op("act", [cums_b], [cums_b], lambda e: e.activation(out=cums[:, 32:48], in_=cums[:, 48:64], func=AF.Exp))
                    kb.op("dve", [cums_b, dt_b], [cums_b], lambda e: e.tensor_tensor(out=cums[:, 32:48], in0=cums[:, 32:48], in1=dt_t[:, tt, :], op=ALU.mult))
                    ecum = cums[:, 0:16]
                    etot = cums[:, 16:32]
                    toend = cums[:, 32:48]
                    for g in range(2):
                        pt3, pb3 = PS[3]
                        kb.op("pe", [xa_bs[8 + g], xa_bs[10 + g]], [pb3], lambda e, g=g: e.matmul(pt3[:, 0:128], lhsT=xa[:, 8 + g, cs], rhs=xa[:, 10 + g, cs], start=True, stop=True))
                        kb.op("dve", [pb3, cf_b], [CBm_b], lambda e, g=g: e.tensor_tensor(out=CBm[:, g, :], in0=pt3[:, 0:128], in1=tri_f, op=ALU.mult), waw=(g > 0))
                    for g in range(2):
                        pt, pb_ = PS[6 + g]
                        kb.op("pe", [xa_bs[10 + g], stbf_b], [pb_], lambda e, g=g: e.matmul(pt[:], lhsT=xa[:, 10 + g, cs], rhs=stbf[:, g, :], start=True, stop=True))
                        kb.op("dve", [pb_, cums_b], [y1_b], lambda e, g=g: e.tensor_tensor(out=y1[:, g * 512:(g + 1) * 512].rearrange("p (h d) -> p h d", h=8), in0=pt[:].rearrange("p (h d) -> p h d", h=8), in1=ecum[:, g * 8:(g + 1) * 8].unsqueeze(2).broadcast_to([128, 8, 64]), op=ALU.mult), waw=(g > 0))
                    def stage_a(hq):
                        sgp, sgb = PS[4 + hq % 2]
                        dct, dcb = dec[hq % 2]
                        for hh in range(4):
                            h = hq * 4 + hh
                            lt, lb = lh[h % 8]
                            kb.op("pool", [cf_b, dta_b], [lb], lambda e, h=h: e.tensor_scalar(out=lt[:], in0=U_f, scalar1=dta_t[:, tt, h:h + 1], scalar2=0.0, op0=ALU.mult, op1=ALU.add))
                            kb.op("pe", [lb, cf_b], [sgb], lambda e, hh=hh: e.matmul(sgp[:, hh * 128:(hh + 1) * 128], lhsT=lt[:], rhs=L_f, start=True, stop=True), waw=(hh > 0))
                        kb.op("act", [sgb], [dcb], lambda e: e.activation(out=dct[:], in_=sgp[:], func=AF.Exp))

                    def stage_b(hq):
                        dct, dcb = dec[hq % 2]
                        for hh in range(4):
                            h = hq * 4 + hh
                            g = h // 8
                            wt_, wtb = wT[h % 4]
                            kb.op("dve", [dcb, dt_b, CBm_b], [wtb], lambda e, h=h, hh=hh, g=g: e.scalar_tensor_tensor(out=wt_[:], in0=dct[:, hh * 128:(hh + 1) * 128], scalar=dt_t[:, tt, h:h + 1], in1=CBm[:, g, :], op0=ALU.mult, op1=ALU.mult))
                            pt, pb_ = PS[6 + g]
                            hl = h % 8
                            kb.op("pe", [wtb, xt_b], [pb_], lambda e, h=h, hl=hl: e.matmul(pt[:, hl * 64:(hl + 1) * 64], lhsT=wt_[:], rhs=xt[:, h * 64:(h + 1) * 64], start=True, stop=True), waw=(hl > 0))

                    stage_a(0)
                    fill(4)
                    stage_a(1)
                    fill(4)
                    stage_b(0)
                    stage_a(2)
                    fill(6)
                    stage_b(1)
                    stage_a(3)
                    fill(6)
                    stage_b(2)
                    fill(6)
                    stage_b(3)
                    fill(6)
                    for g in range(2):
                        pt, pb_ = PS[6 + g]
                        kb.op("dve", [pb_, y1_b], [y1_b], lambda e, g=g: e.tensor_tensor(out=y1[:, g * 512:(g + 1) * 512], in0=pt[:], in1=y1[:, g * 512:(g + 1) * 512], op=ALU.add))
                    kb.op("dve", [xt_b, small_b], [y2_b], lambda e: e.tensor_tensor(out=y2[:].rearrange("p (h d) -> p h d", h=16), in0=xt[:].rearrange("p (h d) -> p h d", h=16), in1=dsk.unsqueeze(2).broadcast_to([128, 16, 64]), op=ALU.mult))
                    kb.op("dve", [y1_b, y2_b], [y1_b], lambda e: e.tensor_tensor(out=y1[:], in0=y1[:], in1=y2[:], op=ALU.add))
                    kb.op("dve", [xt_b, cums_b], [xw_b], lambda e: e.tensor_tensor(out=xw[:].rearrange("p (h d) -> p h d", h=16), in0=xt[:].rearrange("p (h d) -> p h d", h=16), in1=toend.unsqueeze(2).broadcast_to([128, 16, 64]), op=ALU.mult))
                    for g in range(2):
                        pt3, pb3 = PS[3]
                        kb.op("pe", [Bt_b, xw_b], [pb3], lambda e, g=g: e.matmul(pt3[:], lhsT=Bt[:, g, :], rhs=xw[:, g * 512:(g + 1) * 512], start=True, stop=True))
                        kb.op("dve", [st32_b, cums_b], [st32_b], lambda e, g=g: e.tensor_tensor(out=st32[:, g, :].rearrange("p (h d) -> p h d", h=8), in0=st32[:, g, :].rearrange("p (h d) -> p h d", h=8), in1=etot[:, g * 8:(g + 1) * 8].unsqueeze(2).broadcast_to([128, 8, 64]), op=ALU.mult))
                        kb.op("dve", [pb3, st32_b], [st32_b], lambda e, g=g: e.tensor_tensor(out=st32[:, g, :], in0=pt3[:], in1=st32[:, g, :], op=ALU.add))
                    kb.op("act", [st32_b], [stbf_b], lambda e: e.activation(out=stbf[:], in_=st32[:], func=AF.Copy))
                    fill(8)
                    kb.op("dve", [y1_b, sz_b], [y1_b], lambda e: e.tensor_tensor(out=y1[:], in0=y1[:], in1=sz[:, tt, :], op=ALU.mult))
                    for g in range(2):
                        kb.op("act", [y1_b], [junk_b, st4_b], lambda e, g=g: e.activation(out=junk[:, 0:512], in_=y1[:, g * 512:(g + 1) * 512], func=AF.Square, accum_out=st4[:, g:g + 1]))
                    rstd_from_ss(st4[:, 0:2], 512.0, st4[:, 2:4], [st4_b])
                    for g in range(2):
                        kb.op("dve", [y1_b, st4_b, ssdn_b], [ya_b], lambda e, g=g: e.scalar_tensor_tensor(out=ya[:, g * 512:(g + 1) * 512], in0=y1[:, g * 512:(g + 1) * 512], scalar=st4[:, 2 + g:3 + g], in1=ssdn_t[:, g * 512:(g + 1) * 512], op0=ALU.mult, op1=ALU.mult), waw=(g > 0))
                    pt, pb_ = PS[2]
                    ptb = pt[:].bitcast(BF16)
                    for k in range(8):
                        kb.op("pe", [ya_b, cb_b], [pb_], lambda e, k=k: e.transpose(out=ptb[:, k * 128:(k + 1) * 128], in_=ya[:, k * 128:(k + 1) * 128], identity=ident_b))
                    kb.op("act", [pb_], [yaT_b], lambda e, tt=tt: e.activation(out=yaT[:, :, tt * 128:(tt + 1) * 128], in_=ptb.rearrange("p (k t) -> p k t", k=8), func=AF.Copy), waw=(tt > 0))
                    fill(8)

                fill(len(fillers))
                mean_p, mean_pb = PS[4]
                msq_p, msq_pb = PS[5]
                for j in range(8):
                    sqt, sqtb = (sq, sq_b) if j % 2 == 0 else (s1, s1_b)
                    kb.op("act", [uc_bs[j]], [sqtb], lambda e, j=j: e.activation(out=sqt[:], in_=uc[:, j, :], func=AF.Square))
                    kb.op("pe", [uc_bs[j], cf_b], [mean_pb], lambda e, j=j: e.matmul(mean_p[:], lhsT=ones_f, rhs=uc[:, j, :], start=(j == 0), stop=(j == 7)))
                    kb.op("pe", [sqtb, cf_b], [msq_pb], lambda e, j=j: e.matmul(msq_p[:], lhsT=ones_f, rhs=sqt[:], start=(j == 0), stop=(j == 7)))
                kb.op("act", [mean_pb], [lnm_b], lambda e: e.activation(out=lnm[:], in_=mean_p[:], func=AF.Copy, scale=1.0 / 1024))
                kb.op("act", [lnm_b], [sq_b], lambda e: e.activation(out=sq[:], in_=lnm[:], func=AF.Square))
                kb.op("dve", [msq_pb, sq_b], [lnr_b], lambda e: e.scalar_tensor_tensor(out=lnr[:], in0=msq_p[:], scalar=1.0 / 1024, in1=sq[:], op0=ALU.mult, op1=ALU.subtract))
                kb.op("dve", [lnr_b], [lnr_b], lambda e: e.tensor_scalar(out=lnr[:], in0=lnr[:], scalar1=EPS, scalar2=None, op0=ALU.add))
                kb.op("act", [lnr_b], [lnr_b], lambda e: e.activation(out=lnr[:], in_=lnr[:], func=AF.Sqrt))
                kb.op("dve", [lnr_b], [lnr_b], lambda e: e.reciprocal(out=lnr[:], in_=lnr[:]))
                kb.op("dve", [lnm_b, lnr_b], [lnm_b], lambda e: e.scalar_tensor_tensor(out=lnm[:], in0=lnm[:], scalar=-1.0, in1=lnr[:], op0=ALU.mult, op1=ALU.mult))
                for sl in range(2):
                    wz, wzb = load_w(w_in_bf, 0, 4624 + sl * 512, 512, wkey="cwi%d" % li)
                    for jj in range(4):
                        j = sl * 4 + jj
                        pz, pzb = PS[j % 2]
                        proj_fm(wz, wzb, jj * 128, pz, pzb)
                        kb.op("act", [pzb], [szc_b], lambda e: e.activation(out=szc[:], in_=pz[:], func=AF.Silu))
                        kb.op("dve", [uc_bs[j], lnr_b], [un_b], lambda e, j=j: e.tensor_tensor(out=un[:], in0=uc[:, j, :], in1=lnr[:], op=ALU.mult))
                        kb.op("dve", [un_b, lnm_b], [un_b], lambda e: e.tensor_tensor(out=un[:], in0=un[:], in1=lnm[:], op=ALU.add))
                        kb.op("act", [un_b, cc4_b], [s1_b], lambda e, j=j: e.activation(out=s1[:], in_=un[:], func=AF.Silu, scale=cc4[:, 1, j:j + 1], bias=cc4[:, 2, j:j + 1]))
                        kb.op("dve", [s1_b, szc_b], [ybT_b], lambda e, j=j: e.tensor_tensor(out=ybT[:, j, :], in0=s1[:], in1=szc[:], op=ALU.mult), waw=(j > 0))

                for half in range(2):
                    for kh in range(2):
                        wo, wob = load_w(w_out_bf, kh * 1024, half * 512, 512, wkey="cwo%d" % li)
                        srcT, srcb = (yaT, yaT_b) if kh == 0 else (ybT, ybT_b)
                        for tt in range(4):
                            pt, pb_ = PS[4 + tt]
                            for k in range(8):
                                kb.op("pe", [wob, srcb], [pb_], lambda e, k=k, tt=tt, kh=kh: e.matmul(pt[:], lhsT=srcT[:, k, tt * 128:(tt + 1) * 128], rhs=wo[:, k, :], start=(kh == 0 and k == 0), stop=(kh == 1 and k == 7)))
                    for tt in range(4):
                        pt, pb_ = PS[4 + tt]
                        kb.op("dve", [pb_, hblk_b], [hblk_b], lambda e, tt=tt, half=half: e.tensor_tensor(out=hblk[:, tt, half * 512:(half + 1) * 512], in0=pt[:], in1=hblk[:, tt, half * 512:(half + 1) * 512], op=ALU.add))
                for tt in range(4):
                    ti = blk * 4 + tt
                    kb.dma("pool", [hblk_b], [h_b[ti]], hD[t0 + tt * 128:t0 + (tt + 1) * 128, :], hblk[:, tt, :])
            state["src"] = hD


        def odd_layer(li):
            src = state["src"]
            NSEL = S // 64
            NCMP = S // 16 - 1
            NCT = (NCMP + 127) // 128
            NCP = NCT * 128
            w_in_bf = ow["w_in_bf"][li]
            wk = "owi%d" % li
            ksA, ksA_b = kb.sb("o_ksA", [128, 4, S], BF16)
            kwT, kwT_b = kb.sb("o_kwT", [64, 4, S], BF16)
            vsA, vsA_b = kb.sb("o_vsA", [128, NT, 4, 65], BF16)
            vwA, vwA_b = kb.sb("o_vwA", [128, NT, 4, 65], BF16)
            kcT, kcT_b = kb.sb("o_kcT", [64, 4, NCP], BF16)
            vcA, vcA_b = kb.sb("o_vcA", [128, NCT, 4, 65], BF16)
            gts, gts_b = kb.sb("o_gts", [128, NT, 48], F32)
            wo, wo_b = kb.sb("o_wo", [128, 8, D], BF16)
            norm_t, norm_b = kb.sb("o_norm_t", [128, D], F32)
            gbias, gbias_b = kb.sb("o_gbias", [128, 48], F32)
            Amask, Amask_b = kb.sb("o_Amask", [128, 128], F32)
            kb.dma("sp", [], [norm_b], norm_t[:], ow["norm"][li])
            kb.dma("sp", [], [gbias_b], gbias[:], ow["gate_bias"][li])
            kb.dma("sp", [], [Amask_b], Amask[:], ow["amask"][:, :])
            kb.dma("sp", [wdram_b["owo%d" % li]], [wo_b], wo[:], ow["w_out_bf"][li].rearrange("(k p) c -> p k c", p=128))
            for g in range(4):
                kb.dma("sp", [], [ksA_b], ksA[64:128, g, :], ow["emat"][:, :], waw=True)
            kb.op("pool", [], [vsA_b], lambda e: e.memset(vsA[:], 1.0))
            kb.op("pool", [], [vwA_b], lambda e: e.memset(vwA[:], 1.0))
            kb.op("pool", [], [vcA_b], lambda e: e.memset(vcA[:], 0.0))
            kb.op("pool", [vcA_b], [vcA_b], lambda e: e.memset(vcA[:, :, :, 64:65], 1.0))
            kb.op("pool", [], [kcT_b], lambda e: e.memset(kcT[:], 0.0))

            def rstd_from_ss(ss_ap, n, out_ap, bufs):
                kb.op("dve", bufs, bufs, lambda e: e.tensor_scalar(out=out_ap, in0=ss_ap, scalar1=1.0 / n, scalar2=EPS, op0=ALU.mult, op1=ALU.add))
                kb.op("act", bufs, bufs, lambda e: e.activation(out=out_ap, in_=out_ap, func=AF.Sqrt))
                kb.op("dve", bufs, bufs, lambda e: e.reciprocal(out=out_ap, in_=out_ap))

            qT_D = ow["qT_D"]
            kT_D = ow["kT_D"]
            sz_D = ow["sz_D"]
            qD_b = kb.buf("qD")
            kD_b = kb.buf("kD")
            szD_b = kb.buf("szD")

            with ExitStack() as esA:
                old = kb.es
                kb.es = esA
                wgl, wgl_b = kb.sb("oA_wgl", [128, 8, 48], BF16)
                kb.dma("sp", [wdram_b[wk]], [wgl_b], wgl[:], w_in_bf[0:1024, 2560:2608].rearrange("(k p) c -> p k c", p=128))
                hts = [kb.sb("oA_h%d" % i, [128, D], F32) for i in range(2)]
                hn, hn_b = kb.sb("oA_hn", [128, D], BF16)
                junk, junk_b = kb.sb("oA_junk", [128, D], F32)
                st4, st4_b = kb.sb("oA_st4", [128, 2], F32)
                hnT, hnT_b = kb.sb("oA_hnT", [128, 8, 512], BF16)
                stg = [kb.sb("oA_stg%d" % i, [128, 512], BF16) for i in range(6)]
                gtmp, gtmp_b = kb.sb("oA_gtmp", [128, 192], F32)
                si = [0]
                pi = [0]

                def nps():
                    p = PS[(0, 1, 3, 4, 5, 6)[pi[0] % 6]]
                    pi[0] += 1
                    return p

                def fm_tile(wt, wb_, col0, dst_ap, scale=1.0):
                    pt, pb_ = nps()
                    for k in range(8):
                        kb.op("pe", [wb_, hnT_b], [pb_], lambda e, k=k: e.matmul(pt[:], lhsT=wt[:, k, col0:col0 + 128], rhs=hnT[:, k, :], start=(k == 0), stop=(k == 7)))
                    s_, sb_ = stg[si[0] % 6]
                    si[0] += 1
                    kb.op("act", [pb_], [sb_], lambda e: e.activation(out=s_[:], in_=pt[:], func=AF.Copy, scale=scale))
                    return s_, sb_

                hnTs = [(hnT, hnT_b), kb.sb("oA_hnT2", [128, 8, 512], BF16)]

                def norm_block(blk):
                    t0 = blk * 512
                    hT, hT_b = hnTs[blk % 2]
                    for tt in range(4):
                        ti = blk * 4 + tt
                        ht, hb = hts[ti % 2]
                        kb.dma("sp", [h_b[ti]], [hb], ht[:], src[t0 + tt * 128:t0 + (tt + 1) * 128, :])
                        kb.op("act", [hb], [junk_b, st4_b], lambda e: e.activation(out=junk[:], in_=ht[:], func=AF.Square, accum_out=st4[:, 0:1]))
                        rstd_from_ss(st4[:, 0:1], float(D), st4[:, 1:2], [st4_b])
                        kb.op("dve", [hb, st4_b, norm_b], [hn_b], lambda e: e.scalar_tensor_tensor(out=hn[:], in0=ht[:], scalar=st4[:, 1:2], in1=norm_t[:], op0=ALU.mult, op1=ALU.mult))
                        pt, pb_ = PS[2]
                        ptb = pt[:].bitcast(BF16)
                        for k in range(8):
                            kb.op("pe", [hn_b, cb_b], [pb_], lambda e, k=k: e.transpose(out=ptb[:, k * 128:(k + 1) * 128], in_=hn[:, k * 128:(k + 1) * 128], identity=ident_b))
                        kb.op("act", [pb_], [hT_b], lambda e, tt=tt: e.activation(out=hT[:, :, tt * 128:(tt + 1) * 128], in_=ptb.rearrange("p (k t) -> p k t", k=8), func=AF.Copy), waw=(tt > 0))

                norm_block(0)
                for blk in range(NB):
                    t0 = blk * 512
                    hnT, hnT_b = hnTs[blk % 2]
                    for sl in range(2):
                        wt, wb_ = load_w(w_in_bf, 0, sl * 512, 512, wkey=wk)
                        for jj in range(4):
                            s_, sb_ = fm_tile(wt, wb_, jj * 128, None, scale=0.125)
                            r0 = (sl * 4 + jj) * 128
                            kb.dma("pool", [sb_], [qD_b], qT_D[r0:r0 + 128, t0:t0 + 512], s_[:], waw=True)
                    wt, wb_ = load_w(w_in_bf, 0, 1024, 512, wkey=wk)
                    for jj in range(4):
                        s_, sb_ = fm_tile(wt, wb_, jj * 128, None)
                        kind, half = jj // 2, jj % 2
                        kb.dma("pool", [sb_], [kD_b], kT_D[kind, half * 128:(half + 1) * 128, t0:t0 + 512], s_[:], waw=True)
                    if blk + 1 < NB:
                        norm_block(blk + 1)
                    for kind, c0, vA, vA_b in ((2, 1536, vsA, vsA_b), (3, 2048, vwA, vwA_b)):
                        wt, wb_ = load_w(w_in_bf, 0, c0, 512, wkey=wk)
                        for jj in range(2):
                            s_, sb_ = fm_tile(wt, wb_, jj * 128, None)
                            kb.dma("pool", [sb_], [kD_b], kT_D[kind, jj * 128:(jj + 1) * 128, t0:t0 + 512], s_[:], waw=True)
                        for tt in range(4):
                            ti = blk * 4 + tt
                            pt, pb_ = nps()
                            for k in range(8):
                                kb.op("pe", [wb_, hnT_b], [pb_], lambda e, k=k, tt=tt: e.matmul(pt[:, 0:256], lhsT=hnT[:, k, tt * 128:(tt + 1) * 128], rhs=wt[:, k, 256:512], start=(k == 0), stop=(k == 7)))
                            kb.op("act", [pb_], [vA_b], lambda e, ti=ti, vA=vA: e.activation(out=vA[:, ti, :, 0:64], in_=pt[:, 0:256].rearrange("p (g d) -> p g d", g=4), func=AF.Copy), waw=True)
                    wt, wb_ = wgl, wgl_b
                    pt, pb_ = nps()
                    for tt in range(4):
                        for k in range(8):
                            kb.op("pe", [wb_, hnT_b], [pb_], lambda e, k=k, tt=tt: e.matmul(pt[:, tt * 48:(tt + 1) * 48], lhsT=hnT[:, k, tt * 128:(tt + 1) * 128], rhs=wt[:, k, 0:48], start=(k == 0), stop=(k == 7)), waw=(tt > 0 or k > 0))
                    kb.op("act", [pb_], [gtmp_b], lambda e: e.activation(out=gtmp[:], in_=pt[:, 0:192], func=AF.Copy))
                    kb.op("dve", [gtmp_b, gbias_b], [gtmp_b], lambda e: e.tensor_tensor(out=gtmp[:].rearrange("p (t c) -> p t c", t=4), in0=gtmp[:].rearrange("p (t c) -> p t c", t=4), in1=gbias[:].unsqueeze(1).broadcast_to([128, 4, 48]), op=ALU.add))
                    kb.op("act", [gtmp_b], [gts_b], lambda e, blk=blk: e.activation(out=gts[:, blk * 4:(blk + 1) * 4, :], in_=gtmp[:].rearrange("p (t c) -> p t c", t=4), func=AF.Sigmoid), waw=True)
                    for half in range(2):
                        wt, wb_ = load_w(w_in_bf, 0, 2608 + half * 512, 512, wkey=wk)
                        for tt in range(4):
                            pt, pb_ = nps()
                            for k in range(8):
                                kb.op("pe", [wb_, hnT_b], [pb_], lambda e, k=k, tt=tt: e.matmul(pt[:], lhsT=hnT[:, k, tt * 128:(tt + 1) * 128], rhs=wt[:, k, :], start=(k == 0), stop=(k == 7)))
                            s_, sb_ = stg[si[0] % 6]
                            si[0] += 1
                            kb.op("act", [pb_], [sb_], lambda e: e.activation(out=s_[:], in_=pt[:], func=AF.Silu))
                            kb.dma("pool", [sb_], [szD_b], sz_D[t0 + tt * 128:t0 + (tt + 1) * 128, half * 512:(half + 1) * 512], s_[:], waw=True)
                kb.es = old
            kb.barrier()
            if ODD_STOP == "A":
                return
            kb.dma("sp", [kD_b], [ksA_b], ksA[0:64, :, :], kT_D[2].rearrange("(g d) s -> d g s", g=4), waw=True)
            kb.dma("sp", [kD_b], [kwT_b], kwT[:, :, :], kT_D[3].rearrange("(g d) s -> d g s", g=4))

            with ExitStack() as esB:
                old = kb.es
                kb.es = esB
                w1f, w1f_b = kb.sb("oB_w1f", [64, 16, 256], F32)
                w1b, w1b_b = kb.sb("oB_w1b", [64, 32, 256], BF16)
                w2f, w2f_b = kb.sb("oB_w2f", [128, 2, 64], F32)
                w2b, w2b_b = kb.sb("oB_w2b", [128, 2, 64], BF16)
                pef, pef_b = kb.sb("oB_pef", [64, 32], F32)
                peb, peb_b = kb.sb("oB_peb", [64, 32], BF16)
                b1, b1_b = kb.sb("oB_b1", [128, 2], F32)
                tcs = [kb.sb("oB_tc%d" % i, [64, S], BF16) for i in range(2)]
                h1T, h1T_b = kb.sb("oB_h1T", [128, 2, NCP], BF16)
                for kind in range(2):
                    nm = "kv"[kind]
                    for lh_ in range(2):
                        kb.dma("sp", [], [w1f_b], w1f[:], ow["w1_" + nm][li][lh_ * 1024:(lh_ + 1) * 1024, :].rearrange("(l d) j -> d l j", d=64))
                        kb.op("pool", [w1f_b], [w1b_b], lambda e, lh_=lh_: e.tensor_copy(out=w1b[:, lh_ * 16:(lh_ + 1) * 16, :], in_=w1f[:]), waw=(lh_ > 0))
                    kb.dma("sp", [], [w2f_b], w2f[:], ow["w2_" + nm][li].rearrange("(jt p) d -> p jt d", p=128))
                    kb.op("pool", [w2f_b], [w2b_b], lambda e: e.tensor_copy(out=w2b[:], in_=w2f[:]))
                    kb.dma("sp", [], [pef_b], pef[:], ow["peT_" + nm][li])
                    kb.op("pool", [pef_b], [peb_b], lambda e: e.tensor_copy(out=peb[:], in_=pef[:]))
                    pt, pb_ = PS[2]
                    for jt in range(2):
                        for l in range(32):
                            kb.op("pe", [w1b_b, peb_b], [pb_], lambda e, jt=jt, l=l: e.matmul(pt[:, jt:jt + 1], lhsT=w1b[:, l, jt * 128:(jt + 1) * 128], rhs=peb[:, l:l + 1], start=(jt == 0 and l == 0), stop=(l == 31), skip_group_check=True), waw=True)
                    kb.op("dve", [pb_], [b1_b], lambda e: e.tensor_copy(out=b1[:], in_=pt[:, 0:2]))
                    for g in range(4):
                        tc_, tcb = tcs[g % 2]
                        kb.dma("sp", [kD_b], [tcb], tc_[:], kT_D[kind, g * 64:(g + 1) * 64, :])
                        tcv = tc_[:].rearrange("p (n s) -> p n s", s=16)
                        for jt in range(2):
                            pj, pjb = PS[jt]
                            for l in range(32):
                                kb.op("pe", [w1b_b, tcb], [pjb], lambda e, jt=jt, l=l: e.matmul(pj[:, 0:NCMP], lhsT=w1b[:, l, jt * 128:(jt + 1) * 128], rhs=tcv[:, l // 16:l // 16 + NCMP, l % 16], start=(l == 0), stop=(l == 31)))
                            kb.op("act", [pjb, b1_b], [h1T_b], lambda e, jt=jt: e.activation(out=h1T[:, jt, 0:NCMP], in_=pj[:, 0:NCMP], func=AF.Silu, bias=b1[:, jt:jt + 1]), waw=(jt > 0))
                        if kind == 0:
                            po, pob = PS[3]
                            for jt in range(2):
                                kb.op("pe", [w2b_b, h1T_b], [pob], lambda e, jt=jt: e.matmul(po[0:64, 0:NCMP], lhsT=w2b[:, jt, :], rhs=h1T[:, jt, 0:NCMP], start=(jt == 0), stop=(jt == 1)))
                            kb.op("act", [pob], [kcT_b], lambda e, g=g: e.activation(out=kcT[:, g, 0:NCMP], in_=po[0:64, 0:NCMP], func=AF.Copy), waw=True)
                        else:
                            for nt in range(NCT):
                                rows = min(NCMP, (nt + 1) * 128) - nt * 128
                                po, pob = PS[3]
                                for jt in range(2):
                                    kb.op("pe", [w2b_b, h1T_b], [pob], lambda e, jt=jt, nt=nt, rows=rows: e.matmul(po[0:rows, 0:64], lhsT=h1T[:, jt, nt * 128:nt * 128 + rows], rhs=w2b[:, jt, :], start=(jt == 0), stop=(jt == 1)))
                                kb.op("act", [pob], [vcA_b], lambda e, g=g, nt=nt, rows=rows: e.activation(out=vcA[0:rows, nt, g, 0:64], in_=po[0:rows, 0:64], func=AF.Copy), waw=True)
                kb.es = old
            kb.barrier()
            if ODD_STOP == "B":
                return

            with ExitStack() as esC:
                old = kb.es
                kb.es = esC
                qas = [kb.sb("oC_qa%d" % i, [128, 4, 512], BF16) for i in range(2)]
                qab = [[kb.buf("qab%d_%d" % (i, g)) for g in range(4)] for i in range(2)]
                NPT = 4
                pts = [kb.sb("oC_pt%d" % i, [128, 512], BF16) for i in range(NPT)]
                NET = 6
                ets = [kb.sb("oC_et%d" % i, [128, NCP], F32) for i in range(NET)]
                eti = [0]
                rss = [kb.sb("oC_rs%d" % i, [128, 8], F32) for i in range(2)]
                cmsk, cmsk_b = kb.sb("oC_cmsk", [10, 128 + 288], BF16)
                kb.dma("sp", [], [cmsk_b], cmsk[:], ow["cmsk"][:, :])
                pacc, pacc_b = kb.sb("oC_pacc", [128, NCP], F32)
                imp, imp_b = kb.sb("oC_imp", [128, 64], F32)
                imp2, imp2_b = kb.sb("oC_imp2", [128, 64], F32)
                m8, m8_b = kb.sb("oC_m8", [128, 16], F32)
                bts = [kb.sb("oC_bt%d" % g, [128, 128], BF16) for g in range(4)]
                szt = [kb.sb("oC_sz%d" % i, [128, D], BF16) for i in range(2)]
                hts = [kb.sb("oC_h%d" % i, [128, D], F32) for i in range(3)]
                cfs, cfs_b = kb.sb("oC_cfs", [128, 3, 4], F32)
                acc, acc_b = kb.sb("oC_acc", [128, 256], F32)
                ys = [kb.sb("oC_y%d" % i, [128, D], BF16) for i in range(2)]
                yT, yT_b = kb.sb("oC_yT", [128, 8, 128], BF16)
                for g in range(4):
                    kb.op("pool", [], [bts[g][1]], lambda e, g=g: e.memset(bts[g][0][:], 0.0))
                pti = [0]
                sci = [0]

                def sc_bank():
                    p = PS[sci[0] % 4]
                    sci[0] += 1
                    return p

                def o_bank(g, X):
                    return PS[4 + X]

                ob3s = [kb.sb("oC_ob3_%d" % i, [128, 3, 260], F32) for i in range(2)]
                tmp3, tmp3_b = kb.sb("oC_tmp3", [128, 3, 256], F32)

                def load_tile(qi):
                    slot = qi % 2
                    t0 = qi * 128
                    qa, qa_b = qas[slot]
                    kb.dma("sp", [qD_b], [qa_b], qa[0:64, :, :].rearrange("d g (r t) -> d g r t", r=4), qT_D[:, t0:t0 + 128].rearrange("(g r d) t -> d g r t", g=4, r=4))
                    sz_, sz_b = szt[slot]
                    kb.dma("sp", [szD_b], [sz_b], sz_[:], sz_D[t0:t0 + 128, :])
                    ht, hb = hts[qi % 3]
                    kb.dma("sp", [h_b[qi]], [hb], ht[:], src[t0:t0 + 128, :])

                def need_sel(qi):
                    return (qi * 128 + 127) >= 1024 and NSEL > 16

                def imp_front(qi, gsel=None):
                    slot = qi % 2
                    t0 = qi * 128
                    qa, qa_b = qas[slot]
                    if not need_sel(qi):
                        for g in (range(4) if gsel is None else [gsel]):
                            kb.op("pool", [], [qab[slot][g]], lambda e, g=g: e.memset(qa[64:128, g, :], 0.0))
                        return
                    ncol = min(NCP, 8 * (qi + 1))
                    s0 = 128 + 258 - 8 * qi
                    for g in (range(4) if gsel is None else [gsel]):
                        rs4, rs4_b = rss[eti[0] % 2]
                        hets = []
                        for r in range(4):
                            ip, ipb = sc_bank()
                            kb.op("pe", [qa_b, kcT_b], [ipb], lambda e, r=r, g=g: e.matmul(ip[:, 0:ncol], lhsT=qa[0:64, g, r * 128:(r + 1) * 128], rhs=kcT[:, g, 0:ncol], start=True, stop=False))
                            kb.op("pe", [cmsk_b], [ipb], lambda e: e.matmul(ip[:, 0:ncol], lhsT=cmsk[0:10, 0:128], rhs=cmsk[0:10, s0:s0 + ncol], start=False, stop=True), waw=True)
                            et, et_b = ets[eti[0] % NET]
                            eti[0] += 1
                            kb.op("act", [ipb], [et_b, rs4_b], lambda e, r=r: e.activation(out=et[:, 0:ncol], in_=ip[:, 0:ncol], func=AF.Exp, accum_out=rs4[:, r:r + 1]), waw=True)
                            hets.append((et, et_b))
                        kb.op("dve", [rs4_b], [rs4_b], lambda e: e.tensor_scalar(out=rs4[:, 4:8], in0=rs4[:, 0:4], scalar1=1e-30, scalar2=None, op0=ALU.max))
                        kb.op("dve", [rs4_b], [rs4_b], lambda e: e.reciprocal(out=rs4[:, 4:8], in_=rs4[:, 4:8]))
                        if ncol < NCP:
                            kb.op("dve", [], [pacc_b], lambda e: e.memset(pacc[:, ncol:NCP], 0.0))
                        for r in range(4):
                            et, et_b = hets[r]
                            if r == 0:
                                kb.op("dve", [et_b, rs4_b], [pacc_b], lambda e, et=et: e.tensor_scalar(out=pacc[:, 0:ncol], in0=et[:, 0:ncol], scalar1=rs4[:, 4:5], scalar2=None, op0=ALU.mult), waw=True)
                            else:
                                kb.op("dve", [et_b, rs4_b, pacc_b], [pacc_b], lambda e, et=et, r=r: e.scalar_tensor_tensor(out=pacc[:, 0:ncol], in0=et[:, 0:ncol], scalar=rs4[:, 4 + r:5 + r], in1=pacc[:, 0:ncol], op0=ALU.mult, op1=ALU.add))
                        bt, bt_b = bts[g]
                        pv = pacc[:, 0:4 * NSEL].rearrange("p (j i) -> p j i", i=4)
                        kb.op("dve", [pacc_b], [imp_b], lambda e: e.tensor_reduce(out=imp[:, 0:NSEL], in_=pv, axis=AX.X, op=ALU.add))
                        kb.op("dve", [pacc_b, imp_b], [imp_b], lambda e: e.tensor_tensor(out=imp[:, 1:NSEL], in0=imp[:, 1:NSEL], in1=pv[:, 0:NSEL - 1, 3], op=ALU.add))
                        kb.op("dve", [imp_b, Amask_b], [imp_b], lambda e: e.tensor_tensor(out=imp[:, 0:NSEL], in0=imp[:, 0:NSEL], in1=Amask[:, 64 - 2 * qi:64 - 2 * qi + NSEL], op=ALU.add))
                        kb.op("dve", [imp_b], [imp_b], lambda e: e.memset(imp[:, 0:1], 1.0e6))
                        kb.op("dve", [imp_b], [m8_b], lambda e: e.max(out=m8[:, 0:8], in_=imp[:, 0:NSEL]))
                        kb.op("dve", [imp_b, m8_b], [imp2_b], lambda e: e.match_replace(out=imp2[:, 0:NSEL], in_to_replace=m8[:, 0:8], in_values=imp[:, 0:NSEL], imm_value=-2.0e9))
                        kb.op("dve", [imp2_b], [m8_b], lambda e: e.max(out=m8[:, 8:16], in_=imp2[:, 0:NSEL]))
                        kb.op("dve", [imp_b, m8_b], [bt_b], lambda e: e.tensor_scalar(out=bt[:, 64:64 + NSEL], in0=imp[:, 0:NSEL], scalar1=m8[:, 15:16], scalar2=NEG, op0=ALU.is_lt, op1=ALU.mult))

                def imp_back(qi):
                    if not need_sel(qi):
                        return
                    slot = qi % 2
                    qa, qa_b = qas[slot]
                    tp, tpb = PS[7]
                    tpv = tp[:].bitcast(BF16)
                    for g in range(4):
                        bt, bt_b = bts[g]
                        kb.op("pe", [bt_b, cb_b], [tpb], lambda e, g=g: e.transpose(out=tpv[:, g * 128:(g + 1) * 128], in_=bt[:], identity=ident_b), waw=(g > 0))
                    for g in range(4):
                        for r in range(4):
                            kb.op("dve", [tpb], [qab[slot][g]], lambda e, r=r, g=g: e.tensor_copy(out=qa[64:128, g, r * 128:(r + 1) * 128], in_=tpv[64:128, g * 128:(g + 1) * 128]), waw=(r > 0))

                def emit_qk(u):
                    sp_, spb = sc_bank()
                    kb.op("pe", u["rd"], [spb], lambda e: e.matmul(sp_[:], lhsT=u["lhsT"], rhs=u["rhs"], start=True, stop=True))
                    p_, p_b = pts[pti[0] % NPT]
                    pti[0] += 1
                    kb.op("act", [spb], [p_b], lambda e: e.activation(out=p_[:], in_=sp_[:], func=AF.Exp))
                    if u["mask"] is not None:
                        base, cm, step = u["mask"]
                        kb.op("pool", [p_b], [p_b], lambda e: e.affine_select(out=p_[:], in_=p_[:], pattern=[[0, 4], [step, 128]], compare_op=ALU.is_ge, fill=0.0, base=base, channel_multiplier=cm))
                    u["p"] = (p_, p_b)

                def emit_pv(u):
                    p_, p_b = u["p"]
                    o_ps, o_pb = u["o"]
                    for r in range(4):
                        st = u["first"] and r == 0
                        kb.op("pe", [p_b, u["vb"]], [o_pb], lambda e, r=r, st=st: e.matmul(o_ps[:, r * 65:(r + 1) * 65], lhsT=p_[:, r * 128:(r + 1) * 128], rhs=u["v"], start=st, stop=u["last"], skip_group_check=True), waw=not st)
                    if u["last"]:
                        ob, ob_b = ob3s[u["g"] % 2]
                        X_ = u["X"]
                        evq.append([2, lambda: kb.op("act", [o_pb], [ob_b], lambda e: e.activation(out=ob[:, X_, :], in_=o_ps[:, 0:260], func=AF.Copy), waw=True)])

                evq = []

                def evq_tick(force=False):
                    for it in list(evq):
                        it[0] -= 1
                        if it[0] <= 0 or force:
                            it[1]()
                            evq.remove(it)

                def combine(qi, g):
                    evq_tick(force=True)
                    slot = qi % 2
                    sz_, sz_b = szt[slot]
                    y, y_b = ys[qi % 2]
                    ob, ob_b = ob3s[g % 2]
                    gv = gts[:, qi, :].rearrange("p (h x) -> p h x", x=3)[:, 4 * g:4 * g + 4, :].rearrange("p r x -> p x r")
                    ov = ob[:].rearrange("p x (r c) -> p x r c", c=65)
                    kb.op("dve", [ob_b], [cfs_b], lambda e: e.tensor_scalar(out=cfs[:], in0=ov[:, :, :, 64], scalar1=1e-30, scalar2=None, op0=ALU.max))
                    kb.op("dve", [cfs_b], [cfs_b], lambda e: e.reciprocal(out=cfs[:], in_=cfs[:]))
                    kb.op("dve", [cfs_b, gts_b], [cfs_b], lambda e: e.tensor_tensor(out=cfs[:], in0=cfs[:], in1=gv, op=ALU.mult))
                    kb.op("dve", [ob_b, cfs_b], [tmp3_b], lambda e: e.tensor_tensor(out=tmp3[:].rearrange("p x (r d) -> p x r d", r=4), in0=ov[:, :, :, 0:64], in1=cfs[:].unsqueeze(3).broadcast_to([128, 3, 4, 64]), op=ALU.mult))
                    kb.op("dve", [tmp3_b], [acc_b], lambda e: e.tensor_tensor(out=acc[:], in0=tmp3[:, 0, :], in1=tmp3[:, 1, :], op=ALU.add))
                    kb.op("dve", [tmp3_b, acc_b], [acc_b], lambda e: e.tensor_tensor(out=acc[:], in0=acc[:], in1=tmp3[:, 2, :], op=ALU.add))
                    kb.op("dve", [acc_b, sz_b], [y_b], lambda e, g=g: e.tensor_tensor(out=y[:, g * 256:(g + 1) * 256], in0=acc[:], in1=sz_[:, g * 256:(g + 1) * 256], op=ALU.mult), waw=(g > 0))

                def units_for(qi, g):
                    slot = qi % 2
                    t0 = qi * 128
                    qa, qa_b = qas[slot]
                    us = []
                    cts = [ct for ct in range(NCT) if t0 + 127 - 2048 * ct - 31 >= 0]
                    for n_, ct in enumerate(cts):
                        base = t0 - 2048 * ct - 31
                        mk = None if base - 16 * 127 >= 0 else (base, -16, 1)
                        us.append(dict(rd=[qa_b, kcT_b], lhsT=kcT[:, g, ct * 128:(ct + 1) * 128], rhs=qa[0:64, g, :], mask=mk,
                                       v=vcA[:, ct, g, :], vb=vcA_b, o=o_bank(g, 0), X=0, first=(n_ == 0), last=(n_ == len(cts) - 1)))
                    for kt in range(qi + 1):
                        us.append(dict(rd=[qa_b, qab[slot][g], ksA_b], lhsT=ksA[:, g, kt * 128:(kt + 1) * 128], rhs=qa[:, g, :],
                                       mask=((0, -1, 1) if kt == qi else None), v=vsA[:, kt, g, :], vb=vsA_b, o=o_bank(g, 1), X=1, first=(kt == 0), last=(kt == qi)))
                    k0 = max(0, qi - 4)
                    for kt in range(k0, qi + 1):
                        mk = (0, -1, 1) if kt == qi else ((-1, 1, -1) if kt == qi - 4 else None)
                        us.append(dict(rd=[qa_b, kwT_b], lhsT=kwT[:, g, kt * 128:(kt + 1) * 128], rhs=qa[0:64, g, :], mask=mk,
                                       v=vwA[:, kt, g, :], vb=vwA_b, o=o_bank(g, 2), X=2, first=(kt == k0), last=(kt == qi)))
                    for u in us:
                        u["g"] = g
                    return us

                def out_proj(qi):
                    t0 = qi * 128
                    ht, hb = hts[qi % 3]
                    y, y_b = ys[qi % 2]
                    tp, tpb = PS[7]
                    tpv = tp[:].bitcast(BF16)
                    for k in range(8):
                        kb.op("pe", [y_b, cb_b], [tpb], lambda e, k=k: e.transpose(out=tpv[:, k * 128:(k + 1) * 128], in_=y[:, k * 128:(k + 1) * 128], identity=ident_b))
                    kb.op("act", [tpb], [yT_b], lambda e: e.activation(out=yT[:], in_=tpv.rearrange("p (k t) -> p k t", k=8), func=AF.Copy))

                def out_proj2(qi):
                    t0 = qi * 128
                    ht, hb = hts[qi % 3]
                    for half in range(2):
                        pp, ppb = PS[7]
                        for k in range(8):
                            kb.op("pe", [yT_b, wo_b], [ppb], lambda e, k=k, half=half: e.matmul(pp[:], lhsT=yT[:, k, :], rhs=wo[:, k, half * 512:(half + 1) * 512], start=(k == 0), stop=(k == 7)))
                        kb.op("dve", [ppb, hb], [hb], lambda e, half=half: e.tensor_tensor(out=ht[:, half * 512:(half + 1) * 512], in0=pp[:], in1=ht[:, half * 512:(half + 1) * 512], op=ALU.add))
                    if state.get("fuse_final"):
                        fs, fs_b = fns
                        kb.op("act", [hb], [yT_b, fs_b], lambda e: e.activation(out=yT[:].rearrange("p k t -> p (k t)"), in_=ht[:], func=AF.Square, accum_out=fs[:, 0:1]))
                        kb.op("dve", [fs_b], [fs_b], lambda e: e.tensor_scalar(out=fs[:, 1:2], in0=fs[:, 0:1], scalar1=1.0 / D, scalar2=EPS, op0=ALU.mult, op1=ALU.add))
                        kb.op("act", [fs_b], [fs_b], lambda e: e.activation(out=fs[:, 1:2], in_=fs[:, 1:2], func=AF.Sqrt))
                        kb.op("dve", [fs_b], [fs_b], lambda e: e.reciprocal(out=fs[:, 1:2], in_=fs[:, 1:2]))
                        kb.op("dve", [hb, fs_b, norm_b], [hb], lambda e: e.scalar_tensor_tensor(out=ht[:], in0=ht[:], scalar=fs[:, 1:2], in1=norm_t[:], op0=ALU.mult, op1=ALU.mult))
                        ob_ = kb.buf()
                        outs.append(ob_)
                        kb.dma("pool", [hb], [ob_], out_d[t0:t0 + 128, :], ht[:])
                    else:
                        kb.dma("pool", [hb], [h_b[qi]], hD[t0:t0 + 128, :], ht[:])

                if state.get("fuse_final"):
                    fns = kb.sb("oC_fs", [128, 2], F32)
                    kb.dma("sp", [], [norm_b], norm_t[:], final_norm[:, :])
                LOOK = 3
                load_tile(0)
                imp_front(0)
                imp_back(0)
                deferred = []
                for qi in range(NT):
                    if qi + 1 < NT:
                        load_tile(qi + 1)
                    todo = deferred
                    deferred = []
                    if qi + 1 < NT:
                        for g_ in range(4):
                            todo.append((6 + 8 * g_, lambda qi=qi, g_=g_: imp_front(qi + 1, g_)))
                    units = []
                    for g in range(4):
                        units += units_for(qi, g)
                    n = len(units)
                    if qi + 1 < NT:
                        todo.append((max(62, n - 40), lambda qi=qi: imp_back(qi + 1)))
                    for i in range(n + LOOK):
                        if i < n:
                            emit_qk(units[i])
                        evq_tick()
                        if i >= LOOK:
                            u = units[i - LOOK]
                            emit_pv(u)
                            if i - LOOK + 1 == n or units[i - LOOK + 1]["g"] != u["g"]:
                                combine(qi, u["g"])
                        for (k_, fn) in todo:
                            if k_ == i:
                                fn()
                    for (k_, fn) in todo:
                        if k_ >= n + LOOK:
                            fn()
                    deferred.append((40, lambda qi=qi: out_proj(qi)))
                    deferred.append((46, lambda qi=qi: out_proj2(qi)))
                for (k_, fn) in deferred:
                    fn()
                kb.es = old
            state["src"] = hD

        def final_norm_phase():
            src = state["src"]
            fn_t, fn_b = kb.sb("fn_t", [128, D], F32)
            kb.dma("sp", [], [fn_b], fn_t[:], final_norm[:, :])
            hts = [kb.sb("fn_h%d" % i, [128, D], F32) for i in range(2)]
            fj, fj_b = kb.sb("fn_j", [128, D], F32)
            fs, fs_b = kb.sb("fn_s", [128, 2], F32)
            for ti in range(NT):
                ht, hb = hts[ti % 2]
                kb.dma("sp", [h_b[ti]], [hb], ht[:], src[ti * 128:(ti + 1) * 128, :])
                kb.op("act", [hb], [fj_b, fs_b], lambda e: e.activation(out=fj[:], in_=ht[:], func=AF.Square, accum_out=fs[:, 0:1]))
                kb.op("dve", [fs_b], [fs_b], lambda e: e.tensor_scalar(out=fs[:, 1:2], in0=fs[:, 0:1], scalar1=1.0 / D, scalar2=EPS, op0=ALU.mult, op1=ALU.add))
                kb.op("act", [fs_b], [fs_b], lambda e: e.activation(out=fs[:, 1:2], in_=fs[:, 1:2], func=AF.Sqrt))
                kb.op("dve", [fs_b], [fs_b], lambda e: e.reciprocal(out=fs[:, 1:2], in_=fs[:, 1:2]))
                kb.op("dve", [hb, fs_b, fn_b], [hb], lambda e: e.scalar_tensor_tensor(out=ht[:], in0=ht[:], scalar=fs[:, 1:2], in1=fn_t[:], op0=ALU.mult, op1=ALU.mult))
                ob = kb.buf()
                outs.append(ob)
                kb.dma("pool", [hb], [ob], out_d[ti * 128:(ti + 1) * 128, :], ht[:])

        def cast_weight_dma(src_ap, dst_ap, rows, nm):
            wdram_b[nm] = kb.buf(nm)
            for r in range(0, rows, 128):
                kb.dma("pool", [], [wdram_b[nm]], dst_ap[r:r + 128, :], src_ap[r:r + 128, :], waw=True, max_dma_last_dim=4096)

        def cast_layer(kind, li):
            if kind == "e":
                cast_weight_dma(ew["w_in"][li], ew["w_in_bf"][li], D, "cwi%d" % li)
                cast_weight_dma(ew["w_out"][li], ew["w_out_bf"][li], 2048, "cwo%d" % li)
            else:
                cast_weight_dma(ow["w_in"][li], ow["w_in_bf"][li], D, "owi%d" % li)
                cast_weight_dma(ow["w_out"][li], ow["w_out_bf"][li], D, "owo%d" % li)

        cast_layer(*layers[0])
        for idx, (kind, li) in enumerate(layers):
            if idx + 1 < len(layers):
                cast_layer(*layers[idx + 1])
            with ExitStack() as es2:
                old = kb.es
                kb.es = es2
                state["fuse_final"] = (kind == "o" and idx == len(layers) - 1 and not debug_h)
                if kind == "e":
                    even_layer(li)
                else:
                    odd_layer(li)
                kb.es = old
            kb.barrier()
        if debug_h:
            hts = [kb.sb("dbg_h%d" % i, [128, D], F32) for i in range(2)]
            for ti in range(NT):
                ht, hb = hts[ti % 2]
                kb.dma("sp", [h_b[ti]], [hb], ht[:], state["src"][ti * 128:(ti + 1) * 128, :])
                ob = kb.buf()
                outs.append(ob)
                kb.dma("pool", [hb], [ob], out_d[ti * 128:(ti + 1) * 128, :], ht[:])
        elif not state.get("fuse_final"):
            with ExitStack() as es2:
                old = kb.es
                kb.es = es2
                final_norm_phase()
                kb.es = old
        kb.finish(outs)
    return nc


def make_consts():
    k = np.arange(128)
    ident = np.eye(128, dtype=np.float32)
    L = (k[:, None] <= k[None, :]).astype(np.float32)
    U = (k[:, None] > k[None, :]).astype(np.float32)
    tri = (k[None, :] >= k[:, None]).astype(np.float32)
    ones = np.ones((128, 128), np.float32)
    cf = np.concatenate([ident, L, U, tri, ones], axis=1)
    import ml_dtypes
    cb = np.concatenate([ident, L], axis=1).astype(ml_dtypes.bfloat16)
    return cf, cb


def bc(v):
    v = np.asarray(v, np.float32)
    return np.ascontiguousarray(np.broadcast_to(v[:, None], (v.shape[0], 128) + v.shape[1:]))


def host_inputs(inp, layers, S=4096):
    cf, cb = make_consts()
    m = {"consts_f32": cf, "consts_bf16": cb,
         "final_norm": np.ascontiguousarray(np.broadcast_to(np.asarray(inp["final_norm"], np.float32), (128, D)))}
    if any(l[0] == "e" for l in layers):
        m["e_norm"] = bc(inp["e_norm"])
        m["e_w_in"] = np.ascontiguousarray(inp["e_w_in"], dtype=np.float32)
        m["e_ssd_conv_w"] = np.ascontiguousarray(np.asarray(inp["e_ssd_conv_w"], np.float32).reshape(2, 4, 12, 128).transpose(0, 3, 2, 1))
        m["e_ssd_conv_b"] = np.ascontiguousarray(np.asarray(inp["e_ssd_conv_b"], np.float32).reshape(2, 12, 128).transpose(0, 2, 1))
        m["e_dt_bias"] = bc(inp["e_dt_bias"])
        m["e_a_log"] = bc(inp["e_a_log"])
        m["e_d_skip"] = bc(inp["e_d_skip"])
        m["e_ssd_norm"] = bc(inp["e_ssd_norm"])
        m["e_conf_conv_w"] = np.ascontiguousarray(np.asarray(inp["e_conf_conv_w"], np.float32).reshape(2, 31, 8, 128).transpose(0, 3, 2, 1))
        for nm in ("e_conf_conv_b", "e_conf_ln_g", "e_conf_ln_b"):
            m[nm] = np.ascontiguousarray(np.asarray(inp[nm], np.float32).reshape(2, 8, 128).transpose(0, 2, 1))
        m["e_w_out"] = np.ascontiguousarray(inp["e_w_out"], dtype=np.float32)
    if any(l[0] == "o" for l in layers):
        import ml_dtypes
        m["o_norm"] = bc(inp["o_norm"])
        m["o_w_in"] = np.ascontiguousarray(inp["o_w_in"], dtype=np.float32)
        m["o_gate_bias"] = bc(inp["o_gate_bias"])
        for nm in ("k", "v"):
            m["o_peT_" + nm] = np.ascontiguousarray(np.asarray(inp["o_cmp_pe_" + nm], np.float32).transpose(0, 2, 1))
            m["o_cmp_w1_" + nm] = np.ascontiguousarray(inp["o_cmp_w1_" + nm], dtype=np.float32)
            m["o_cmp_w2_" + nm] = np.ascontiguousarray(inp["o_cmp_w2_" + nm], dtype=np.float32)
        m["o_w_out"] = np.ascontiguousarray(inp["o_w_out"], dtype=np.float32)
        p = np.arange(128)[:, None]
        xx = np.arange(128)[None, :] - 64
        off = (p >= 64).astype(np.int64)
        A = np.zeros((128, 128), np.float32)
        A[(xx == off) | (xx == off - 1)] = 1.0e6
        A[xx > off] = -1.0e9
        m["o_amask"] = A
        E = (np.arange(64)[:, None] == (np.arange(S)[None, :] // 64)).astype(np.float32)
        m["o_emat"] = E.astype(ml_dtypes.bfloat16)
        jj = np.arange(10)[:, None]
        Mq = np.where(jj > (np.arange(128)[None, :] + 1) // 16, NEG, 0.0)
        Bd = (np.arange(288)[None, :] == jj + 256).astype(np.float32)
        m["o_cmsk"] = np.concatenate([Mq, Bd], axis=1).astype(ml_dtypes.bfloat16)
    return m


LAYERS = [("e", 0), ("o", 0), ("e", 1), ("o", 1)]


def kernel(**inputs):
    x = np.asarray(inputs["x"], np.float32)
    B, S, _ = x.shape
    nc = build_program(S, LAYERS)
    shared = host_inputs(inputs, LAYERS, S)
    in_maps = []
    for b in range(B):
        mm = dict(shared)
        mm["x"] = np.ascontiguousarray(x[b])
        in_maps.append(mm)
    res = run_bass_kernel_spmd(nc, in_maps, core_ids=list(range(B)))
    return np.stack([np.asarray(r["out"], np.float32) for r in res.results], axis=0)
```
